# Optimizing a Trainium2 kernel written in Bass

```python
import jax
import jax.numpy as jnp
from jax import lax
import numpy as np

D_MODEL = 2048
BATCH = 16
SEQ = 2048
DEPTH = 4

GRID_W = 64
CTX_LEN = 256
N_MIXERS = 2
N_RWKV = (DEPTH + N_MIXERS - 1) // N_MIXERS
N_MLSTM = DEPTH // N_MIXERS
N_DIRS = 2
NORM_EPS = 1e-6

RWKV_HEAD = 64
RWKV_HEADS = D_MODEL // RWKV_HEAD
DECAY_LORA = 96
ICLR_LORA = 96
VRES_LORA = 64
GATE_LORA = 256
RWKV_GN_EPS = RWKV_HEAD * 1e-5

MLSTM_HEADS = 8
MLSTM_DV = D_MODEL // MLSTM_HEADS
MLSTM_DK = MLSTM_DV // 2
MLSTM_QK = MLSTM_HEADS * MLSTM_DK
MLSTM_V = MLSTM_HEADS * MLSTM_DV
MLSTM_PROJ = 2 * MLSTM_QK + 2 * MLSTM_V + N_DIRS * 2 * MLSTM_HEADS
CHUNK = 64
GATE_CAP = 15.0

D_FF = ((8 * D_MODEL + 767) // 768) * 256

kernel_name = 'hybrid_rwkv7_mlstm_flow_trunk'


def _rmsnorm(x, g):
    x32 = x.astype(jnp.float32)
    y = x32 * lax.rsqrt(jnp.mean(x32 * x32, axis=-1, keepdims=True) + NORM_EPS)
    return (y * g.astype(jnp.float32)).astype(x.dtype)


def _head_layernorm(y, n_heads, eps):
    shp = y.shape
    y32 = y.astype(jnp.float32).reshape(shp[:-1] + (n_heads, shp[-1] // n_heads))
    mu = jnp.mean(y32, axis=-1, keepdims=True)
    var = jnp.mean(jnp.square(y32 - mu), axis=-1, keepdims=True)
    return ((y32 - mu) * lax.rsqrt(var + eps)).reshape(shp)


def _grid_shift(h):
    b, t, d = h.shape
    rows = t // GRID_W
    q = d // 4
    g = h.reshape(b, rows, GRID_W, d)
    left = jnp.pad(g[:, :, :-1, :q], ((0, 0), (0, 0), (1, 0), (0, 0)))
    right = jnp.pad(g[:, :, 1:, q:2 * q], ((0, 0), (0, 0), (0, 1), (0, 0)))
    up = jnp.pad(g[:, :-1, :, 2 * q:3 * q], ((0, 0), (1, 0), (0, 0), (0, 0)))
    down = jnp.pad(g[:, 1:, :, 3 * q:], ((0, 0), (0, 1), (0, 0), (0, 0)))
    return jnp.concatenate([left, right, up, down], axis=-1).reshape(b, t, d)


def _seq_shift(h):
    half = h.shape[-1] // 2
    prev = jnp.pad(h[:, :-1, :half], ((0, 0), (1, 0), (0, 0)))
    nxt = jnp.pad(h[:, 1:, half:], ((0, 0), (0, 1), (0, 0)))
    return jnp.concatenate([prev, nxt], axis=-1)


def _dwconv_grid(u, w, bias):
    b, t, ch = u.shape
    rows = t // GRID_W
    y = lax.conv_general_dilated(u.reshape(b, rows, GRID_W, ch), w[:, :, None, :].astype(u.dtype),
                                 (1, 1), 'SAME', dimension_numbers=('NHWC', 'HWIO', 'NHWC'),
                                 feature_group_count=ch)
    return y.reshape(b, t, ch) + bias


def _dwconv_seq(u, w, bias):
    ch = u.shape[-1]
    y = lax.conv_general_dilated(u, w[:, None, :].astype(u.dtype), (1,), 'SAME',
                                 dimension_numbers=('NWC', 'WIO', 'NWC'), feature_group_count=ch)
    return y + bias


def _to_dirs(c_fwd, x_fwd, c_bwd, x_bwd):
    fwd = jnp.concatenate([c_fwd, x_fwd], axis=1)
    bwd = jnp.concatenate([jnp.flip(c_bwd, 1), jnp.flip(x_bwd, 1)], axis=1)
    return jnp.stack([fwd, bwd], axis=0)


def _from_dirs(y, l):
    yf, yb = y[0], y[1]
    return (yf[:, :l] + jnp.flip(yb[:, :l], 1), yf[:, l:] + jnp.flip(yb[:, l:], 1))


def _swiglu(h, w_in, w_out):
    u = h @ w_in
    return (jax.nn.silu(u[..., :D_FF]) * u[..., D_FF:]) @ w_out


def _rwkv7_scan(r, decay, k, v, z, b):
    nz, nb, tt, nh, n = r.shape
    tfirst = lambda a: jnp.moveaxis(a, 2, 0).astype(jnp.float32)

    def step(s, inp):
        r_t, w_t, k_t, v_t, z_t, b_t = inp
        sz = jnp.einsum('zbhvk,zbhk->zbhv', s, z_t)
        s = s * w_t[..., None, :] + sz[..., :, None] * b_t[..., None, :] + v_t[..., :, None] * k_t[..., None, :]
        return s, jnp.einsum('zbhvk,zbhk->zbhv', s, r_t)

    s0 = jnp.zeros((nz, nb, nh, n, n), jnp.float32)
    _, y = lax.scan(step, s0, (tfirst(r), tfirst(decay), tfirst(k), tfirst(v), tfirst(z), tfirst(b)))
    return jnp.moveaxis(y, 0, 2)


def _rwkv7_project(h, xx, p, v_first):
    xr, xw, xk, xv, xa, xg = [h + xx * p['mu'][n] for n in range(6)]
    r = xr @ p['w_r']
    k = xk @ p['w_k']
    v = xv @ p['w_v']
    if v_first is not None:
        v = v + (v_first - v) * jax.nn.sigmoid(p['v0'] + (xv @ p['v1']) @ p['v2'])
    w_pre = p['w0'][:, None, None, :] + jnp.einsum(
        'zbtr,zrd->zbtd', jnp.tanh(jnp.einsum('btd,zdr->zbtr', xw, p['w1'])), p['w2'])
    decay = jnp.exp(-jnp.exp(-jax.nn.softplus(-w_pre.astype(jnp.float32)) - 0.5))
    a = jax.nn.sigmoid(p['a0'][:, None, None, :] + jnp.einsum(
        'zbtr,zrd->zbtd', jnp.einsum('btd,zdr->zbtr', xa, p['a1']), p['a2']))
    g = jax.nn.sigmoid(xg @ p['g1']) @ p['g2']
    kk = (k * p['k_k']).astype(jnp.float32).reshape(k.shape[:-1] + (RWKV_HEADS, RWKV_HEAD))
    kk = (kk / jnp.maximum(jnp.linalg.norm(kk, axis=-1, keepdims=True), 1e-12)).reshape(k.shape).astype(k.dtype)
    k_mod = k * (1 + (a - 1) * p['k_a'])
    return r, decay, k_mod, v, -kk, kk * a, g


def _rwkv7_mixer(hx, hc, p, v_first, need_ctx):
    nb, t, d = hx.shape
    l = hc.shape[1]
    vf_c, vf_x = (None, None) if v_first is None else v_first
    rx, dx, kx, vx, zx, bx, gx = _rwkv7_project(hx, _grid_shift(hx) - hx, p, vf_x)
    rc, dc, kc, vc, zc, bc, gc = _rwkv7_project(hc, _seq_shift(hc) - hc, p, vf_c)
    heads = lambda a: a.reshape(a.shape[:-1] + (RWKV_HEADS, RWKV_HEAD))
    shared = lambda u_c, u_x: heads(_to_dirs(u_c, u_x, u_c, u_x))
    split = lambda u_c, u_x: heads(_to_dirs(u_c[0], u_x[0], u_c[1], u_x[1]))
    y = _rwkv7_scan(shared(rc, rx), split(dc, dx), split(kc, kx), shared(vc, vx), shared(zc, zx), split(bc, bx))
    y_c, y_x = _from_dirs(y.reshape(N_DIRS, nb, l + t, d), l)

    def readout(y_s, r, k_mod, v, g):
        yn = _head_layernorm(y_s, RWKV_HEADS, RWKV_GN_EPS) * p['ln_w'] + p['ln_b']
        rk = heads(r[None] * k_mod * p['r_k'].reshape(-1)).sum(-1).sum(0)
        bonus = (rk[..., None] * heads(v)).reshape(v.shape)
        return ((yn + bonus) * g).astype(hx.dtype) @ p['w_o']

    out_x = readout(y_x, rx, kx, vx, gx)
    out_c = readout(y_c, rc, kc, vc, gc) if need_ctx else None
    return out_x, out_c, (vc, vx)


def _mlstm_chunkwise(q, k, v, ig, lf):
    nz, nb, tt, nh, dk = q.shape
    dv = v.shape[-1]
    nc = tt // CHUNK

    def chunks(a):
        a = a.astype(jnp.float32).reshape((nz, nb, nc, CHUNK, nh) + a.shape[4:])
        return jnp.swapaxes(jnp.moveaxis(a, 2, 0), 3, 4)

    causal = jnp.tril(jnp.ones((CHUNK, CHUNK), dtype=bool))

    def step(carry, inp):
        c_st, n_st, m_st = carry
        qc, kc, vc, ic, fc = inp
        bcum = jnp.cumsum(fc, axis=-1)
        log_d = jnp.where(causal, bcum[..., :, None] - bcum[..., None, :] + ic[..., None, :], -jnp.inf)
        log_inter = bcum + m_st[..., None]
        m_t = jnp.maximum(log_inter, jnp.max(log_d, axis=-1))
        s = jnp.einsum('zbhtd,zbhjd->zbhtj', qc, kc) * jnp.exp(log_d - m_t[..., None])
        w_inter = jnp.exp(log_inter - m_t)
        num = w_inter[..., None] * jnp.einsum('zbhtd,zbhde->zbhte', qc, c_st) + jnp.einsum('zbhtj,zbhje->zbhte', s, vc)
        den = w_inter * jnp.einsum('zbhtd,zbhd->zbht', qc, n_st) + jnp.sum(s, axis=-1)
        h_out = num / jnp.maximum(jnp.abs(den), jnp.exp(-m_t))[..., None]
        b_end = bcum[..., -1]
        a_j = b_end[..., None] - bcum + ic
        m_new = jnp.maximum(b_end + m_st, jnp.max(a_j, axis=-1))
        wk = jnp.exp(a_j - m_new[..., None])[..., None] * kc
        dec = jnp.exp(b_end + m_st - m_new)
        c_st = dec[..., None, None] * c_st + jnp.einsum('zbhjd,zbhje->zbhde', wk, vc)
        n_st = dec[..., None] * n_st + jnp.sum(wk, axis=-2)
        return (c_st, n_st, m_new), h_out

    init = (jnp.zeros((nz, nb, nh, dk, dv), jnp.float32), jnp.zeros((nz, nb, nh, dk), jnp.float32),
            jnp.zeros((nz, nb, nh), jnp.float32))
    _, hs = lax.scan(step, init, (chunks(q), chunks(k), chunks(v), chunks(ig), chunks(lf)))
    hs = jnp.moveaxis(jnp.swapaxes(hs, 3, 4), 0, 2)
    return hs.reshape(nz, nb, tt, nh, dv).astype(v.dtype)


def _mlstm_mixer(hx, hc, p, need_ctx):
    nb, t, d = hx.shape
    l = hc.shape[1]

    def project(h, conv):
        u = h @ p['w_in']
        qk = jax.nn.silu(conv(u[..., :2 * MLSTM_QK]))
        v = u[..., 2 * MLSTM_QK:2 * MLSTM_QK + MLSTM_V]
        o = u[..., 2 * MLSTM_QK + MLSTM_V:2 * MLSTM_QK + 2 * MLSTM_V]
        gates = u[..., 2 * MLSTM_QK + 2 * MLSTM_V:].reshape(u.shape[:-1] + (N_DIRS, 2, MLSTM_HEADS)) + p['b_gate']
        gates = GATE_CAP * jnp.tanh(gates.astype(jnp.float32) / GATE_CAP)
        q = qk[..., :MLSTM_QK] * (MLSTM_DK ** -0.5)
        k = qk[..., MLSTM_QK:]
        return q, k, v, o, gates[..., 0, :], jax.nn.log_sigmoid(gates[..., 1, :])

    qx, kx, vx, ox, ix, fx = project(hx, lambda a: _dwconv_grid(a, p['conv_w'], p['conv_b']))
    qc, kc, vc, oc, ic, fc = project(hc, lambda a: _dwconv_seq(a, p['conv_w'][1], p['conv_b']))
    hd = lambda a, dh: a.reshape(a.shape[:-1] + (MLSTM_HEADS, dh))
    q = hd(_to_dirs(qc, qx, qc, qx), MLSTM_DK)
    k = hd(_to_dirs(kc, kx, kc, kx), MLSTM_DK)
    v = hd(_to_dirs(vc, vx, vc, vx), MLSTM_DV)
    ig = _to_dirs(ic[:, :, 0], ix[:, :, 0], ic[:, :, 1], ix[:, :, 1])
    lf = _to_dirs(fc[:, :, 0], fx[:, :, 0], fc[:, :, 1], fx[:, :, 1])
    h = _mlstm_chunkwise(q, k, v, ig, lf).reshape(N_DIRS, nb, l + t, MLSTM_V)
    h_c, h_x = _from_dirs(h, l)

    def readout(h_s, o):
        hn = _head_layernorm(h_s, MLSTM_HEADS, NORM_EPS) * p['norm_w']
        return (hn * jax.nn.sigmoid(o.astype(jnp.float32))).astype(hx.dtype) @ p['w_out']

    out_x = readout(h_x, ox)
    out_c = readout(h_c, oc) if need_ctx else None
    return out_x, out_c


def setup_inputs(seed: int = 0) -> dict:
    key = jax.random.key(seed)
    ks = iter(jax.random.split(key, 64))
    d = D_MODEL
    na, nbl = N_RWKV, N_MLSTM
    nrm = lambda shape, scale: jax.random.normal(next(ks), shape, jnp.float32) * scale
    uni = lambda shape, lo, hi: jax.random.uniform(next(ks), shape, jnp.float32, lo, hi)
    b_gate = jnp.stack([nrm((nbl, N_DIRS, MLSTM_HEADS), 0.1),
                        uni((nbl, N_DIRS, MLSTM_HEADS), 3.0, 6.0)], axis=2)
    return {
        'x': nrm((BATCH, SEQ, d), 1.0),
        'c': nrm((BATCH, d), 1.0),
        'ctx': nrm((BATCH, CTX_LEN, d), 1.0),
        'c_ctx': nrm((d,), 1.0),
        'mod_w': nrm((DEPTH, d, 6 * d), 0.5 * d ** -0.5),
        'mod_b': nrm((DEPTH, 6 * d), 0.02),
        'norm_g': 1.0 + nrm((DEPTH, 2, d), 0.02),
        'final_g': 1.0 + nrm((d,), 0.02),
        'rwkv_mu': uni((na, 6, d), 0.0, 1.0),
        'rwkv_w_r': nrm((na, d, d), d ** -0.5),
        'rwkv_w_k': nrm((na, d, d), d ** -0.5),
        'rwkv_w_v': nrm((na, d, d), d ** -0.5),
        'rwkv_w_o': nrm((na, d, d), d ** -0.5),
        'rwkv_w0': uni((na, N_DIRS, d), -6.0, -0.5),
        'rwkv_w1': nrm((na, N_DIRS, d, DECAY_LORA), d ** -0.5),
        'rwkv_w2': nrm((na, N_DIRS, DECAY_LORA, d), 0.5 * DECAY_LORA ** -0.5),
        'rwkv_a0': nrm((na, N_DIRS, d), 0.3),
        'rwkv_a1': nrm((na, N_DIRS, d, ICLR_LORA), d ** -0.5),
        'rwkv_a2': nrm((na, N_DIRS, ICLR_LORA, d), 0.5 * ICLR_LORA ** -0.5),
        'rwkv_g1': nrm((na, d, GATE_LORA), d ** -0.5),
        'rwkv_g2': nrm((na, GATE_LORA, d), GATE_LORA ** -0.5),
        'rwkv_k_k': 0.85 + nrm((na, d), 0.02),
        'rwkv_k_a': 1.0 + nrm((na, d), 0.02),
        'rwkv_r_k': nrm((na, RWKV_HEADS, RWKV_HEAD), 0.1),
        'rwkv_ln_w': 1.0 + nrm((na, d), 0.02),
        'rwkv_ln_b': nrm((na, d), 0.02),
        'rwkv_v0': nrm((na - 1, d), 0.3),
        'rwkv_v1': nrm((na - 1, d, VRES_LORA), d ** -0.5),
        'rwkv_v2': nrm((na - 1, VRES_LORA, d), 0.5 * VRES_LORA ** -0.5),
        'mlstm_w_in': nrm((nbl, d, MLSTM_PROJ), d ** -0.5),
        'mlstm_b_gate': b_gate,
        'mlstm_conv_w': nrm((nbl, 3, 3, 2 * MLSTM_QK), 1.0 / 3.0),
        'mlstm_conv_b': nrm((nbl, 2 * MLSTM_QK), 0.02),
        'mlstm_norm_w': 1.0 + nrm((nbl, MLSTM_V), 0.02),
        'mlstm_w_out': nrm((nbl, MLSTM_V, d), MLSTM_V ** -0.5),
        'ffn_w_in': nrm((DEPTH, d, 2 * D_FF), d ** -0.5),
        'ffn_w_out': nrm((DEPTH, D_FF, d), D_FF ** -0.5),
    }


def reference(x, c, ctx, c_ctx, mod_w, mod_b, norm_g, final_g,
              rwkv_mu, rwkv_w_r, rwkv_w_k, rwkv_w_v, rwkv_w_o, rwkv_w0, rwkv_w1, rwkv_w2,
              rwkv_a0, rwkv_a1, rwkv_a2, rwkv_g1, rwkv_g2, rwkv_k_k, rwkv_k_a, rwkv_r_k,
              rwkv_ln_w, rwkv_ln_b, rwkv_v0, rwkv_v1, rwkv_v2,
              mlstm_w_in, mlstm_b_gate, mlstm_conv_w, mlstm_conv_b, mlstm_norm_w, mlstm_w_out,
              ffn_w_in, ffn_w_out):
    v_first = None
    for i in range(DEPTH):
        last = i == DEPTH - 1
        j = i // N_MIXERS
        mod_x = jax.nn.silu(c) @ mod_w[i] + mod_b[i]
        mod_c = jax.nn.silu(c_ctx) @ mod_w[i] + mod_b[i]
        sh1, sc1, gt1, sh2, sc2, gt2 = jnp.split(mod_x[:, None, :], 6, axis=-1)
        csh1, csc1, cgt1, csh2, csc2, cgt2 = jnp.split(mod_c, 6, axis=-1)
        hx = _rmsnorm(x, norm_g[i, 0]) * (1 + sc1) + sh1
        hc = _rmsnorm(ctx, norm_g[i, 0]) * (1 + csc1) + csh1
        if i % N_MIXERS == 0:
            p = {'mu': rwkv_mu[j], 'w_r': rwkv_w_r[j], 'w_k': rwkv_w_k[j], 'w_v': rwkv_w_v[j],
                 'w_o': rwkv_w_o[j], 'w0': rwkv_w0[j], 'w1': rwkv_w1[j], 'w2': rwkv_w2[j],
                 'a0': rwkv_a0[j], 'a1': rwkv_a1[j], 'a2': rwkv_a2[j], 'g1': rwkv_g1[j], 'g2': rwkv_g2[j],
                 'k_k': rwkv_k_k[j], 'k_a': rwkv_k_a[j], 'r_k': rwkv_r_k[j],
                 'ln_w': rwkv_ln_w[j], 'ln_b': rwkv_ln_b[j]}
            if j > 0:
                p['v0'] = rwkv_v0[j - 1]
                p['v1'] = rwkv_v1[j - 1]
                p['v2'] = rwkv_v2[j - 1]
            out_x, out_c, v_pair = _rwkv7_mixer(hx, hc, p, v_first if j > 0 else None, not last)
            if j == 0:
                v_first = v_pair
        else:
            p = {'w_in': mlstm_w_in[j], 'b_gate': mlstm_b_gate[j], 'conv_w': mlstm_conv_w[j],
                 'conv_b': mlstm_conv_b[j], 'norm_w': mlstm_norm_w[j], 'w_out': mlstm_w_out[j]}
            out_x, out_c = _mlstm_mixer(hx, hc, p, not last)
        x = x + gt1 * out_x
        hx = _rmsnorm(x, norm_g[i, 1]) * (1 + sc2) + sh2
        x = x + gt2 * _swiglu(hx, ffn_w_in[i], ffn_w_out[i])
        if not last:
            ctx = ctx + cgt1 * out_c
            hc = _rmsnorm(ctx, norm_g[i, 1]) * (1 + csc2) + csh2
            ctx = ctx + cgt2 * _swiglu(hc, ffn_w_in[i], ffn_w_out[i])
    return _rmsnorm(x, final_g)
```

```python
import numpy as np
from contextlib import ExitStack, contextmanager
import concourse.bass as bass
import concourse.mybir as mybir
from concourse.bass_utils import run_bass_kernel_spmd

F32 = mybir.dt.float32
BF16 = mybir.dt.bfloat16
AF = mybir.ActivationFunctionType
ALU = mybir.AluOpType
AX = mybir.AxisListType

SEM_LIMIT = 30000
NORM_EPS = 1e-6
GRID_W = 64
CHUNK = 64
GATE_CAP = 15.0


class Counter:
    def __init__(self, b, name):
        self.b = b
        self.name = name
        self.n = 0
        self.sem, self.val = b.take_sem(f"{name}_0")

    def next(self, inc):
        if self.val + inc > SEM_LIMIT:
            self.n += 1
            self.sem, self.val = self.b.take_sem(f"{self.name}_{self.n}")
        self.val += inc
        return (self.sem, self.val)


class Rec:
    def __init__(self):
        self.w = {}
        self.r = {}


def _merge(d, ev):
    s, v = ev
    if d.get(s, -1) < v:
        d[s] = v


class Buf:
    def __init__(self, b, handle, name, space):
        self.b = b
        self.h = handle
        self.name = name
        self.space = space
        self.rec = Rec()
        self.subs = {}
        self.dctr = None

    def recs(self):
        return [self.rec] + [s.rec for s in self.subs.values()]

    def sub(self, key):
        if key not in self.subs:
            self.subs[key] = SubBuf(self, key)
        return self.subs[key]

    def __getitem__(self, idx):
        return View(self, self.h[idx])

    @property
    def v(self):
        return View(self, self.h[:])

    def dma_counter(self):
        if self.dctr is None:
            self.dctr = Counter(self.b, "d_" + self.name)
        return self.dctr


class SubBuf:
    def __init__(self, parent, key):
        self.parent = parent
        self.rec = Rec()
        self.name = f"{parent.name}.{key}"
        self.space = parent.space
        self.h = parent.h
        self.b = parent.b

    def recs(self):
        return [self.rec, self.parent.rec]

    def __getitem__(self, idx):
        return View(self, self.h[idx])

    @property
    def v(self):
        return View(self, self.h[:])

    def dma_counter(self):
        return self.parent.dma_counter()


class View:
    def __init__(self, buf, ap):
        self.buf = buf
        self.ap = ap

    def __getitem__(self, idx):
        return View(self.buf, self.ap[idx])

    def re(self, s, **kw):
        return View(self.buf, self.ap.rearrange(s, **kw))

    def bc(self, shape):
        return View(self.buf, self.ap.broadcast_to(list(shape)))

    def us(self, axis):
        return View(self.buf, self.ap.unsqueeze(axis))


def _v(x):
    return x if isinstance(x, View) else x.v


class B:
    ENG = ("pe", "act", "dve", "pool", "sp")

    def __init__(self):
        self.nc = bass.Bass("TRN2", target_bir_lowering=False)
        nc = self.nc
        self.es = ExitStack()
        self.eng = {"pe": nc.tensor, "act": nc.scalar, "dve": nc.vector, "pool": nc.gpsimd, "sp": nc.sync}
        self.ctr = {}
        self.known = {e: {} for e in self.ENG}
        for e in self.ENG:
            self.ctr[e] = Counter(self, "c_" + e)
        self.all_bufs = []
        self.ninst = 0
        self.stack = self.es
        self.sem_pool = []

    def take_sem(self, name):
        pool = self.__dict__.setdefault("sem_pool", [])
        while pool:
            sem, val = pool.pop()
            if val < SEM_LIMIT // 2:
                return sem, val
        self.nsem = self.__dict__.get("nsem", 0) + 1
        return self.es.enter_context(self.nc.semaphore(f"{name}_{self.nsem}")), 0

    def dram(self, name, shape, dtype, kind="Internal"):
        t = self.nc.dram_tensor(name, list(shape), dtype, kind=kind).ap()
        bf = Buf(self, t, name, "dram")
        self.all_bufs.append(bf)
        return bf

    def sbuf(self, name, shape, dtype):
        self.uid = getattr(self, "uid", 0) + 1
        name = f"{name}_{self.uid}"
        t = self.stack.enter_context(self.nc.sbuf_tensor(name, list(shape), dtype))
        bf = Buf(self, t, name, "sbuf")
        self.all_bufs.append(bf)
        return bf

    def psum(self, name, shape, dtype):
        self.uid = getattr(self, "uid", 0) + 1
        name = f"{name}_{self.uid}"
        t = self.stack.enter_context(self.nc.psum_tensor(name, list(shape), dtype))
        bf = Buf(self, t, name, "psum")
        self.all_bufs.append(bf)
        return bf

    @contextmanager
    def scope(self):
        prev = self.stack
        nb = len(self.all_bufs)
        with ExitStack() as st:
            self.stack = st
            yield
            self.barrier()
            for x in self.all_bufs[nb:]:
                if x.space != "dram" and x.dctr is not None:
                    self.sem_pool.append((x.dctr.sem, x.dctr.val))
            self.all_bufs = self.all_bufs[:nb] + [x for x in self.all_bufs[nb:] if x.space == "dram"]
            self.stack = prev

    def _deps(self, reads, writes):
        d = {}
        for x in reads:
            for rec in x.buf.recs():
                for ev in rec.w.items():
                    _merge(d, ev)
        for x in writes:
            for rec in x.buf.recs():
                for ev in rec.w.items():
                    _merge(d, ev)
                for ev in rec.r.items():
                    _merge(d, ev)
        return d

    def _wait(self, e, deps, skip_own=False):
        eng = self.eng[e]
        kn = self.known[e]
        own = self.ctr[e].sem
        for s, v in deps.items():
            if skip_own and s is own:
                continue
            if kn.get(s, -1) >= v:
                continue
            eng.wait_ge(s, v)
            kn[s] = v
            self.ninst += 1

    def _record(self, ev, reads, writes):
        for x in reads:
            _merge(x.buf.rec.r, ev)
        for x in writes:
            x.buf.rec.w = {ev[0]: ev[1]}
            x.buf.rec.r = {}
            if isinstance(x.buf, Buf):
                for s in x.buf.subs.values():
                    s.rec.w = {ev[0]: ev[1]}
                    s.rec.r = {}

    def op(self, e, fn, w=(), r=()):
        w = [_v(x) for x in w]
        r = [_v(x) for x in r]
        deps = self._deps(r, w)
        self._wait(e, deps, skip_own=(e == "pe"))
        inst = fn(self.eng[e])
        ev = self.ctr[e].next(1)
        inst.then_inc(ev[0], 1)
        self.ninst += 1
        self._record(ev, r, w)
        return ev

    def mm(self, out, pairs, start=True, stop=True):
        out = _v(out)
        pairs = [(_v(a), _v(c)) for a, c in pairs]
        reads = [x for p in pairs for x in p]
        deps = self._deps(reads, [out])
        self._wait("pe", deps, skip_own=True)
        n = len(pairs)
        inst = None
        for i, (a, c) in enumerate(pairs):
            inst = self.nc.tensor.matmul(out.ap, a.ap, c.ap, start=(start and i == 0), stop=(stop and i == n - 1))
            self.ninst += 1
        ev = self.ctr["pe"].next(1)
        inst.then_inc(ev[0], 1)
        self._record(ev, reads, [out])
        return ev

    def transpose(self, out, in_, ident):
        out, in_, ident = _v(out), _v(in_), _v(ident)
        return self.op("pe", lambda e: e.transpose(out.ap, in_.ap, ident.ap), w=[out], r=[in_, ident])

    def dma(self, out, in_, q="sp"):
        out, in_ = _v(out), _v(in_)
        deps = self._deps([in_], [out])
        self._wait(q, deps)
        owner = out.buf if out.buf.space != "dram" else in_.buf
        ctr = owner.dma_counter()
        inst = self.eng[q].dma_start(out=out.ap, in_=in_.ap)
        ev = ctr.next(16)
        inst.then_inc(ev[0], 16)
        self.ninst += 1
        self._record(ev, [in_], [out])
        return ev

    def barrier(self):
        d = {}
        for bf in self.all_bufs:
            for rec in bf.recs():
                for ev in rec.w.items():
                    _merge(d, ev)
                for ev in rec.r.items():
                    _merge(d, ev)
        for e in self.ENG:
            c = self.ctr[e]
            if c.val > 0:
                _merge(d, (c.sem, c.val))
        for e in self.ENG:
            self._wait(e, d)

    @staticmethod
    def _sc(x):
        return x.ap if isinstance(x, View) else x

    def act(self, out, in_, func, bias=None, scale=None, accum=None):
        out, in_ = _v(out), _v(in_)
        kw = {}
        rd = [in_]
        wr = [out]
        if bias is not None:
            kw["bias"] = self._sc(bias)
            if isinstance(bias, View):
                rd.append(bias)
        if scale is not None:
            kw["scale"] = self._sc(scale)
            if isinstance(scale, View):
                rd.append(scale)
        if accum is not None:
            kw["accum_out"] = accum.ap
            wr.append(accum)
        return self.op("act", lambda e: e.activation(out=out.ap, in_=in_.ap, func=func, **kw), w=wr, r=rd)

    def tt(self, e, out, a, c, op):
        out, a, c = _v(out), _v(a), _v(c)
        return self.op(e, lambda en: en.tensor_tensor(out=out.ap, in0=a.ap, in1=c.ap, op=op), w=[out], r=[a, c])

    def ts(self, e, out, a, s1, op0, s2=None, op1=None):
        out, a = _v(out), _v(a)
        rd = [a] + [s for s in (s1, s2) if isinstance(s, View)]
        if op1 is None:
            return self.op(e, lambda en: en.tensor_scalar(out=out.ap, in0=a.ap, scalar1=self._sc(s1), scalar2=None,
                                                          op0=op0), w=[out], r=rd)
        return self.op(e, lambda en: en.tensor_scalar(out=out.ap, in0=a.ap, scalar1=self._sc(s1),
                                                      scalar2=self._sc(s2), op0=op0, op1=op1), w=[out], r=rd)

    def stt(self, e, out, a, s, c, op0, op1):
        out, a, c = _v(out), _v(a), _v(c)
        rd = [a, c] + ([s] if isinstance(s, View) else [])
        return self.op(e, lambda en: en.scalar_tensor_tensor(out=out.ap, in0=a.ap, scalar=self._sc(s), in1=c.ap,
                                                             op0=op0, op1=op1), w=[out], r=rd)

    def red(self, out, in_, op=ALU.add, axis=AX.X):
        out, in_ = _v(out), _v(in_)
        return self.op("dve", lambda en: en.tensor_reduce(out=out.ap, in_=in_.ap, axis=axis, op=op), w=[out], r=[in_])

    def copy(self, e, out, in_):
        out, in_ = _v(out), _v(in_)
        if e == "act":
            return self.op("act", lambda en: en.copy(out=out.ap, in_=in_.ap), w=[out], r=[in_])
        return self.op(e, lambda en: en.tensor_copy(out=out.ap, in_=in_.ap), w=[out], r=[in_])

    def memset(self, e, out, val):
        out = _v(out)
        return self.op(e, lambda en: en.memset(out.ap, val), w=[out])


def make_cfg(D=2048, NB=2, L=256, T=2048, DEPTH=4, MH=8):
    c = dict(D=D, NB=NB, L=L, T=T, DEPTH=DEPTH, MH=MH)
    c["KT"] = D // 128
    c["S"] = L + T
    c["R"] = NB * (L + T)
    c["FF"] = ((8 * D + 767) // 768) * 256
    c["FT"] = c["FF"] // 128
    c["RH"] = D // 64
    c["DV"] = D // MH
    c["DK"] = c["DV"] // 2
    c["QK"] = MH * c["DK"]
    c["PROJ"] = 2 * c["QK"] + 2 * D + 4 * MH
    c["NRW"] = (DEPTH + 1) // 2
    c["NML"] = DEPTH // 2
    return c


WEIGHT_SPECS = None


def input_shapes(c):
    D, DEPTH, NRW, NML = c["D"], c["DEPTH"], c["NRW"], c["NML"]
    return {
        "xs0": [c["R"], D], "ccT": [128, c["KT"], c["NB"] + 1], "ident": [128, 128],
        "msel": [64, 64, 64], "mmask": [64, 2, 64],
        "mod_w": [DEPTH, D, 6 * D], "mod_b": [DEPTH, 6 * D], "norm_g": [DEPTH, 2, D], "final_g": [D],
        "rwkv_mu": [NRW, 6, D], "rwkv_w_r": [NRW, D, D], "rwkv_w_k": [NRW, D, D], "rwkv_w_v": [NRW, D, D],
        "rwkv_w_o": [NRW, D, D], "rwkv_w0": [NRW, 2, D], "rwkv_w1": [NRW, 2, D, 96], "rwkv_w2": [NRW, 2, 96, D],
        "rwkv_a0": [NRW, 2, D], "rwkv_a1": [NRW, 2, D, 96], "rwkv_a2": [NRW, 2, 96, D],
        "rwkv_g1": [NRW, D, 256], "rwkv_g2": [NRW, 256, D], "rwkv_k_k": [NRW, D], "rwkv_k_a": [NRW, D],
        "rwkv_r_k": [NRW, c["RH"], 64], "rwkv_ln_w": [NRW, D], "rwkv_ln_b": [NRW, D],
        "rwkv_v0": [max(NRW - 1, 1), D], "rwkv_v1": [max(NRW - 1, 1), D, 64], "rwkv_v2": [max(NRW - 1, 1), 64, D],
        "mlstm_w_in": [NML, D, c["PROJ"]], "mlstm_b_gate": [NML, 2, 2, c["MH"]],
        "mlstm_conv_w": [NML, 3, 3, 2 * c["QK"]], "mlstm_conv_b": [NML, 2 * c["QK"]],
        "mlstm_norm_w": [NML, D], "mlstm_w_out": [NML, D, D],
        "ffn_w_in": [DEPTH, D, 2 * c["FF"]], "ffn_w_out": [DEPTH, c["FF"], D],
    }


class Net:
    def __init__(self, cfg, skip_rwkv=False, skip_mlstm=False):
        self.c = cfg
        self.b = B()
        self.skip_rwkv = skip_rwkv
        self.skip_mlstm = skip_mlstm
        self.cvt_rr = 0

    def declare(self):
        b, c = self.b, self.c
        self.inp = {}
        for name, shp in input_shapes(c).items():
            self.inp[name] = b.dram(name, shp, F32, kind="ExternalInput")
        self.out = b.dram("out", [c["NB"] * c["T"], c["D"]], F32, kind="ExternalOutput")
        self.xs = b.dram("xs", [c["R"], c["D"]], F32)
        self.modrows = b.dram("modrows", [c["DEPTH"], c["NB"] + 1, 6 * c["D"]], F32)

    def consts(self):
        b = self.b
        self.ident_f = b.sbuf("ident_f", [128, 128], F32)
        self.ident_b = b.sbuf("ident_b", [128, 128], BF16)
        b.dma(self.ident_f, self.inp["ident"])
        b.copy("dve", self.ident_b, self.ident_f)
        self.eps_t = b.sbuf("eps_t", [128, 4], F32)
        b.memset("dve", self.eps_t[:, 0:1], NORM_EPS)

    def groups(self, gmax=512, include_ctx=True):
        c = self.c
        gs = []
        for bb in range(c["NB"]):
            base = bb * c["S"]
            if include_ctx:
                t = 0
                while t < c["L"]:
                    n = min(gmax, c["L"] - t)
                    gs.append((base + t, n, c["NB"], True, bb))
                    t += n
            t = 0
            while t < c["T"]:
                n = min(gmax, c["T"] - t)
                gs.append((base + c["L"] + t, n, bb, False, bb))
                t += n
        return gs

    def phase_mod(self):
        b, c = self.b, self.c
        D, KT, NG = c["D"], c["KT"], c["NB"] + 1
        with b.scope():
            ccT = b.sbuf("ccT", [128, KT, NG], F32)
            csT = b.sbuf("csT", [128, KT, NG], F32)
            b.dma(ccT, self.inp["ccT"])
            b.act(csT, ccT, AF.Silu)
            wt = [b.sbuf(f"modw{i}", [128, KT, 512], F32) for i in range(2)]
            bias = b.sbuf("modbias", [NG, 6 * D], F32)
            rows = b.sbuf("modrow_sb", [NG, 6 * D], F32)
            ps = [b.psum(f"modps{i}", [128, 512], F32) for i in range(2)]
            k = 0
            for i in range(c["DEPTH"]):
                b.dma(bias, View(self.inp["mod_b"], self.inp["mod_b"].h[i, :].partition_broadcast(NG)))
                for n0 in range(0, 6 * D, 512):
                    w = wt[k % 2]
                    p = ps[k % 2]
                    k += 1
                    src = self.inp["mod_w"].h[i, :, n0:n0 + 512].rearrange("(kt p) n -> p kt n", p=128)
                    b.dma(w, View(self.inp["mod_w"], src))
                    b.mm(p[0:NG, :], [(csT[:, kt, :], w[:, kt, :]) for kt in range(KT)])
                    b.tt("dve", rows[:, n0:n0 + 512], p[0:NG, :], bias[:, n0:n0 + 512], ALU.add)
                b.dma(self.modrows[i], rows, q="pool")

    def cvt_weight(self, name, src_ap, kdim, n, stage, ncmax=4096):
        b = self.b
        pk = min(kdim, 128)
        ktn = kdim // pk
        dst = b.dram(name, [pk, ktn, n], BF16)
        srcv = src_ap.rearrange("(kt p) n -> p kt n", p=pk)
        if n >= ncmax:
            ktc, ncw = 1, ncmax
        else:
            ktc, ncw = max(1, min(ktn, ncmax // n)), n
        for k0 in range(0, ktn, ktc):
            kk = min(ktc, ktn - k0)
            for n0 in range(0, n, ncw):
                nn = min(ncw, n - n0)
                f, h = stage[self.cvt_rr % len(stage)]
                eng = ("dve", "pool", "act")[self.cvt_rr % 3]
                self.cvt_rr += 1
                fv = View(f, f.h[0:pk, 0:kk * nn].rearrange("p (k n) -> p k n", k=kk))
                hv = View(h, h.h[0:pk, 0:kk * nn].rearrange("p (k n) -> p k n", k=kk))
                b.dma(fv, View(self.srcbuf, srcv[:, k0:k0 + kk, n0:n0 + nn]))
                b.copy(eng, hv, fv)
                b.dma(View(dst, dst.h[:, k0:k0 + kk, n0:n0 + nn]), hv, q="pool" if eng != "pool" else "sp")
        return dst

    def phase_prep(self):
        b, c = self.b, self.c
        D, FF = c["D"], c["FF"]
        self.wb = {}
        with b.scope():
            stage = [(b.sbuf(f"cvf{i}", [128, 4096], F32), b.sbuf(f"cvh{i}", [128, 4096], BF16)) for i in range(3)]

            def cv(key, inname, idx, kdim, n):
                self.srcbuf = self.inp[inname]
                ap = self.inp[inname].h
                for j in idx:
                    ap = ap[j]
                self.wb[key] = self.cvt_weight("wb_" + "_".join(str(x) for x in key), ap, kdim, n, stage)

            for i in range(c["DEPTH"]):
                cv(("ffn_in", i), "ffn_w_in", (i,), D, 2 * FF)
                cv(("ffn_out", i), "ffn_w_out", (i,), FF, D)
            if not self.skip_rwkv:
                for j in range(c["NRW"]):
                    for nm in ("w_r", "w_k", "w_v", "w_o"):
                        cv((nm, j), "rwkv_" + nm, (j,), D, D)
                    for z in range(2):
                        cv(("w1", j, z), "rwkv_w1", (j, z), D, 96)
                        cv(("w2", j, z), "rwkv_w2", (j, z), 96, D)
                        cv(("a1", j, z), "rwkv_a1", (j, z), D, 96)
                        cv(("a2", j, z), "rwkv_a2", (j, z), 96, D)
                    cv(("g1", j), "rwkv_g1", (j,), D, 256)
                    cv(("g2", j), "rwkv_g2", (j,), 256, D)
                    if j > 0:
                        cv(("v1", j), "rwkv_v1", (j - 1,), D, 64)
                        cv(("v2", j), "rwkv_v2", (j - 1,), 64, D)
            if not self.skip_mlstm:
                for j in range(c["NML"]):
                    cv(("m_in", j), "mlstm_w_in", (j,), D, c["PROJ"])
                    cv(("m_out", j), "mlstm_w_out", (j,), D, D)

    def load_bc(self, dst, src_buf, src_ap, q="sp"):
        P = _v(dst).ap.shape[0]
        self.b.dma(dst, View(src_buf, src_ap.partition_broadcast(P)), q=q)

    def mod_tiles(self, i, sub, g, A, Bsh, G, tmp):
        b, c = self.b, self.c
        D = c["D"]
        o = 3 * D * sub
        mr = self.modrows
        self.load_bc(Bsh, mr, mr.h[i, g, o:o + D])
        self.load_bc(tmp, mr, mr.h[i, g, o + D:o + 2 * D])
        self.load_bc(A, self.inp["norm_g"], self.inp["norm_g"].h[i, sub, :])
        b.stt("dve", A, tmp, 1.0, A, ALU.add, ALU.mult)
        if G is not None:
            self.load_bc(G, mr, mr.h[i, g, o + 2 * D:o + 3 * D])

    def norm_tile(self, xt, A, Bsh, hb, junk, hf, st):
        b, D = self.b, self.c["D"]
        b.memset("pool", st[:, 0:1], 0.0)
        b.act(junk, xt, AF.Square, accum=st[:, 0:1])
        b.act(st[:, 1:2], st[:, 0:1], AF.Sqrt, bias=self.eps_t[:, 0:1], scale=1.0 / D)
        b.op("dve", lambda en: en.reciprocal(out=st.h[:, 1:2], in_=st.h[:, 1:2]), w=[st[:, 1:2]], r=[st[:, 1:2]])
        b.stt("dve", hf, xt, st[:, 1:2], A, ALU.mult, ALU.mult)
        if Bsh is None:
            return
        b.tt("dve", hb, hf, Bsh, ALU.add)

    def transpose_tile(self, hb, dstT, tok0, tps, k):
        b, KT = self.b, self.c["KT"]
        for k0 in range(0, KT, 4):
            tp = tps[(k + k0 // 4) % len(tps)]
            for q in range(4):
                b.transpose(tp[:, q, :], hb[:, (k0 + q) * 128:(k0 + q + 1) * 128], self.ident_b)
            eng = "act" if (k0 // 4) % 2 == 0 else "dve"
            b.copy(eng, dstT[:, k0:k0 + 4, tok0:tok0 + 128], tp[:, 0:4, :])

    def ffn(self, i):
        b, c = self.b, self.c
        D, KT, FT, FF = c["D"], c["KT"], c["FT"], c["FF"]
        last = i == c["DEPTH"] - 1
        w_in, w_out = self.wb[("ffn_in", i)], self.wb[("ffn_out", i)]
        JH = FT // 2
        NCW = 512
        with b.scope():
            A = b.sbuf("fA", [128, D], F32)
            Bsh = b.sbuf("fB", [128, D], F32)
            G = b.sbuf("fG", [128, D], F32)
            tmpm = b.sbuf("ftmpm", [128, D], F32)
            xt = [b.sbuf(f"fx{k}", [128, D], F32) for k in range(2)]
            hf = b.sbuf("fhf", [128, D], F32)
            hb = [b.sbuf(f"fhb{k}", [128, D], BF16) for k in range(2)]
            junk = b.sbuf("fjunk", [128, D], BF16)
            st = b.sbuf("fst", [128, 2], F32)
            hT = b.sbuf("fhT", [128, KT, 512], BF16)
            actT = b.sbuf("factT", [128, FT, 512], BF16)
            wg = [b.sbuf(f"fwg{k}", [128, KT, 128], BF16) for k in range(2)]
            wu = [b.sbuf(f"fwu{k}", [128, KT, 128], BF16) for k in range(2)]
            wo = [b.sbuf(f"fwo{k}", [128, JH, NCW], BF16) for k in range(2)]
            sg = [b.sbuf(f"fsg{k}", [128, 512], F32) for k in range(2)]
            ot = [b.sbuf(f"fot{k}", [128, NCW], F32) for k in range(2)]
            tps = [b.psum(f"ftp{k}", [128, 4, 128], BF16) for k in range(2)]
            pg = [b.psum(f"fpg{k}", [128, 512], F32) for k in range(6)]
            cur_g = None
            kx = 0
            for (r0, ntok, mg, is_ctx, bb) in self.groups(512, include_ctx=not last):
                nt = ntok // 128
                if mg != cur_g:
                    self.mod_tiles(i, 1, mg, A, Bsh, G, tmpm)
                    cur_g = mg
                for t in range(nt):
                    x = xt[kx % 2]
                    h = hb[kx % 2]
                    kx += 1
                    b.dma(x, self.xs[r0 + t * 128:r0 + (t + 1) * 128, :])
                    self.norm_tile(x, A, Bsh, h, junk, hf, st)
                    self.transpose_tile(h, hT, t * 128, tps, t)
                for j in range(FT):
                    g_w, u_w = wg[j % 2], wu[j % 2]
                    b.dma(g_w, w_in[:, :, j * 128:(j + 1) * 128])
                    b.dma(u_w, w_in[:, :, FF + j * 128:FF + (j + 1) * 128], q="act")
                    p_g, p_u = pg[(j % 2) * 2], pg[(j % 2) * 2 + 1]
                    b.mm(p_g[:, 0:ntok], [(g_w[:, kt, :], hT[:, kt, 0:ntok]) for kt in range(KT)])
                    b.mm(p_u[:, 0:ntok], [(u_w[:, kt, :], hT[:, kt, 0:ntok]) for kt in range(KT)])
                    s = sg[j % 2]
                    b.act(s[:, 0:ntok], p_g[:, 0:ntok], AF.Silu)
                    b.tt("dve", actT[:, j, 0:ntok], s[:, 0:ntok], p_u[:, 0:ntok], ALU.mult)
                ko = 0
                for n0 in range(0, D, NCW):
                    for half in range(2):
                        w = wo[ko % 2]
                        ko += 1
                        b.dma(w, w_out[:, half * JH:(half + 1) * JH, n0:n0 + NCW])
                        for t in range(nt):
                            b.mm(pg[2 + t], [(actT[:, half * JH + jj, t * 128:(t + 1) * 128], w[:, jj, :])
                                             for jj in range(JH)], start=(half == 0), stop=(half == 1))
                    for t in range(nt):
                        o = ot[t % 2]
                        rows = self.xs[r0 + t * 128:r0 + (t + 1) * 128, n0:n0 + NCW]
                        b.dma(o, rows, q="act")
                        b.tt("dve", sg[t % 2][:, 0:NCW], pg[2 + t], G[:, n0:n0 + NCW], ALU.mult)
                        b.tt("pool", o, o, sg[t % 2][:, 0:NCW], ALU.add)
                        b.dma(rows, o, q="pool")

    def final(self):
        b, c = self.b, self.c
        D = c["D"]
        with b.scope():
            A = b.sbuf("nA", [128, D], F32)
            self.load_bc(A, self.inp["final_g"], self.inp["final_g"].h[:])
            xt = [b.sbuf(f"nx{k}", [128, D], F32) for k in range(2)]
            hf = [b.sbuf(f"nh{k}", [128, D], F32) for k in range(2)]
            junk = b.sbuf("njunk", [128, D], BF16)
            st = b.sbuf("nst", [128, 2], F32)
            k = 0
            for bb in range(c["NB"]):
                for t in range(c["T"] // 128):
                    r0 = bb * c["S"] + c["L"] + t * 128
                    x, h = xt[k % 2], hf[k % 2]
                    k += 1
                    b.dma(x, self.xs[r0:r0 + 128, :])
                    self.norm_tile(x, A, None, None, junk, h, st)
                    b.dma(self.out[bb * c["T"] + t * 128: bb * c["T"] + (t + 1) * 128, :], h, q="pool")

    def build(self):
        b, c = self.b, self.c
        self.declare()
        self.consts()
        for r0 in range(0, c["R"], 512):
            r1 = min(c["R"], r0 + 512)
            b.dma(self.xs[r0:r1, :], self.inp["xs0"][r0:r1, :], q=("sp", "act", "pool")[(r0 // 512) % 3])
        self.phase_mod()
        self.phase_prep()
        for i in range(c["DEPTH"]):
            self.mixer(i)
            self.ffn(i)
        self.final()
        b.barrier()
        return b.nc

    def mixer(self, i):
        if i % 2 == 0:
            if not self.skip_rwkv:
                self.rwkv(i)
        else:
            if not self.skip_mlstm:
                self.mlstm(i)

    def rwkv(self, i):
        c = self.c
        j = i // 2
        if not hasattr(self, "rscr"):
            b = self.b
            R, D = c["R"], c["D"]
            self.rscr = {k: b.dram("rs_" + k, [R, D], F32) for k in
                         ("r", "k", "v", "w0", "w1", "a0", "a1", "g", "y0", "y1", "vf")}
        for bb in range(c["NB"]):
            self.rwkv_proj(i, j, bb)
        self.rwkv_scan(i, j)
        self.rwkv_readout(i, j)

    def rwkv_proj(self, i, j, bb):
        b, c = self.b, self.c
        D, KT, L, T, S = c["D"], c["KT"], c["L"], c["T"], c["S"]
        base = bb * S
        scr = self.rscr
        vdst = scr["vf"] if j == 0 else scr["v"]
        with b.scope():
            hT = b.sbuf("rhT", [128, KT, S], BF16)
            with b.scope():
                A = b.sbuf("rA", [128, D], F32)
                Bsh = b.sbuf("rB", [128, D], F32)
                tmpm = b.sbuf("rtm", [128, D], F32)
                xt = [b.sbuf(f"rx{k}", [128, D], F32) for k in range(2)]
                hf = b.sbuf("rhf", [128, D], F32)
                hb = [b.sbuf(f"rhb{k}", [128, D], BF16) for k in range(2)]
                junk = b.sbuf("rjunk", [128, D], BF16)
                st = b.sbuf("rst", [128, 2], F32)
                tps = [b.psum(f"rtp{k}", [128, 4, 128], BF16) for k in range(2)]
                for (g, t0, n) in ((c["NB"], 0, L), (bb, L, T)):
                    self.mod_tiles(i, 0, g, A, Bsh, None, tmpm)
                    for t in range(n // 128):
                        x, h = xt[t % 2], hb[t % 2]
                        b.dma(x, self.xs[base + t0 + t * 128: base + t0 + (t + 1) * 128, :])
                        self.norm_tile(x, A, Bsh, h, junk, hf, st)
                        self.transpose_tile(h, hT, t0 + t * 128, tps, t)
            with b.scope():
                NM = 6
                mu_rows = b.sbuf("rmur", [NM * KT, 128], F32)
                mu = b.sbuf("rmu", [128, NM * KT], F32)
                omu = b.sbuf("romu", [128, NM * KT], F32)
                b.dma(mu_rows, View(self.inp["rwkv_mu"], self.inp["rwkv_mu"].h[j].rearrange("m (kt p) -> (m kt) p", p=128)))
                pmu = b.psum("rpmu", [128, 128], F32)
                b.transpose(pmu[:, 0:NM * KT], mu_rows, self.ident_f[0:NM * KT, 0:NM * KT])
                b.copy("dve", mu, pmu[:, 0:NM * KT])
                b.ts("dve", omu, mu, -1.0, ALU.mult, 1.0, ALU.add)
                xm = [b.sbuf(f"rxm{k}", [128, KT, 512], BF16) for k in range(2)]
                wch = [b.sbuf(f"rwch{k}", [128, KT, 512], BF16) for k in range(2)]
                brow = b.sbuf("rbrow", [128, D], F32)
                l1w = b.sbuf("rl1w", [128, KT, 256], BF16)
                l2w = b.sbuf("rl2w", [128, 2, D], BF16)
                l1 = b.sbuf("rl1", [128, 2, 512], BF16)
                ev = [b.sbuf(f"rev{k}", [128, 512], F32) for k in range(3)]
                vft = b.sbuf("rvft", [128, 512], F32)
                ps = [b.psum(f"rps{k}", [128, 512], F32) for k in range(4)]
                pl = [b.psum(f"rpl{k}", [128, 512], F32) for k in range(2)]
                state = dict(kx=0, kw=0, kp=0, ke=0)
                blocks = [(0, L, True)] + [(L + t0, min(512, T - t0), False) for t0 in range(0, T, 512)]
                MIX = {"r": 0, "w": 1, "k": 2, "v": 3, "a": 4, "g": 5}
                NEG_EXP_HALF = -float(np.exp(-0.5))

                def build_xm(m, tok0, n, is_ctx):
                    x = xm[state["kx"] % 2]
                    state["kx"] += 1
                    for kt in range(KT):
                        col = m * KT + kt
                        eng = "dve"
                        b.ts(eng, x[:, kt, 0:n], hT[:, kt, tok0:tok0 + n], omu[:, col:col + 1], ALU.mult)
                        muv = mu[:, col:col + 1]
                        if is_ctx:
                            if kt < KT // 2:
                                dst, src = x[:, kt, 1:n], hT[:, kt, tok0:tok0 + n - 1]
                            else:
                                dst, src = x[:, kt, 0:n - 1], hT[:, kt, tok0 + 1:tok0 + n]
                        else:
                            q = kt // (KT // 4)
                            t0 = tok0 - L
                            if q == 0:
                                dst = View(x, x.h[:, kt, 0:n].rearrange("p (r c) -> p r c", c=GRID_W)[:, :, 1:GRID_W])
                                src = View(hT, hT.h[:, kt, tok0:tok0 + n].rearrange("p (r c) -> p r c", c=GRID_W)[:, :, 0:GRID_W - 1])
                            elif q == 1:
                                dst = View(x, x.h[:, kt, 0:n].rearrange("p (r c) -> p r c", c=GRID_W)[:, :, 0:GRID_W - 1])
                                src = View(hT, hT.h[:, kt, tok0:tok0 + n].rearrange("p (r c) -> p r c", c=GRID_W)[:, :, 1:GRID_W])
                            elif q == 2:
                                lo = GRID_W if t0 == 0 else 0
                                dst, src = x[:, kt, lo:n], hT[:, kt, tok0 + lo - GRID_W:tok0 + n - GRID_W]
                            else:
                                hi = n - GRID_W if t0 + n == T else n
                                dst, src = x[:, kt, 0:hi], hT[:, kt, tok0 + GRID_W:tok0 + hi + GRID_W]
                        b.stt("dve", dst, src, muv, dst, ALU.mult, ALU.add)
                    return x

                def load_w(key, n0):
                    w = wch[state["kw"] % 2]
                    state["kw"] += 1
                    b.dma(w, self.wb[key][:, :, n0:n0 + 512])
                    return w

                def nextps():
                    p = ps[state["kp"] % 4]
                    state["kp"] += 1
                    return p

                def nextev():
                    e = ev[state["ke"] % 3]
                    state["ke"] += 1
                    return e

                def rows(tok0, t):
                    return slice(base + tok0 + t * 128, base + tok0 + (t + 1) * 128)

                def lora_hidden(x, n, key, width, func):
                    b.dma(l1w[:, :, 0:width], self.wb[key])
                    for mt in range((width + 127) // 128):
                        wd = min(128, width - mt * 128)
                        p = pl[mt % 2]
                        b.mm(p[0:wd, 0:n], [(l1w[:, kt, mt * 128:mt * 128 + wd], x[:, kt, 0:n]) for kt in range(KT)])
                        if func is None:
                            b.copy("act", l1[0:wd, mt, 0:n], p[0:wd, 0:n])
                        else:
                            b.act(l1[0:wd, mt, 0:n], p[0:wd, 0:n], func)

                def lora_out(t, n0, width):
                    p = nextps()
                    nmt = (width + 127) // 128
                    b.mm(p, [(l1[0:min(128, width - mt * 128), mt, t * 128:(t + 1) * 128],
                              l2w[0:min(128, width - mt * 128), mt, n0:n0 + 512]) for mt in range(nmt)])
                    return p

                def load_l2(key, width):
                    src = self.wb[key]
                    nmt = (width + 127) // 128
                    pk = min(width, 128)
                    b.dma(l2w[0:pk, 0:nmt, :], src)

                for (tok0, n, is_ctx) in blocks:
                    nt = n // 128
                    for nm, key, dst in (("r", ("w_r", j), scr["r"]), ("k", ("w_k", j), scr["k"]), ("v", ("w_v", j), vdst)):
                        x = build_xm(MIX[nm], tok0, n, is_ctx)
                        vres = (nm == "v" and j > 0)
                        if vres:
                            lora_hidden(x, n, ("v1", j), 64, None)
                            load_l2(("v2", j), 64)
                            self.load_bc(brow, self.inp["rwkv_v0"], self.inp["rwkv_v0"].h[j - 1, :])
                        for n0 in range(0, D, 512):
                            w = load_w(key, n0)
                            for t in range(nt):
                                p = nextps()
                                b.mm(p, [(x[:, kt, t * 128:(t + 1) * 128], w[:, kt, :]) for kt in range(KT)])
                                e = nextev()
                                b.copy("act", e, p)
                                if vres:
                                    p2 = lora_out(t, n0, 64)
                                    e2 = nextev()
                                    b.tt("dve", e2, p2, brow[:, n0:n0 + 512], ALU.add)
                                    b.act(e2, e2, AF.Sigmoid)
                                    b.dma(vft, scr["vf"][rows(tok0, t), n0:n0 + 512], q="act")
                                    b.tt("dve", vft, vft, e, ALU.subtract)
                                    b.tt("dve", vft, vft, e2, ALU.mult)
                                    b.tt("dve", e, e, vft, ALU.add)
                                b.dma(dst[rows(tok0, t), n0:n0 + 512], e, q="pool")
                    for nm in ("w", "a"):
                        x = build_xm(MIX[nm], tok0, n, is_ctx)
                        for z in range(2):
                            lora_hidden(x, n, (nm + "1", j, z), 96, AF.Tanh if nm == "w" else None)
                            load_l2((nm + "2", j, z), 96)
                            src0 = self.inp["rwkv_w0" if nm == "w" else "rwkv_a0"]
                            self.load_bc(brow, src0, src0.h[j, z, :])
                            for n0 in range(0, D, 512):
                                for t in range(nt):
                                    p2 = lora_out(t, n0, 96)
                                    e = nextev()
                                    b.tt("dve", e, p2, brow[:, n0:n0 + 512], ALU.add)
                                    b.act(e, e, AF.Sigmoid)
                                    if nm == "w":
                                        b.act(e, e, AF.Exp, scale=NEG_EXP_HALF)
                                    b.dma(scr[nm + str(z)][rows(tok0, t), n0:n0 + 512], e, q="pool")
                    x = build_xm(MIX["g"], tok0, n, is_ctx)
                    lora_hidden(x, n, ("g1", j), 256, AF.Sigmoid)
                    load_l2(("g2", j), 256)
                    for n0 in range(0, D, 512):
                        for t in range(nt):
                            p2 = lora_out(t, n0, 256)
                            e = nextev()
                            b.copy("act", e, p2)
                            b.dma(scr["g"][rows(tok0, t), n0:n0 + 512], e, q="pool")

    def lane_ap(self, buf, bb, tok_first, step, nt):
        c = self.c
        D = c["D"]
        h = buf.h
        off = h.offset + (bb * c["S"] + tok_first) * D
        return View(buf, bass.AP(h.tensor, off, [[64, c["RH"]], [step * D, nt], [1, 64]]))

    def rwkv_scan(self, i, j):
        b, c = self.b, self.c
        NB, RH, L, T, S = c["NB"], c["RH"], c["L"], c["T"], c["S"]
        NL = 2 * NB * RH
        TC = 32
        scr = self.rscr
        vsrc = scr["vf"] if j == 0 else scr["v"]
        with b.scope():
            St = b.sbuf("sS", [NL, 64, 64], F32)
            tmp = b.sbuf("stmp", [NL, 64, 64], F32)
            vk = [b.sbuf(f"svk{k}", [NL, 64, 64], F32) for k in range(2)]
            sz = b.sbuf("ssz", [NL, 64], F32)
            inb = [{nm: b.sbuf(f"s{nm}{k}", [NL, TC, 64], F32) for nm in ("R", "K", "V", "W", "A")} for k in range(2)]
            Zb = b.sbuf("sZ", [NL, TC, 64], F32)
            T2 = b.sbuf("sT2", [NL, TC, 64], F32)
            yb = [b.sbuf(f"sy{k}", [NL, TC, 64], F32) for k in range(2)]
            ssq = b.sbuf("sssq", [NL, TC], F32)
            kk_t = b.sbuf("skk", [NL, 64], F32)
            ka_t = b.sbuf("ska", [NL, 64], F32)
            oka_t = b.sbuf("soka", [NL, 64], F32)
            for z in range(2):
                for bb in range(NB):
                    p0 = (z * NB + bb) * RH
                    b.dma(kk_t[p0:p0 + RH, :], View(self.inp["rwkv_k_k"], self.inp["rwkv_k_k"].h[j].rearrange("(h n) -> h n", n=64)))
                    b.dma(ka_t[p0:p0 + RH, :], View(self.inp["rwkv_k_a"], self.inp["rwkv_k_a"].h[j].rearrange("(h n) -> h n", n=64)))
            b.ts("dve", oka_t, ka_t, -1.0, ALU.mult, 1.0, ALU.add)
            b.memset("dve", St, 0.0)
            shp3 = [NL, 64, 64]
            shpc = [NL, TC, 64]
            nch = S // TC

            def chunk_src(ci, z):
                s0 = ci * TC
                if z == 0:
                    return s0, 1
                if s0 < L:
                    return L - 1 - s0, -1
                return L + T - 1 - (s0 - L), -1

            def load_chunk(ci):
                bufs = inb[ci % 2]
                for z in range(2):
                    tf, step = chunk_src(ci, z)
                    for bb in range(NB):
                        p0 = (z * NB + bb) * RH
                        for nm, src in (("R", scr["r"]), ("K", scr["k"]), ("V", vsrc), ("W", scr[f"w{z}"]), ("A", scr[f"a{z}"])):
                            b.dma(bufs[nm][p0:p0 + RH, :, :], self.lane_ap(src, bb, tf, step, TC))

            load_chunk(0)
            kv = 0
            for ci in range(nch):
                if ci + 1 < nch:
                    load_chunk(ci + 1)
                bufs = inb[ci % 2]
                Rb, Kb, Vb, Wb, Ab = (bufs[nm] for nm in ("R", "K", "V", "W", "A"))
                y = yb[ci % 2]
                b.tt("dve", Zb, Kb, kk_t.v.us(1).bc(shpc), ALU.mult)
                b.tt("dve", T2, Zb, Zb, ALU.mult)
                b.red(ssq, T2)
                b.act(ssq, ssq, AF.Sqrt)
                b.ts("dve", ssq, ssq, 1e-12, ALU.max)
                b.op("dve", lambda en: en.reciprocal(out=ssq.h[:], in_=ssq.h[:]), w=[ssq], r=[ssq])
                b.ts("dve", ssq, ssq, -1.0, ALU.mult)
                b.tt("dve", Zb, Zb, ssq.v.us(2).bc(shpc), ALU.mult)
                b.tt("dve", T2, Ab, ka_t.v.us(1).bc(shpc), ALU.mult)
                b.tt("dve", T2, T2, oka_t.v.us(1).bc(shpc), ALU.add)
                b.tt("dve", Kb, Kb, T2, ALU.mult)
                b.stt("dve", Ab, Zb, -1.0, Ab, ALU.mult, ALU.mult)
                for t in range(TC):
                    zt = Zb[:, t, :].us(1).bc(shp3)
                    wt = Wb[:, t, :].us(1).bc(shp3)
                    bt = Ab[:, t, :].us(1).bc(shp3)
                    kt_ = Kb[:, t, :].us(1).bc(shp3)
                    rt = Rb[:, t, :].us(1).bc(shp3)
                    vt = Vb[:, t, :].us(2).bc(shp3)
                    vkb = vk[kv % 2]
                    kv += 1
                    b.tt("pool", vkb, vt, kt_, ALU.mult)
                    b.tt("dve", tmp, St, zt, ALU.mult)
                    b.red(sz, tmp)
                    b.tt("dve", St, St, wt, ALU.mult)
                    b.tt("dve", tmp, sz.v.us(2).bc(shp3), bt, ALU.mult)
                    b.tt("dve", St, St, tmp, ALU.add)
                    b.tt("dve", St, St, vkb, ALU.add)
                    b.tt("dve", tmp, St, rt, ALU.mult)
                    b.red(y[:, t, :], tmp)
                for z in range(2):
                    tf, step = chunk_src(ci, z)
                    for bb in range(NB):
                        p0 = (z * NB + bb) * RH
                        b.dma(self.lane_ap(scr[f"y{z}"], bb, tf, step, TC), y[p0:p0 + RH, :, :], q="act")

    def rwkv_readout(self, i, j):
        b, c = self.b, self.c
        D, KT, RH = c["D"], c["KT"], c["RH"]
        last = i == c["DEPTH"] - 1
        scr = self.rscr
        vsrc = scr["vf"] if j == 0 else scr["v"]
        shp = [128, RH, 64]
        with b.scope():
            names = ("y0", "y1", "r", "k", "v", "g", "a0", "a1")
            ld = {nm: b.sbuf("q" + nm, [128, D], F32) for nm in names}
            srcs = dict(y0=scr["y0"], y1=scr["y1"], r=scr["r"], k=scr["k"], v=vsrc, g=scr["g"], a0=scr["a0"], a1=scr["a1"])
            lnw = b.sbuf("qlnw", [128, D], F32)
            lnb = b.sbuf("qlnb", [128, D], F32)
            kab = b.sbuf("qka", [128, D], F32)
            ka2 = b.sbuf("qka2", [128, D], F32)
            rkb = b.sbuf("qrk", [128, D], F32)
            G = b.sbuf("qG", [128, D], F32)
            self.load_bc(lnw, self.inp["rwkv_ln_w"], self.inp["rwkv_ln_w"].h[j, :])
            self.load_bc(lnb, self.inp["rwkv_ln_b"], self.inp["rwkv_ln_b"].h[j, :])
            self.load_bc(kab, self.inp["rwkv_k_a"], self.inp["rwkv_k_a"].h[j, :])
            self.load_bc(rkb, self.inp["rwkv_r_k"], self.inp["rwkv_r_k"].h[j].rearrange("h n -> (h n)"))
            b.ts("dve", ka2, kab, -2.0, ALU.mult, 2.0, ALU.add)
            st1 = b.sbuf("qst1", [128, RH], F32)
            st2 = b.sbuf("qst2", [128, RH], F32)
            gneps = b.sbuf("qeps", [128, 1], F32)
            b.memset("dve", gneps, 64 * 1e-5)
            ob = b.sbuf("qob", [128, D], BF16)
            oT = b.sbuf("qoT", [128, KT, 512], BF16)
            wch = [b.sbuf(f"qw{k}", [128, KT, 512], BF16) for k in range(2)]
            ot = [b.sbuf(f"qot{k}", [128, 512], F32) for k in range(2)]
            og = [b.sbuf(f"qog{k}", [128, 512], F32) for k in range(2)]
            tps = [b.psum(f"qtp{k}", [128, 4, 128], BF16) for k in range(2)]
            ps = [b.psum(f"qps{k}", [128, 512], F32) for k in range(4)]
            cur_g = None
            kw = 0
            kp = 0
            v3 = lambda t_: View(t_, t_.h[:].rearrange("p (h n) -> p h n", n=64))
            for (r0, ntok, mg, is_ctx, bb) in self.groups(512, include_ctx=not last):
                nt = ntok // 128
                if mg != cur_g:
                    mr = self.modrows
                    self.load_bc(G, mr, mr.h[i, mg, 2 * D:3 * D])
                    cur_g = mg
                for t in range(nt):
                    rr = slice(r0 + t * 128, r0 + (t + 1) * 128)
                    for nm in names:
                        b.dma(ld[nm], srcs[nm][rr, :], q="sp" if nm in ("y0", "r", "v", "a0") else "act")
                    y0, y1, r_, k_, v_, g_, a0, a1 = (ld[nm] for nm in names)
                    b.tt("dve", y0, y0, y1, ALU.add)
                    b.red(st1, v3(y0))
                    b.ts("dve", st1, st1, 1.0 / 64, ALU.mult)
                    b.tt("dve", v3(y0), v3(y0), st1.v.us(2).bc(shp), ALU.subtract)
                    b.tt("pool", y1, y0, y0, ALU.mult)
                    b.red(st2, v3(y1))
                    b.act(st2, st2, AF.Sqrt, bias=gneps[:, 0:1], scale=1.0 / 64)
                    b.op("dve", lambda en: en.reciprocal(out=st2.h[:], in_=st2.h[:]), w=[st2], r=[st2])
                    b.tt("dve", v3(y0), v3(y0), st2.v.us(2).bc(shp), ALU.mult)
                    b.tt("dve", y0, y0, lnw, ALU.mult)
                    b.tt("dve", y0, y0, lnb, ALU.add)
                    b.tt("pool", a0, a0, a1, ALU.add)
                    b.tt("pool", a0, a0, kab, ALU.mult)
                    b.tt("pool", a0, a0, ka2, ALU.add)
                    b.tt("pool", k_, k_, a0, ALU.mult)
                    b.tt("pool", r_, r_, k_, ALU.mult)
                    b.tt("dve", r_, r_, rkb, ALU.mult)
                    b.red(st1, v3(r_))
                    b.tt("dve", v3(v_), v3(v_), st1.v.us(2).bc(shp), ALU.mult)
                    b.tt("dve", y0, y0, v_, ALU.add)
                    b.tt("dve", ob, y0, g_, ALU.mult)
                    self.transpose_tile(ob, oT, t * 128, tps, t)
                for n0 in range(0, D, 512):
                    w = wch[kw % 2]
                    kw += 1
                    b.dma(w, self.wb[("w_o", j)][:, :, n0:n0 + 512])
                    for t in range(nt):
                        p = ps[kp % 4]
                        kp += 1
                        b.mm(p, [(oT[:, kt, t * 128:(t + 1) * 128], w[:, kt, :]) for kt in range(KT)])
                        o, gg = ot[t % 2], og[t % 2]
                        rows = self.xs[r0 + t * 128:r0 + (t + 1) * 128, n0:n0 + 512]
                        b.dma(o, rows, q="act")
                        b.tt("dve", gg, p, G[:, n0:n0 + 512], ALU.mult)
                        b.tt("pool", o, o, gg, ALU.add)
                        b.dma(rows, o, q="pool")

    def build_hT(self, i, sub, bb, hT):
        b, c = self.b, self.c
        D, L, T, S = c["D"], c["L"], c["T"], c["S"]
        base = bb * S
        with b.scope():
            A = b.sbuf("hA", [128, D], F32)
            Bsh = b.sbuf("hB", [128, D], F32)
            xt = [b.sbuf(f"hx{k}", [128, D], F32) for k in range(2)]
            hf = b.sbuf("hhf", [128, D], F32)
            tmpm = hf
            hb = [b.sbuf(f"hhb{k}", [128, D], BF16) for k in range(2)]
            st = b.sbuf("hst", [128, 2], F32)
            tps = [b.psum(f"htp{k}", [128, 4, 128], BF16) for k in range(2)]
            for (g, t0, n) in ((c["NB"], 0, L), (bb, L, T)):
                self.mod_tiles(i, sub, g, A, Bsh, None, tmpm)
                for t in range(n // 128):
                    x, h = xt[t % 2], hb[t % 2]
                    b.dma(x, self.xs[base + t0 + t * 128: base + t0 + (t + 1) * 128, :])
                    self.norm_tile(x, A, Bsh, h, h, hf, st)
                    self.transpose_tile(h, hT, t0 + t * 128, tps, t)

    def mlstm(self, i):
        c = self.c
        j = i // 2
        if not hasattr(self, "mscr"):
            b = self.b
            R, D = c["R"], c["D"]
            self.mscr = dict(v=b.dram("ms_v", [R, D], BF16), o=b.dram("ms_o", [R, D], F32),
                             h0=b.dram("ms_h0", [R, D], F32), h1=b.dram("ms_h1", [R, D], F32),
                             dec=b.dram("ms_dec", [(32 + c["MH"]) * (c["S"] // CHUNK)], F32))
        for bb in range(c["NB"]):
            self.mlstm_seq(i, j, bb)
        self.mlstm_readout(i, j)

    def mlstm_seq(self, i, j, bb):
        b, c = self.b, self.c
        D, KT, L, T, S, MH, DK, DV, QK = c["D"], c["KT"], c["L"], c["T"], c["S"], c["MH"], c["DK"], c["DV"], c["QK"]
        assert DK == 128
        base = bb * S
        scr = self.mscr
        w_in = self.wb[("m_in", j)]
        NL = 2 * MH
        NLP = 32 + MH
        NC = S // CHUNK
        NCc = L // CHUNK
        NCl = T // CHUNK
        NR = T // GRID_W
        G0 = 2 * QK + 2 * D
        NG = 4 * MH
        with b.scope():
            qT = b.sbuf("mqT", [128, MH, S], BF16)
            kT = b.sbuf("mkT", [128, MH, S], BF16)
            GTall = b.sbuf("mGT", [NG, S], F32)
            with b.scope():
                hT = b.sbuf("mhT", [128, KT, S], BF16)
                self.build_hT(i, 0, bb, hT)
                NCT = 2 * QK // 128
                cw = b.sbuf("mcw", [128, NCT, 10], F32)
                with b.scope():
                    crow = b.sbuf("mcrow", [10, 2 * QK], F32)
                    b.dma(crow[0:9, :], View(self.inp["mlstm_conv_w"], self.inp["mlstm_conv_w"].h[j].rearrange("a b c -> (a b) c")))
                    b.dma(crow[9:10, :], View(self.inp["mlstm_conv_b"], self.inp["mlstm_conv_b"].h[j:j + 1, :]))
                    pcw = b.psum("mpcw", [128, 512], F32)
                    for ct in range(NCT):
                        b.transpose(pcw[:, 0:10], crow[:, ct * 128:(ct + 1) * 128], self.ident_f[0:10, 0:10])
                        b.copy("dve", cw[:, ct, :], pcw[:, 0:10])
                pre = b.sbuf("mpre", [128, S], F32)
                acc = b.sbuf("macc", [128, S], F32)
                wq = [b.sbuf(f"mwq{k}", [128, KT, 128], BF16) for k in range(2)]
                pp = [b.psum(f"mpp{k}", [128, 512], F32) for k in range(2)]
                kp = 0
                for ct in range(NCT):
                    w = wq[ct % 2]
                    b.dma(w, w_in[:, :, ct * 128:(ct + 1) * 128])
                    for t0 in range(0, S, 512):
                        n = min(512, S - t0)
                        p = pp[kp % 2]
                        kp += 1
                        b.mm(p[:, 0:n], [(w[:, kt, :], hT[:, kt, t0:t0 + n]) for kt in range(KT)])
                        b.copy("act", pre[:, t0:t0 + n], p[:, 0:n])
                    wv = lambda dy, dx: cw[:, ct, dy * 3 + dx: dy * 3 + dx + 1]
                    b.ts("dve", acc, pre, wv(1, 1), ALU.mult, cw[:, ct, 9:10], ALU.add)
                    b.stt("dve", acc[:, 1:L], pre[:, 0:L - 1], wv(1, 0), acc[:, 1:L], ALU.mult, ALU.add)
                    b.stt("dve", acc[:, 0:L - 1], pre[:, 1:L], wv(1, 2), acc[:, 0:L - 1], ALU.mult, ALU.add)
                    a3 = lambda t_: t_.h[:, L:S].rearrange("p (r c) -> p r c", c=GRID_W)
                    for dy in range(3):
                        for dx in range(3):
                            if dy == 1 and dx == 1:
                                continue
                            r_lo, r_hi = max(0, 1 - dy), min(NR, NR + 1 - dy)
                            c_lo, c_hi = max(0, 1 - dx), min(GRID_W, GRID_W + 1 - dx)
                            dst = View(acc, a3(acc)[:, r_lo:r_hi, c_lo:c_hi])
                            src = View(pre, a3(pre)[:, r_lo + dy - 1:r_hi + dy - 1, c_lo + dx - 1:c_hi + dx - 1])
                            b.stt("dve", dst, src, wv(dy, dx), dst, ALU.mult, ALU.add)
                    if ct < NCT // 2:
                        b.act(pre, acc, AF.Silu)
                        b.ts("pool", qT[:, ct, :], pre, float(DK) ** -0.5, ALU.mult)
                    else:
                        b.act(kT[:, ct - NCT // 2, :], acc, AF.Silu)
                NCW = 256
                wch = [b.sbuf(f"mwch{k}", [128, KT, NCW], BF16) for k in range(2)]
                evb = [b.sbuf(f"mevb{k}", [128, NCW], BF16) for k in range(2)]
                evf = [b.sbuf(f"mevf{k}", [128, NCW], F32) for k in range(2)]
                pv = [b.psum(f"mpv{k}", [128, 512], F32) for k in range(3)]
                kw = 0
                kq = 0
                for which in ("v", "o"):
                    c0 = 2 * QK + (0 if which == "v" else D)
                    for n0 in range(0, D, NCW):
                        w = wch[kw % 2]
                        kw += 1
                        b.dma(w, w_in[:, :, c0 + n0:c0 + n0 + NCW])
                        for t in range(S // 128):
                            p = pv[kq % 3]
                            b.mm(p[:, 0:NCW], [(hT[:, kt, t * 128:(t + 1) * 128], w[:, kt, :]) for kt in range(KT)])
                            rows = slice(base + t * 128, base + (t + 1) * 128)
                            if which == "v":
                                e = evb[kq % 2]
                                b.copy("act", e, p[:, 0:NCW])
                                b.dma(scr["v"][rows, n0:n0 + NCW], e, q="pool")
                            else:
                                e = evf[kq % 2]
                                b.act(e, p[:, 0:NCW], AF.Sigmoid)
                                b.dma(scr["o"][rows, n0:n0 + NCW], e, q="pool")
                            kq += 1
                wg = b.sbuf("mwg", [128, KT, NG], BF16)
                b.dma(wg, w_in[:, :, G0:G0 + NG])
                bg = b.sbuf("mbg", [128, NG], F32)
                self.load_bc(bg, self.inp["mlstm_b_gate"], self.inp["mlstm_b_gate"].h[j].rearrange("z f h -> (z f h)"))
                gt = [b.sbuf(f"mgt{k}", [128, NG], F32) for k in range(2)]
                pgt = b.psum("mpgt", [128, 512], F32)
                for t in range(S // 128):
                    p = pv[t % 3]
                    g = gt[t % 2]
                    b.mm(p[:, 0:NG], [(hT[:, kt, t * 128:(t + 1) * 128], wg[:, kt, :]) for kt in range(KT)])
                    b.tt("dve", g, p[:, 0:NG], bg, ALU.add)
                    b.act(g, g, AF.Tanh, scale=1.0 / GATE_CAP)
                    b.ts("dve", g, g, GATE_CAP, ALU.mult)
                    b.transpose(pgt[0:NG, 0:128], g, self.ident_f)
                    b.copy("dve", GTall[:, t * 128:(t + 1) * 128], pgt[0:NG, 0:128])
            self.mnegM = b.sbuf("mnegM", [NLP, S], F32)
            self.mcol = [b.sbuf(f"mcol{k}", [64, NC, NLP], F32) for k in range(4)]
            self.mdecB = b.sbuf("mdecB", [128, NLP * NC], F32)
            with b.scope():
                bufs = [b.sbuf(f"gb{k}", [NLP, S], F32) for k in range(7)]
                IGn, LFn, IG, LF, X2, natA, natB = bufs
                b.memset("dve", IGn, 0.0)
                b.memset("pool", LFn, 0.0)
                for z in range(2):
                    b.dma(IGn[z * 32:z * 32 + MH, :], GTall[z * 2 * MH:z * 2 * MH + MH, :])
                    b.dma(LFn[z * 32:z * 32 + MH, :], GTall[z * 2 * MH + MH:(z + 1) * 2 * MH, :])

                def rev(t_, s0, n):
                    h = t_.h[32:NLP, :]
                    return View(t_, bass.AP(h.tensor, h.offset + s0 + n - 1, [list(h.ap[0]), [-1, n]]))

                def to_scan(dst, src, eng):
                    b.copy(eng, dst[0:32, :], src[0:32, :])
                    for (s0, n) in ((0, L), (L, T)):
                        b.copy(eng, dst[32:NLP, s0:s0 + n], rev(src, s0, n))

                to_scan(IG, IGn, "dve")
                to_scan(LF, LFn, "pool")
                b.act(LF, LF, AF.Exp, scale=-1.0)
                b.ts("dve", LF, LF, 1.0, ALU.add)
                b.act(LF, LF, AF.Ln)
                b.ts("dve", LF, LF, -1.0, ALU.mult)
                one = b.sbuf("gone", [NLP, 1], F32)
                b.memset("dve", one, 1.0)
                Gc = IGn
                onesb = one.v.bc([NLP, S])
                b.op("dve", lambda en: en.tensor_tensor_scan(out=Gc.h[:], data0=onesb.ap, data1=LF.h[:], initial=0.0,
                                                             op0=ALU.mult, op1=ALU.add), w=[Gc], r=[one, LF])
                c3 = lambda t_: View(t_, t_.h[:].rearrange("p (c t) -> p c t", t=CHUNK))
                shp = [NLP, NC, CHUNK]
                Gs = b.sbuf("gGs", [NLP, NC], F32)
                b.memset("dve", Gs[:, 0:1], 0.0)
                b.copy("dve", Gs[:, 1:NC], c3(Gc)[:, 0:NC - 1, CHUNK - 1])
                bcum = Gc
                b.tt("dve", c3(bcum), c3(Gc), Gs.v.us(2).bc(shp), ALU.subtract)
                gq = IG
                b.tt("dve", gq, IG, bcum, ALU.subtract)
                cur, nxt = LF, LFn
                b.copy("dve", cur, gq)
                s_ = 1
                while s_ < CHUNK:
                    b.copy("pool", c3(nxt)[:, :, 0:s_], c3(cur)[:, :, 0:s_])
                    b.tt("dve", c3(nxt)[:, :, s_:CHUNK], c3(cur)[:, :, s_:CHUNK], c3(cur)[:, :, 0:CHUNK - s_], ALU.max)
                    cur, nxt = nxt, cur
                    s_ *= 2
                cm, spare = cur, nxt
                gmax = b.sbuf("ggmax", [NLP, NC], F32)
                bend = b.sbuf("gbend", [NLP, NC], F32)
                b.copy("dve", gmax, c3(cm)[:, :, CHUNK - 1])
                b.copy("dve", bend, c3(bcum)[:, :, CHUNK - 1])
                mnext = b.sbuf("gmnext", [NLP, NC], F32)
                b.op("dve", lambda en: en.tensor_tensor_scan(out=mnext.h[:], data0=gmax.h[:], data1=bend.h[:], initial=0.0,
                                                             op0=ALU.max, op1=ALU.add), w=[mnext], r=[gmax, bend])
                mst = b.sbuf("gmst", [NLP, NC], F32)
                b.memset("dve", mst[:, 0:1], 0.0)
                b.copy("dve", mst[:, 1:NC], mnext[:, 0:NC - 1])
                M = spare
                b.tt("dve", c3(M), c3(cm), mst.v.us(2).bc(shp), ALU.max)
                Mlast = b.sbuf("gMlast", [NLP, NC], F32)
                b.tt("dve", Mlast, mst, gmax, ALU.max)
                dec = b.sbuf("gdec", [NLP, NC], F32)
                b.tt("dve", dec, mst, Mlast, ALU.subtract)
                b.act(dec, dec, AF.Exp)
                b.dma(View(scr["dec"], scr["dec"].h[:].rearrange("(l c) -> l c", c=NC)), dec)
                t1_ = cm
                b.tt("dve", c3(t1_), mst.v.us(2).bc(shp), c3(M), ALU.subtract)
                b.act(t1_, t1_, AF.Exp)
                t2_ = bcum
                b.tt("dve", t2_, bcum, M, ALU.add)
                b.act(t2_, t2_, AF.Exp, scale=-1.0)
                t3_ = X2
                b.tt("dve", c3(t3_), c3(gq), Mlast.v.us(2).bc(shp), ALU.subtract)
                b.act(t3_, t3_, AF.Exp)
                b.ts("dve", M, M, -1.0, ALU.mult)
                tabs = [gq, t1_, t2_, t3_, M]

                def to_nat(dst, src, eng):
                    b.copy(eng, dst[0:32, :], src[0:32, :])
                    for (s0, n) in ((0, L), (L, T)):
                        b.copy(eng, dst[32:NLP, s0:s0 + n], rev(src, s0, n))

                to_nat(self.mnegM, M, "pool")
                pct = [b.psum(f"gpct{k}", [64, 512], F32) for k in range(2)]
                per = 512 // NLP
                kk_ = 0
                for k in range(4):
                    nat = (natA, natB)[k % 2]
                    to_nat(nat, tabs[k], ("dve", "pool")[k % 2])
                    for c0 in range(0, NC, per):
                        nn = min(per, NC - c0)
                        p = pct[kk_ % 2]
                        kk_ += 1
                        for q in range(nn):
                            b.transpose(p[:, q * NLP:(q + 1) * NLP], nat[:, (c0 + q) * CHUNK:(c0 + q + 1) * CHUNK],
                                        self.ident_f[0:NLP, 0:NLP])
                        b.copy("dve", View(self.mcol[k], self.mcol[k].h[:, c0:c0 + nn, :]),
                               View(p, p.h[:, 0:nn * NLP].rearrange("p (c l) -> p c l", l=NLP)))
                b.dma(self.mdecB, View(scr["dec"], scr["dec"].h[:].partition_broadcast(128)))
            self.mlstm_chunks(bb, qT, kT)

    def mlstm_chunks(self, bb, qT, kT):
        b, c = self.b, self.c
        D, L, T, S, MH, DV = c["D"], c["L"], c["T"], c["S"], c["MH"], c["DV"]
        base = bb * S
        scr = self.mscr
        NL = 2 * MH
        NLP = 32 + MH
        NC, NCc, NCl = S // CHUNK, L // CHUNK, T // CHUNK
        DV1 = DV + 1
        negM, col, decB = self.mnegM, self.mcol, self.mdecB
        with b.scope():
            Cst = b.sbuf("cC", [128, NL, DV1], F32)
            Cbf = b.sbuf("cCb", [128, NL, DV1], BF16)
            b.memset("dve", Cst, 0.0)
            b.memset("pool", Cbf, 0.0)
            sel = b.sbuf("csel", [NLP, NLP, 64], F32)
            b.dma(sel, self.inp["msel"][0:NLP, 0:NLP, :])
            mneg = b.sbuf("cmneg", [64, 2, 64], F32)
            b.dma(mneg, self.inp["mmask"])
            vx = [b.sbuf(f"cvx{k}", [64, MH, DV1], BF16) for k in range(4)]
            for t_ in vx:
                b.memset("dve", t_, 1.0)
            ho = [b.sbuf(f"cho{k}", [64, D], F32) for k in range(4)]
            Dt = [b.sbuf(f"cDt{k}", [64, 64], F32) for k in range(2)]
            SpT = [b.sbuf(f"cSp{k}", [64, 64], BF16) for k in range(2)]
            t1 = [b.sbuf(f"ct1{k}", [64, DV1], F32) for k in range(2)]
            nd = [b.sbuf(f"cnd{k}", [64, DV1], F32) for k in range(2)]
            dd = [b.sbuf(f"cdd{k}", [64, 1], F32) for k in range(2)]
            wk = [b.sbuf(f"cwk{k}", [64, 128], BF16) for k in range(2)]
            pE = [b.psum(f"cpE{k}", [64, 512], F32) for k in range(2)]
            pnum = b.psum("cpnum", [64, 512], F32)
            pqC = b.psum("cpqC", [64, 512], F32)
            pkk = b.psum("cpkk", [64, 128], BF16)
            pCu = b.psum("cpCu", [128, 512], F32)
            it = 0
            for cs in range(NC):
                for z in range(2):
                    if z == 0:
                        cn = cs
                    else:
                        cn = (NCc - 1 - cs) if cs < NCc else (NCc + NCl - 1 - (cs - NCc))
                    tok0 = cn * CHUNK
                    vt = vx[(cs * 2 + z) % 4]
                    hout = ho[(cs * 2 + z) % 4]
                    rows = slice(base + tok0, base + tok0 + CHUNK)
                    b.dma(vt[:, :, 0:DV], View(scr["v"], scr["v"].h[rows, :].rearrange("t (h e) -> t h e", e=DV)))
                    for h in range(MH):
                        lane = z * MH + h
                        lp = z * 32 + h
                        k2 = it % 2
                        it += 1
                        qs = qT[:, h, tok0:tok0 + CHUNK]
                        ks = kT[:, h, tok0:tok0 + CHUNK]
                        pe_ = pE[k2]
                        b.mm(pe_[:, 0:64], [(sel[:, lp, :], negM[:, tok0:tok0 + CHUNK]),
                                            (self.ident_f[0:64, 0:64], mneg[:, z, :])])
                        b.mm(pe_[:, 64:128], [(ks, qs)])
                        b.act(Dt[k2], pe_[:, 0:64], AF.Exp, bias=col[0][:, cn, lp:lp + 1])
                        b.tt("dve", SpT[k2], pe_[:, 64:128], Dt[k2], ALU.mult)
                        b.mm(pnum[:, 0:DV1], [(SpT[k2], vt[:, h, :])])
                        b.mm(pqC[:, 0:DV1], [(qs, Cbf[:, lane, :])])
                        b.act(t1[k2], pqC[:, 0:DV1], AF.Copy, scale=col[1][:, cn, lp:lp + 1])
                        b.tt("dve", nd[k2], t1[k2], pnum[:, 0:DV1], ALU.add)
                        b.act(dd[k2], nd[k2][:, DV:DV1], AF.Abs)
                        b.tt("dve", dd[k2], dd[k2], col[2][:, cn, lp:lp + 1], ALU.max)
                        b.op("dve", lambda en: en.reciprocal(out=dd[k2].h[:], in_=dd[k2].h[:]), w=[dd[k2]], r=[dd[k2]])
                        b.ts("dve", hout[:, h * DV:(h + 1) * DV], nd[k2][:, 0:DV], dd[k2][:, 0:1], ALU.mult)
                        b.transpose(pkk, ks, self.ident_b)
                        b.ts("dve", wk[k2], pkk, col[3][:, cn, lp:lp + 1], ALU.mult)
                        b.mm(pCu[:, 0:DV1], [(wk[k2], vt[:, h, :])])
                        b.stt("dve", Cst[:, lane, :], Cst[:, lane, :], decB[:, lp * NC + cs:lp * NC + cs + 1],
                              pCu[:, 0:DV1], ALU.mult, ALU.add)
                        b.copy("act", Cbf[:, lane, :], Cst[:, lane, :])
                    b.dma(scr[f"h{z}"][rows, :], hout, q="pool")

    def mlstm_readout(self, i, j):
        b, c = self.b, self.c
        D, KT, MH, DV = c["D"], c["KT"], c["MH"], c["DV"]
        last = i == c["DEPTH"] - 1
        scr = self.mscr
        shp = [128, MH, DV]
        with b.scope():
            names = ("h0", "h1", "o")
            ld = [{nm: b.sbuf(f"u{nm}{k}", [128, D], F32) for nm in names} for k in range(2)]
            nw = b.sbuf("unw", [128, D], F32)
            G = b.sbuf("uG", [128, D], F32)
            self.load_bc(nw, self.inp["mlstm_norm_w"], self.inp["mlstm_norm_w"].h[j, :])
            sq = b.sbuf("usq", [128, D], F32)
            st1 = b.sbuf("ust1", [128, MH], F32)
            st2 = b.sbuf("ust2", [128, MH], F32)
            ob = b.sbuf("uob", [128, D], BF16)
            oT = b.sbuf("uoT", [128, KT, 512], BF16)
            wch = [b.sbuf(f"uw{k}", [128, KT, 512], BF16) for k in range(2)]
            ot = [b.sbuf(f"uot{k}", [128, 512], F32) for k in range(2)]
            og = [b.sbuf(f"uog{k}", [128, 512], F32) for k in range(2)]
            tps = [b.psum(f"utp{k}", [128, 4, 128], BF16) for k in range(2)]
            ps = [b.psum(f"ups{k}", [128, 512], F32) for k in range(4)]
            cur_g = None
            kw = kp = kl = 0
            v3 = lambda t_: View(t_, t_.h[:].rearrange("p (h n) -> p h n", n=DV))
            for (r0, ntok, mg, is_ctx, bb) in self.groups(512, include_ctx=not last):
                nt = ntok // 128
                if mg != cur_g:
                    mr = self.modrows
                    self.load_bc(G, mr, mr.h[i, mg, 2 * D:3 * D])
                    cur_g = mg
                for t in range(nt):
                    rr = slice(r0 + t * 128, r0 + (t + 1) * 128)
                    l_ = ld[kl % 2]
                    kl += 1
                    b.dma(l_["h0"], scr["h0"][rr, :])
                    b.dma(l_["h1"], scr["h1"][rr, :], q="act")
                    b.dma(l_["o"], scr["o"][rr, :])
                    h0, h1, o_ = l_["h0"], l_["h1"], l_["o"]
                    b.tt("dve", h0, h0, h1, ALU.add)
                    b.red(st1, v3(h0))
                    b.ts("dve", st1, st1, 1.0 / DV, ALU.mult)
                    b.tt("dve", v3(h0), v3(h0), st1.v.us(2).bc(shp), ALU.subtract)
                    b.tt("pool", sq, h0, h0, ALU.mult)
                    b.red(st2, v3(sq))
                    b.act(st2, st2, AF.Sqrt, bias=self.eps_t[:, 0:1], scale=1.0 / DV)
                    b.op("dve", lambda en: en.reciprocal(out=st2.h[:], in_=st2.h[:]), w=[st2], r=[st2])
                    b.tt("dve", v3(h0), v3(h0), st2.v.us(2).bc(shp), ALU.mult)
                    b.tt("pool", h0, h0, nw, ALU.mult)
                    b.tt("dve", ob, h0, o_, ALU.mult)
                    self.transpose_tile(ob, oT, t * 128, tps, t)
                for n0 in range(0, D, 512):
                    w = wch[kw % 2]
                    kw += 1
                    b.dma(w, self.wb[("m_out", j)][:, :, n0:n0 + 512])
                    for t in range(nt):
                        p = ps[kp % 4]
                        kp += 1
                        b.mm(p, [(oT[:, kt, t * 128:(t + 1) * 128], w[:, kt, :]) for kt in range(KT)])
                        o, gg = ot[t % 2], og[t % 2]
                        rows = self.xs[r0 + t * 128:r0 + (t + 1) * 128, n0:n0 + 512]
                        b.dma(o, rows, q="act")
                        b.tt("dve", gg, p, G[:, n0:n0 + 512], ALU.mult)
                        b.tt("pool", o, o, gg, ALU.add)
                        b.dma(rows, o, q="pool")


def host_inputs(cfg, inputs, ncores):
    c = cfg
    NB = c["NB"]
    x = np.asarray(inputs["x"], dtype=np.float32)
    ctx = np.asarray(inputs["ctx"], dtype=np.float32)
    cc = np.asarray(inputs["c"], dtype=np.float32)
    c_ctx = np.asarray(inputs["c_ctx"], dtype=np.float32)
    shared = {}
    for name in input_shapes(c):
        if name in ("xs0", "ccT", "ident", "msel", "mmask"):
            continue
        a = np.ascontiguousarray(np.asarray(inputs[name], dtype=np.float32))
        shp = input_shapes(c)[name]
        if list(a.shape) != shp:
            a = np.zeros(shp, np.float32)
        shared[name] = a
    shared["ident"] = np.eye(128, dtype=np.float32)
    shared["msel"], shared["mmask"] = const_tables()
    maps = []
    for k in range(ncores):
        xs0 = np.concatenate([np.concatenate([ctx[k * NB + j], x[k * NB + j]], axis=0) for j in range(NB)], axis=0)
        rows = np.concatenate([cc[k * NB:(k + 1) * NB], c_ctx[None, :]], axis=0)
        ccT = np.ascontiguousarray(rows.T.reshape(c["KT"], 128, NB + 1).transpose(1, 0, 2))
        m = dict(shared)
        m["xs0"] = np.ascontiguousarray(xs0)
        m["ccT"] = ccT
        maps.append(m)
    return maps


def const_tables():
    msel = np.zeros((64, 64, 64), np.float32)
    for l in range(64):
        msel[l, l, :] = 1.0
    jj, tt = np.meshgrid(np.arange(64), np.arange(64), indexing="ij")
    mmask = np.zeros((64, 2, 64), np.float32)
    mmask[:, 0, :] = np.where(jj <= tt, 0.0, -30000.0)
    mmask[:, 1, :] = np.where(jj >= tt, 0.0, -30000.0)
    return msel, mmask


_NC_CACHE = {}


def run_net(cfg, inputs, ncores, **netkw):
    key = (tuple(sorted(cfg.items())), tuple(sorted(netkw.items())))
    if key not in _NC_CACHE:
        net = Net(cfg, **netkw)
        _NC_CACHE[key] = net.build()
    nc = _NC_CACHE[key]
    maps = host_inputs(cfg, inputs, ncores)
    res = run_bass_kernel_spmd(nc, maps, core_ids=list(range(ncores)))
    outs = [r["out"].reshape(cfg["NB"], cfg["T"], cfg["D"]) for r in res.results]
    return np.concatenate(outs, axis=0)


def kernel(**inputs):
    cfg = make_cfg()
    return run_net(cfg, inputs, 8).astype(np.float32)
```

```python
import numpy as np
from contextlib import ExitStack, contextmanager
import concourse.bass as bass
import concourse.mybir as mybir
from concourse.bass_utils import run_bass_kernel_spmd

F32 = mybir.dt.float32
BF16 = mybir.dt.bfloat16
AF = mybir.ActivationFunctionType
ALU = mybir.AluOpType
AX = mybir.AxisListType

SEM_LIMIT = 30000
NORM_EPS = 1e-6
GRID_W = 64
CHUNK = 64
GATE_CAP = 15.0


class Counter:
    def __init__(self, b, name):
        self.b = b
        self.name = name
        self.n = 0
        self.sem, self.val = b.take_sem(f"{name}_0")

    def next(self, inc):
        if self.val + inc > SEM_LIMIT:
            self.n += 1
            self.sem, self.val = self.b.take_sem(f"{self.name}_{self.n}")
        self.val += inc
        return (self.sem, self.val)


class Rec:
    def __init__(self):
        self.w = {}
        self.r = {}


def _merge(d, ev):
    s, v = ev
    if d.get(s, -1) < v:
        d[s] = v


class Buf:
    def __init__(self, b, handle, name, space):
        self.b = b
        self.h = handle
        self.name = name
        self.space = space
        self.rec = Rec()
        self.subs = {}
        self.dctr = None

    def recs(self):
        return [self.rec] + [s.rec for s in self.subs.values()]

    def sub(self, key):
        if key not in self.subs:
            self.subs[key] = SubBuf(self, key)
        return self.subs[key]

    def __getitem__(self, idx):
        return View(self, self.h[idx])

    @property
    def v(self):
        return View(self, self.h[:])

    def dma_counter(self):
        if self.dctr is None:
            self.dctr = Counter(self.b, "d_" + self.name)
        return self.dctr


class SubBuf:
    def __init__(self, parent, key):
        self.parent = parent
        self.rec = Rec()
        self.name = f"{parent.name}.{key}"
        self.space = parent.space
        self.h = parent.h
        self.b = parent.b

    def recs(self):
        return [self.rec, self.parent.rec]

    def __getitem__(self, idx):
        return View(self, self.h[idx])

    @property
    def v(self):
        return View(self, self.h[:])

    def dma_counter(self):
        return self.parent.dma_counter()


class View:
    def __init__(self, buf, ap):
        self.buf = buf
        self.ap = ap

    def __getitem__(self, idx):
        return View(self.buf, self.ap[idx])

    def re(self, s, **kw):
        return View(self.buf, self.ap.rearrange(s, **kw))

    def bc(self, shape):
        return View(self.buf, self.ap.broadcast_to(list(shape)))

    def us(self, axis):
        return View(self.buf, self.ap.unsqueeze(axis))


def _v(x):
    return x if isinstance(x, View) else x.v


class B:
    ENG = ("pe", "act", "dve", "pool", "sp")

    def __init__(self):
        self.nc = bass.Bass("TRN2", target_bir_lowering=False)
        nc = self.nc
        self.es = ExitStack()
        self.eng = {"pe": nc.tensor, "act": nc.scalar, "dve": nc.vector, "pool": nc.gpsimd, "sp": nc.sync}
        self.ctr = {}
        self.known = {e: {} for e in self.ENG}
        for e in self.ENG:
            self.ctr[e] = Counter(self, "c_" + e)
        self.all_bufs = []
        self.ninst = 0
        self.stack = self.es
        self.sem_pool = []

    def take_sem(self, name):
        pool = self.__dict__.setdefault("sem_pool", [])
        while pool:
            sem, val = pool.pop()
            if val < SEM_LIMIT // 2:
                return sem, val
        self.nsem = self.__dict__.get("nsem", 0) + 1
        return self.es.enter_context(self.nc.semaphore(f"{name}_{self.nsem}")), 0

    def dram(self, name, shape, dtype, kind="Internal"):
        t = self.nc.dram_tensor(name, list(shape), dtype, kind=kind).ap()
        bf = Buf(self, t, name, "dram")
        self.all_bufs.append(bf)
        return bf

    def sbuf(self, name, shape, dtype):
        self.uid = getattr(self, "uid", 0) + 1
        name = f"{name}_{self.uid}"
        t = self.stack.enter_context(self.nc.sbuf_tensor(name, list(shape), dtype))
        bf = Buf(self, t, name, "sbuf")
        self.all_bufs.append(bf)
        return bf

    def psum(self, name, shape, dtype):
        self.uid = getattr(self, "uid", 0) + 1
        name = f"{name}_{self.uid}"
        t = self.stack.enter_context(self.nc.psum_tensor(name, list(shape), dtype))
        bf = Buf(self, t, name, "psum")
        self.all_bufs.append(bf)
        return bf

    @contextmanager
    def scope(self):
        prev = self.stack
        nb = len(self.all_bufs)
        with ExitStack() as st:
            self.stack = st
            yield
            self.barrier()
            for x in self.all_bufs[nb:]:
                if x.space != "dram" and x.dctr is not None:
                    self.sem_pool.append((x.dctr.sem, x.dctr.val))
            self.all_bufs = self.all_bufs[:nb] + [x for x in self.all_bufs[nb:] if x.space == "dram"]
            self.stack = prev

    def _deps(self, reads, writes):
        d = {}
        for x in reads:
            for rec in x.buf.recs():
                for ev in rec.w.items():
                    _merge(d, ev)
        for x in writes:
            for rec in x.buf.recs():
                for ev in rec.w.items():
                    _merge(d, ev)
                for ev in rec.r.items():
                    _merge(d, ev)
        return d

    def _wait(self, e, deps, skip_own=False):
        eng = self.eng[e]
        kn = self.known[e]
        own = self.ctr[e].sem
        for s, v in deps.items():
            if skip_own and s is own:
                continue
            if kn.get(s, -1) >= v:
                continue
            eng.wait_ge(s, v)
            kn[s] = v
            self.ninst += 1

    def _record(self, ev, reads, writes):
        for x in reads:
            _merge(x.buf.rec.r, ev)
        for x in writes:
            x.buf.rec.w = {ev[0]: ev[1]}
            x.buf.rec.r = {}
            if isinstance(x.buf, Buf):
                for s in x.buf.subs.values():
                    s.rec.w = {ev[0]: ev[1]}
                    s.rec.r = {}

    def op(self, e, fn, w=(), r=()):
        w = [_v(x) for x in w]
        r = [_v(x) for x in r]
        deps = self._deps(r, w)
        self._wait(e, deps, skip_own=(e == "pe"))
        inst = fn(self.eng[e])
        ev = self.ctr[e].next(1)
        inst.then_inc(ev[0], 1)
        self.ninst += 1
        self._record(ev, r, w)
        return ev

    def mm(self, out, pairs, start=True, stop=True):
        out = _v(out)
        pairs = [(_v(a), _v(c)) for a, c in pairs]
        reads = [x for p in pairs for x in p]
        deps = self._deps(reads, [out])
        self._wait("pe", deps, skip_own=True)
        n = len(pairs)
        inst = None
        for i, (a, c) in enumerate(pairs):
            inst = self.nc.tensor.matmul(out.ap, a.ap, c.ap, start=(start and i == 0), stop=(stop and i == n - 1))
            self.ninst += 1
        ev = self.ctr["pe"].next(1)
        inst.then_inc(ev[0], 1)
        self._record(ev, reads, [out])
        return ev

    def transpose(self, out, in_, ident):
        out, in_, ident = _v(out), _v(in_), _v(ident)
        return self.op("pe", lambda e: e.transpose(out.ap, in_.ap, ident.ap), w=[out], r=[in_, ident])

    def dma(self, out, in_, q="sp"):
        out, in_ = _v(out), _v(in_)
        deps = self._deps([in_], [out])
        self._wait(q, deps)
        owner = out.buf if out.buf.space != "dram" else in_.buf
        ctr = owner.dma_counter()
        inst = self.eng[q].dma_start(out=out.ap, in_=in_.ap)
        ev = ctr.next(16)
        inst.then_inc(ev[0], 16)
        self.ninst += 1
        self._record(ev, [in_], [out])
        return ev

    def barrier(self):
        d = {}
        for bf in self.all_bufs:
            for rec in bf.recs():
                for ev in rec.w.items():
                    _merge(d, ev)
                for ev in rec.r.items():
                    _merge(d, ev)
        for e in self.ENG:
            c = self.ctr[e]
            if c.val > 0:
                _merge(d, (c.sem, c.val))
        for e in self.ENG:
            self._wait(e, d)

    @staticmethod
    def _sc(x):
        return x.ap if isinstance(x, View) else x

    def act(self, out, in_, func, bias=None, scale=None, accum=None):
        out, in_ = _v(out), _v(in_)
        kw = {}
        rd = [in_]
        wr = [out]
        if bias is not None:
            kw["bias"] = self._sc(bias)
            if isinstance(bias, View):
                rd.append(bias)
        if scale is not None:
            kw["scale"] = self._sc(scale)
            if isinstance(scale, View):
                rd.append(scale)
        if accum is not None:
            kw["accum_out"] = accum.ap
            wr.append(accum)
        return self.op("act", lambda e: e.activation(out=out.ap, in_=in_.ap, func=func, **kw), w=wr, r=rd)

    def tt(self, e, out, a, c, op):
        out, a, c = _v(out), _v(a), _v(c)
        return self.op(e, lambda en: en.tensor_tensor(out=out.ap, in0=a.ap, in1=c.ap, op=op), w=[out], r=[a, c])

    def ts(self, e, out, a, s1, op0, s2=None, op1=None):
        out, a = _v(out), _v(a)
        rd = [a] + [s for s in (s1, s2) if isinstance(s, View)]
        if op1 is None:
            return self.op(e, lambda en: en.tensor_scalar(out=out.ap, in0=a.ap, scalar1=self._sc(s1), scalar2=None,
                                                          op0=op0), w=[out], r=rd)
        return self.op(e, lambda en: en.tensor_scalar(out=out.ap, in0=a.ap, scalar1=self._sc(s1),
                                                      scalar2=self._sc(s2), op0=op0, op1=op1), w=[out], r=rd)

    def stt(self, e, out, a, s, c, op0, op1):
        out, a, c = _v(out), _v(a), _v(c)
        rd = [a, c] + ([s] if isinstance(s, View) else [])
        return self.op(e, lambda en: en.scalar_tensor_tensor(out=out.ap, in0=a.ap, scalar=self._sc(s), in1=c.ap,
                                                             op0=op0, op1=op1), w=[out], r=rd)

    def red(self, out, in_, op=ALU.add, axis=AX.X):
        out, in_ = _v(out), _v(in_)
        return self.op("dve", lambda en: en.tensor_reduce(out=out.ap, in_=in_.ap, axis=axis, op=op), w=[out], r=[in_])

    def copy(self, e, out, in_):
        out, in_ = _v(out), _v(in_)
        if e == "act":
            return self.op("act", lambda en: en.copy(out=out.ap, in_=in_.ap), w=[out], r=[in_])
        return self.op(e, lambda en: en.tensor_copy(out=out.ap, in_=in_.ap), w=[out], r=[in_])

    def memset(self, e, out, val):
        out = _v(out)
        return self.op(e, lambda en: en.memset(out.ap, val), w=[out])


def make_cfg(D=2048, NB=2, L=256, T=2048, DEPTH=4, MH=8):
    c = dict(D=D, NB=NB, L=L, T=T, DEPTH=DEPTH, MH=MH)
    c["KT"] = D // 128
    c["S"] = L + T
    c["R"] = NB * (L + T)
    c["FF"] = ((8 * D + 767) // 768) * 256
    c["FT"] = c["FF"] // 128
    c["RH"] = D // 64
    c["DV"] = D // MH
    c["DK"] = c["DV"] // 2
    c["QK"] = MH * c["DK"]
    c["PROJ"] = 2 * c["QK"] + 2 * D + 4 * MH
    c["NRW"] = (DEPTH + 1) // 2
    c["NML"] = DEPTH // 2
    return c


WEIGHT_SPECS = None


def input_shapes(c):
    D, DEPTH, NRW, NML = c["D"], c["DEPTH"], c["NRW"], c["NML"]
    return {
        "xs0": [c["R"], D], "ccT": [128, c["KT"], c["NB"] + 1], "ident": [128, 128],
        "msel": [64, 64, 64], "mmask": [64, 2, 64], "rconst": [64, 7, 64], "rconst2": [64, 2, 128],
        "mod_w": [DEPTH, D, 6 * D], "mod_b": [DEPTH, 6 * D], "norm_g": [DEPTH, 2, D], "final_g": [D],
        "rwkv_mu": [NRW, 6, D], "rwkv_w_r": [NRW, D, D], "rwkv_w_k": [NRW, D, D], "rwkv_w_v": [NRW, D, D],
        "rwkv_w_o": [NRW, D, D], "rwkv_w0": [NRW, 2, D], "rwkv_w1": [NRW, 2, D, 96], "rwkv_w2": [NRW, 2, 96, D],
        "rwkv_a0": [NRW, 2, D], "rwkv_a1": [NRW, 2, D, 96], "rwkv_a2": [NRW, 2, 96, D],
        "rwkv_g1": [NRW, D, 256], "rwkv_g2": [NRW, 256, D], "rwkv_k_k": [NRW, D], "rwkv_k_a": [NRW, D],
        "rwkv_r_k": [NRW, c["RH"], 64], "rwkv_ln_w": [NRW, D], "rwkv_ln_b": [NRW, D],
        "rwkv_v0": [max(NRW - 1, 1), D], "rwkv_v1": [max(NRW - 1, 1), D, 64], "rwkv_v2": [max(NRW - 1, 1), 64, D],
        "mlstm_w_in": [NML, D, c["PROJ"]], "mlstm_b_gate": [NML, 2, 2, c["MH"]],
        "mlstm_conv_w": [NML, 3, 3, 2 * c["QK"]], "mlstm_conv_b": [NML, 2 * c["QK"]],
        "mlstm_norm_w": [NML, D], "mlstm_w_out": [NML, D, D],
        "ffn_w_in": [DEPTH, D, 2 * c["FF"]], "ffn_w_out": [DEPTH, c["FF"], D],
    }


class Net:
    def __init__(self, cfg, skip_rwkv=False, skip_mlstm=False, chunked=True):
        self.c = cfg
        self.chunked = chunked
        self.b = B()
        self.skip_rwkv = skip_rwkv
        self.skip_mlstm = skip_mlstm
        self.cvt_rr = 0

    def declare(self):
        b, c = self.b, self.c
        self.inp = {}
        for name, shp in input_shapes(c).items():
            self.inp[name] = b.dram(name, shp, F32, kind="ExternalInput")
        self.out = b.dram("out", [c["NB"] * c["T"], c["D"]], F32, kind="ExternalOutput")
        self.xs = b.dram("xs", [c["R"], c["D"]], F32)
        self.modrows = b.dram("modrows", [c["DEPTH"], c["NB"] + 1, 6 * c["D"]], F32)

    def consts(self):
        b = self.b
        self.ident_f = b.sbuf("ident_f", [128, 128], F32)
        self.ident_b = b.sbuf("ident_b", [128, 128], BF16)
        b.dma(self.ident_f, self.inp["ident"])
        b.copy("dve", self.ident_b, self.ident_f)
        self.eps_t = b.sbuf("eps_t", [128, 4], F32)
        b.memset("dve", self.eps_t[:, 0:1], NORM_EPS)

    def groups(self, gmax=512, include_ctx=True):
        c = self.c
        gs = []
        for bb in range(c["NB"]):
            base = bb * c["S"]
            if include_ctx:
                t = 0
                while t < c["L"]:
                    n = min(gmax, c["L"] - t)
                    gs.append((base + t, n, c["NB"], True, bb))
                    t += n
            t = 0
            while t < c["T"]:
                n = min(gmax, c["T"] - t)
                gs.append((base + c["L"] + t, n, bb, False, bb))
                t += n
        return gs

    def phase_mod(self):
        b, c = self.b, self.c
        D, KT, NG = c["D"], c["KT"], c["NB"] + 1
        with b.scope():
            ccT = b.sbuf("ccT", [128, KT, NG], F32)
            csT = b.sbuf("csT", [128, KT, NG], F32)
            b.dma(ccT, self.inp["ccT"])
            b.act(csT, ccT, AF.Silu)
            wt = [b.sbuf(f"modw{i}", [128, KT, 512], F32) for i in range(2)]
            bias = b.sbuf("modbias", [NG, 6 * D], F32)
            rows = b.sbuf("modrow_sb", [NG, 6 * D], F32)
            ps = [b.psum(f"modps{i}", [128, 512], F32) for i in range(2)]
            k = 0
            for i in range(c["DEPTH"]):
                b.dma(bias, View(self.inp["mod_b"], self.inp["mod_b"].h[i, :].partition_broadcast(NG)))
                for n0 in range(0, 6 * D, 512):
                    w = wt[k % 2]
                    p = ps[k % 2]
                    k += 1
                    src = self.inp["mod_w"].h[i, :, n0:n0 + 512].rearrange("(kt p) n -> p kt n", p=128)
                    b.dma(w, View(self.inp["mod_w"], src))
                    b.mm(p[0:NG, :], [(csT[:, kt, :], w[:, kt, :]) for kt in range(KT)])
                    b.tt("dve", rows[:, n0:n0 + 512], p[0:NG, :], bias[:, n0:n0 + 512], ALU.add)
                b.dma(self.modrows[i], rows, q="pool")

    def cvt_weight(self, name, src_ap, kdim, n, stage, ncmax=4096):
        b = self.b
        pk = min(kdim, 128)
        ktn = kdim // pk
        dst = b.dram(name, [pk, ktn, n], BF16)
        srcv = src_ap.rearrange("(kt p) n -> p kt n", p=pk)
        if n >= ncmax:
            ktc, ncw = 1, ncmax
        else:
            ktc, ncw = max(1, min(ktn, ncmax // n)), n
        for k0 in range(0, ktn, ktc):
            kk = min(ktc, ktn - k0)
            for n0 in range(0, n, ncw):
                nn = min(ncw, n - n0)
                f, h = stage[self.cvt_rr % len(stage)]
                eng = ("dve", "pool", "act")[self.cvt_rr % 3]
                self.cvt_rr += 1
                fv = View(f, f.h[0:pk, 0:kk * nn].rearrange("p (k n) -> p k n", k=kk))
                hv = View(h, h.h[0:pk, 0:kk * nn].rearrange("p (k n) -> p k n", k=kk))
                b.dma(fv, View(self.srcbuf, srcv[:, k0:k0 + kk, n0:n0 + nn]))
                b.copy(eng, hv, fv)
                b.dma(View(dst, dst.h[:, k0:k0 + kk, n0:n0 + nn]), hv, q="pool" if eng != "pool" else "sp")
        return dst

    def phase_prep(self):
        b, c = self.b, self.c
        D, FF = c["D"], c["FF"]
        self.wb = {}
        with b.scope():
            stage = [(b.sbuf(f"cvf{i}", [128, 4096], F32), b.sbuf(f"cvh{i}", [128, 4096], BF16)) for i in range(3)]

            def cv(key, inname, idx, kdim, n):
                self.srcbuf = self.inp[inname]
                ap = self.inp[inname].h
                for j in idx:
                    ap = ap[j]
                self.wb[key] = self.cvt_weight("wb_" + "_".join(str(x) for x in key), ap, kdim, n, stage)

            for i in range(c["DEPTH"]):
                cv(("ffn_in", i), "ffn_w_in", (i,), D, 2 * FF)
                cv(("ffn_out", i), "ffn_w_out", (i,), FF, D)
            if not self.skip_rwkv:
                for j in range(c["NRW"]):
                    for nm in ("w_r", "w_k", "w_v", "w_o"):
                        cv((nm, j), "rwkv_" + nm, (j,), D, D)
                    for z in range(2):
                        cv(("w1", j, z), "rwkv_w1", (j, z), D, 96)
                        cv(("w2", j, z), "rwkv_w2", (j, z), 96, D)
                        cv(("a1", j, z), "rwkv_a1", (j, z), D, 96)
                        cv(("a2", j, z), "rwkv_a2", (j, z), 96, D)
                    cv(("g1", j), "rwkv_g1", (j,), D, 256)
                    cv(("g2", j), "rwkv_g2", (j,), 256, D)
                    if j > 0:
                        cv(("v1", j), "rwkv_v1", (j - 1,), D, 64)
                        cv(("v2", j), "rwkv_v2", (j - 1,), 64, D)
            if not self.skip_mlstm:
                for j in range(c["NML"]):
                    cv(("m_in", j), "mlstm_w_in", (j,), D, c["PROJ"])
                    cv(("m_out", j), "mlstm_w_out", (j,), D, D)

    def load_bc(self, dst, src_buf, src_ap, q="sp"):
        P = _v(dst).ap.shape[0]
        self.b.dma(dst, View(src_buf, src_ap.partition_broadcast(P)), q=q)

    def mod_tiles(self, i, sub, g, A, Bsh, G, tmp):
        b, c = self.b, self.c
        D = c["D"]
        o = 3 * D * sub
        mr = self.modrows
        self.load_bc(Bsh, mr, mr.h[i, g, o:o + D])
        self.load_bc(tmp, mr, mr.h[i, g, o + D:o + 2 * D])
        self.load_bc(A, self.inp["norm_g"], self.inp["norm_g"].h[i, sub, :])
        b.stt("dve", A, tmp, 1.0, A, ALU.add, ALU.mult)
        if G is not None:
            self.load_bc(G, mr, mr.h[i, g, o + 2 * D:o + 3 * D])

    def norm_tile(self, xt, A, Bsh, hb, junk, hf, st):
        b, D = self.b, self.c["D"]
        b.memset("pool", st[:, 0:1], 0.0)
        b.act(junk, xt, AF.Square, accum=st[:, 0:1])
        b.act(st[:, 1:2], st[:, 0:1], AF.Sqrt, bias=self.eps_t[:, 0:1], scale=1.0 / D)
        b.op("dve", lambda en: en.reciprocal(out=st.h[:, 1:2], in_=st.h[:, 1:2]), w=[st[:, 1:2]], r=[st[:, 1:2]])
        b.stt("dve", hf, xt, st[:, 1:2], A, ALU.mult, ALU.mult)
        if Bsh is None:
            return
        b.tt("dve", hb, hf, Bsh, ALU.add)

    def transpose_tile(self, hb, dstT, tok0, tps, k):
        b, KT = self.b, self.c["KT"]
        for k0 in range(0, KT, 4):
            tp = tps[(k + k0 // 4) % len(tps)]
            for q in range(4):
                b.transpose(tp[:, q, :], hb[:, (k0 + q) * 128:(k0 + q + 1) * 128], self.ident_b)
            eng = "act" if (k0 // 4) % 2 == 0 else "dve"
            b.copy(eng, dstT[:, k0:k0 + 4, tok0:tok0 + 128], tp[:, 0:4, :])

    def ffn(self, i):
        b, c = self.b, self.c
        D, KT, FT, FF = c["D"], c["KT"], c["FT"], c["FF"]
        last = i == c["DEPTH"] - 1
        w_in, w_out = self.wb[("ffn_in", i)], self.wb[("ffn_out", i)]
        JH = FT // 2
        NCW = 512
        with b.scope():
            A = b.sbuf("fA", [128, D], F32)
            Bsh = b.sbuf("fB", [128, D], F32)
            G = b.sbuf("fG", [128, D], F32)
            tmpm = b.sbuf("ftmpm", [128, D], F32)
            xt = [b.sbuf(f"fx{k}", [128, D], F32) for k in range(2)]
            hf = b.sbuf("fhf", [128, D], F32)
            hb = [b.sbuf(f"fhb{k}", [128, D], BF16) for k in range(2)]
            junk = b.sbuf("fjunk", [128, D], BF16)
            st = b.sbuf("fst", [128, 2], F32)
            hT = b.sbuf("fhT", [128, KT, 512], BF16)
            actT = b.sbuf("factT", [128, FT, 512], BF16)
            wg = [b.sbuf(f"fwg{k}", [128, KT, 128], BF16) for k in range(2)]
            wu = [b.sbuf(f"fwu{k}", [128, KT, 128], BF16) for k in range(2)]
            wo = [b.sbuf(f"fwo{k}", [128, JH, NCW], BF16) for k in range(2)]
            sg = [b.sbuf(f"fsg{k}", [128, 512], F32) for k in range(2)]
            ot = [b.sbuf(f"fot{k}", [128, NCW], F32) for k in range(2)]
            tps = [b.psum(f"ftp{k}", [128, 4, 128], BF16) for k in range(2)]
            pg = [b.psum(f"fpg{k}", [128, 512], F32) for k in range(6)]
            cur_g = None
            kx = 0
            for (r0, ntok, mg, is_ctx, bb) in self.groups(512, include_ctx=not last):
                nt = ntok // 128
                if mg != cur_g:
                    self.mod_tiles(i, 1, mg, A, Bsh, G, tmpm)
                    cur_g = mg
                for t in range(nt):
                    x = xt[kx % 2]
                    h = hb[kx % 2]
                    kx += 1
                    b.dma(x, self.xs[r0 + t * 128:r0 + (t + 1) * 128, :])
                    self.norm_tile(x, A, Bsh, h, junk, hf, st)
                    self.transpose_tile(h, hT, t * 128, tps, t)
                for j in range(FT):
                    g_w, u_w = wg[j % 2], wu[j % 2]
                    b.dma(g_w, w_in[:, :, j * 128:(j + 1) * 128])
                    b.dma(u_w, w_in[:, :, FF + j * 128:FF + (j + 1) * 128], q="act")
                    p_g, p_u = pg[(j % 2) * 2], pg[(j % 2) * 2 + 1]
                    b.mm(p_g[:, 0:ntok], [(g_w[:, kt, :], hT[:, kt, 0:ntok]) for kt in range(KT)])
                    b.mm(p_u[:, 0:ntok], [(u_w[:, kt, :], hT[:, kt, 0:ntok]) for kt in range(KT)])
                    s = sg[j % 2]
                    b.act(s[:, 0:ntok], p_g[:, 0:ntok], AF.Silu)
                    b.tt("dve", actT[:, j, 0:ntok], s[:, 0:ntok], p_u[:, 0:ntok], ALU.mult)
                ko = 0
                for n0 in range(0, D, NCW):
                    for half in range(2):
                        w = wo[ko % 2]
                        ko += 1
                        b.dma(w, w_out[:, half * JH:(half + 1) * JH, n0:n0 + NCW])
                        for t in range(nt):
                            b.mm(pg[2 + t], [(actT[:, half * JH + jj, t * 128:(t + 1) * 128], w[:, jj, :])
                                             for jj in range(JH)], start=(half == 0), stop=(half == 1))
                    for t in range(nt):
                        o = ot[t % 2]
                        rows = self.xs[r0 + t * 128:r0 + (t + 1) * 128, n0:n0 + NCW]
                        b.dma(o, rows, q="act")
                        b.tt("dve", sg[t % 2][:, 0:NCW], pg[2 + t], G[:, n0:n0 + NCW], ALU.mult)
                        b.tt("pool", o, o, sg[t % 2][:, 0:NCW], ALU.add)
                        b.dma(rows, o, q="pool")

    def final(self):
        b, c = self.b, self.c
        D = c["D"]
        with b.scope():
            A = b.sbuf("nA", [128, D], F32)
            self.load_bc(A, self.inp["final_g"], self.inp["final_g"].h[:])
            xt = [b.sbuf(f"nx{k}", [128, D], F32) for k in range(2)]
            hf = [b.sbuf(f"nh{k}", [128, D], F32) for k in range(2)]
            junk = b.sbuf("njunk", [128, D], BF16)
            st = b.sbuf("nst", [128, 2], F32)
            k = 0
            for bb in range(c["NB"]):
                for t in range(c["T"] // 128):
                    r0 = bb * c["S"] + c["L"] + t * 128
                    x, h = xt[k % 2], hf[k % 2]
                    k += 1
                    b.dma(x, self.xs[r0:r0 + 128, :])
                    self.norm_tile(x, A, None, None, junk, h, st)
                    b.dma(self.out[bb * c["T"] + t * 128: bb * c["T"] + (t + 1) * 128, :], h, q="pool")

    def build(self):
        b, c = self.b, self.c
        self.declare()
        self.consts()
        for r0 in range(0, c["R"], 512):
            r1 = min(c["R"], r0 + 512)
            b.dma(self.xs[r0:r1, :], self.inp["xs0"][r0:r1, :], q=("sp", "act", "pool")[(r0 // 512) % 3])
        self.phase_mod()
        self.phase_prep()
        for i in range(c["DEPTH"]):
            self.mixer(i)
            self.ffn(i)
        self.final()
        b.barrier()
        return b.nc

    def mixer(self, i):
        if i % 2 == 0:
            if not self.skip_rwkv:
                self.rwkv(i)
        else:
            if not self.skip_mlstm:
                self.mlstm(i)

    def rwkv(self, i):
        c = self.c
        j = i // 2
        if not hasattr(self, "rscr"):
            b = self.b
            R, D = c["R"], c["D"]
            self.rscr = {k: b.dram("rs_" + k, [R, D], F32) for k in
                         ("r", "k", "v", "w0", "w1", "a0", "a1", "g", "y0", "y1", "vf")}
        for bb in range(c["NB"]):
            self.rwkv_proj(i, j, bb)
        if self.chunked:
            self.rwkv_scan_chunked(i, j)
        else:
            self.rwkv_scan(i, j)
        self.rwkv_readout(i, j)

    def rwkv_proj(self, i, j, bb):
        b, c = self.b, self.c
        D, KT, L, T, S = c["D"], c["KT"], c["L"], c["T"], c["S"]
        base = bb * S
        scr = self.rscr
        vdst = scr["vf"] if j == 0 else scr["v"]
        with b.scope():
            hT = b.sbuf("rhT", [128, KT, S], BF16)
            with b.scope():
                A = b.sbuf("rA", [128, D], F32)
                Bsh = b.sbuf("rB", [128, D], F32)
                tmpm = b.sbuf("rtm", [128, D], F32)
                xt = [b.sbuf(f"rx{k}", [128, D], F32) for k in range(2)]
                hf = b.sbuf("rhf", [128, D], F32)
                hb = [b.sbuf(f"rhb{k}", [128, D], BF16) for k in range(2)]
                junk = b.sbuf("rjunk", [128, D], BF16)
                st = b.sbuf("rst", [128, 2], F32)
                tps = [b.psum(f"rtp{k}", [128, 4, 128], BF16) for k in range(2)]
                for (g, t0, n) in ((c["NB"], 0, L), (bb, L, T)):
                    self.mod_tiles(i, 0, g, A, Bsh, None, tmpm)
                    for t in range(n // 128):
                        x, h = xt[t % 2], hb[t % 2]
                        b.dma(x, self.xs[base + t0 + t * 128: base + t0 + (t + 1) * 128, :])
                        self.norm_tile(x, A, Bsh, h, junk, hf, st)
                        self.transpose_tile(h, hT, t0 + t * 128, tps, t)
            with b.scope():
                NM = 6
                mu_rows = b.sbuf("rmur", [NM * KT, 128], F32)
                mu = b.sbuf("rmu", [128, NM * KT], F32)
                omu = b.sbuf("romu", [128, NM * KT], F32)
                b.dma(mu_rows, View(self.inp["rwkv_mu"], self.inp["rwkv_mu"].h[j].rearrange("m (kt p) -> (m kt) p", p=128)))
                pmu = b.psum("rpmu", [128, 128], F32)
                b.transpose(pmu[:, 0:NM * KT], mu_rows, self.ident_f[0:NM * KT, 0:NM * KT])
                b.copy("dve", mu, pmu[:, 0:NM * KT])
                b.ts("dve", omu, mu, -1.0, ALU.mult, 1.0, ALU.add)
                xm = [b.sbuf(f"rxm{k}", [128, KT, 512], BF16) for k in range(2)]
                wch = [b.sbuf(f"rwch{k}", [128, KT, 512], BF16) for k in range(2)]
                brow = b.sbuf("rbrow", [128, D], F32)
                l1w = b.sbuf("rl1w", [128, KT, 256], BF16)
                l2w = b.sbuf("rl2w", [128, 2, D], BF16)
                l1 = b.sbuf("rl1", [128, 2, 512], BF16)
                ev = [b.sbuf(f"rev{k}", [128, 512], F32) for k in range(3)]
                vft = b.sbuf("rvft", [128, 512], F32)
                ps = [b.psum(f"rps{k}", [128, 512], F32) for k in range(4)]
                pl = [b.psum(f"rpl{k}", [128, 512], F32) for k in range(2)]
                state = dict(kx=0, kw=0, kp=0, ke=0)
                blocks = [(0, L, True)] + [(L + t0, min(512, T - t0), False) for t0 in range(0, T, 512)]
                MIX = {"r": 0, "w": 1, "k": 2, "v": 3, "a": 4, "g": 5}
                NEG_EXP_HALF = -float(np.exp(-0.5))

                def build_xm(m, tok0, n, is_ctx):
                    x = xm[state["kx"] % 2]
                    state["kx"] += 1
                    for kt in range(KT):
                        col = m * KT + kt
                        eng = "dve"
                        b.ts(eng, x[:, kt, 0:n], hT[:, kt, tok0:tok0 + n], omu[:, col:col + 1], ALU.mult)
                        muv = mu[:, col:col + 1]
                        if is_ctx:
                            if kt < KT // 2:
                                dst, src = x[:, kt, 1:n], hT[:, kt, tok0:tok0 + n - 1]
                            else:
                                dst, src = x[:, kt, 0:n - 1], hT[:, kt, tok0 + 1:tok0 + n]
                        else:
                            q = kt // (KT // 4)
                            t0 = tok0 - L
                            if q == 0:
                                dst = View(x, x.h[:, kt, 0:n].rearrange("p (r c) -> p r c", c=GRID_W)[:, :, 1:GRID_W])
                                src = View(hT, hT.h[:, kt, tok0:tok0 + n].rearrange("p (r c) -> p r c", c=GRID_W)[:, :, 0:GRID_W - 1])
                            elif q == 1:
                                dst = View(x, x.h[:, kt, 0:n].rearrange("p (r c) -> p r c", c=GRID_W)[:, :, 0:GRID_W - 1])
                                src = View(hT, hT.h[:, kt, tok0:tok0 + n].rearrange("p (r c) -> p r c", c=GRID_W)[:, :, 1:GRID_W])
                            elif q == 2:
                                lo = GRID_W if t0 == 0 else 0
                                dst, src = x[:, kt, lo:n], hT[:, kt, tok0 + lo - GRID_W:tok0 + n - GRID_W]
                            else:
                                hi = n - GRID_W if t0 + n == T else n
                                dst, src = x[:, kt, 0:hi], hT[:, kt, tok0 + GRID_W:tok0 + hi + GRID_W]
                        b.stt("dve", dst, src, muv, dst, ALU.mult, ALU.add)
                    return x

                def load_w(key, n0):
                    w = wch[state["kw"] % 2]
                    state["kw"] += 1
                    b.dma(w, self.wb[key][:, :, n0:n0 + 512])
                    return w

                def nextps():
                    p = ps[state["kp"] % 4]
                    state["kp"] += 1
                    return p

                def nextev():
                    e = ev[state["ke"] % 3]
                    state["ke"] += 1
                    return e

                def rows(tok0, t):
                    return slice(base + tok0 + t * 128, base + tok0 + (t + 1) * 128)

                def lora_hidden(x, n, key, width, func):
                    b.dma(l1w[:, :, 0:width], self.wb[key])
                    for mt in range((width + 127) // 128):
                        wd = min(128, width - mt * 128)
                        p = pl[mt % 2]
                        b.mm(p[0:wd, 0:n], [(l1w[:, kt, mt * 128:mt * 128 + wd], x[:, kt, 0:n]) for kt in range(KT)])
                        if func is None:
                            b.copy("act", l1[0:wd, mt, 0:n], p[0:wd, 0:n])
                        else:
                            b.act(l1[0:wd, mt, 0:n], p[0:wd, 0:n], func)

                def lora_out(t, n0, width):
                    p = nextps()
                    nmt = (width + 127) // 128
                    b.mm(p, [(l1[0:min(128, width - mt * 128), mt, t * 128:(t + 1) * 128],
                              l2w[0:min(128, width - mt * 128), mt, n0:n0 + 512]) for mt in range(nmt)])
                    return p

                def load_l2(key, width):
                    src = self.wb[key]
                    nmt = (width + 127) // 128
                    pk = min(width, 128)
                    b.dma(l2w[0:pk, 0:nmt, :], src)

                for (tok0, n, is_ctx) in blocks:
                    nt = n // 128
                    for nm, key, dst in (("r", ("w_r", j), scr["r"]), ("k", ("w_k", j), scr["k"]), ("v", ("w_v", j), vdst)):
                        x = build_xm(MIX[nm], tok0, n, is_ctx)
                        vres = (nm == "v" and j > 0)
                        if vres:
                            lora_hidden(x, n, ("v1", j), 64, None)
                            load_l2(("v2", j), 64)
                            self.load_bc(brow, self.inp["rwkv_v0"], self.inp["rwkv_v0"].h[j - 1, :])
                        for n0 in range(0, D, 512):
                            w = load_w(key, n0)
                            for t in range(nt):
                                p = nextps()
                                b.mm(p, [(x[:, kt, t * 128:(t + 1) * 128], w[:, kt, :]) for kt in range(KT)])
                                e = nextev()
                                b.copy("act", e, p)
                                if vres:
                                    p2 = lora_out(t, n0, 64)
                                    e2 = nextev()
                                    b.tt("dve", e2, p2, brow[:, n0:n0 + 512], ALU.add)
                                    b.act(e2, e2, AF.Sigmoid)
                                    b.dma(vft, scr["vf"][rows(tok0, t), n0:n0 + 512], q="act")
                                    b.tt("dve", vft, vft, e, ALU.subtract)
                                    b.tt("dve", vft, vft, e2, ALU.mult)
                                    b.tt("dve", e, e, vft, ALU.add)
                                b.dma(dst[rows(tok0, t), n0:n0 + 512], e, q="pool")
                    for nm in ("w", "a"):
                        x = build_xm(MIX[nm], tok0, n, is_ctx)
                        for z in range(2):
                            lora_hidden(x, n, (nm + "1", j, z), 96, AF.Tanh if nm == "w" else None)
                            load_l2((nm + "2", j, z), 96)
                            src0 = self.inp["rwkv_w0" if nm == "w" else "rwkv_a0"]
                            self.load_bc(brow, src0, src0.h[j, z, :])
                            for n0 in range(0, D, 512):
                                for t in range(nt):
                                    p2 = lora_out(t, n0, 96)
                                    e = nextev()
                                    b.tt("dve", e, p2, brow[:, n0:n0 + 512], ALU.add)
                                    b.act(e, e, AF.Sigmoid)
                                    if nm == "w":
                                        if self.chunked:
                                            b.ts("dve", e, e, NEG_EXP_HALF, ALU.mult)
                                        else:
                                            b.act(e, e, AF.Exp, scale=NEG_EXP_HALF)
                                    b.dma(scr[nm + str(z)][rows(tok0, t), n0:n0 + 512], e, q="pool")
                    x = build_xm(MIX["g"], tok0, n, is_ctx)
                    lora_hidden(x, n, ("g1", j), 256, AF.Sigmoid)
                    load_l2(("g2", j), 256)
                    for n0 in range(0, D, 512):
                        for t in range(nt):
                            p2 = lora_out(t, n0, 256)
                            e = nextev()
                            b.copy("act", e, p2)
                            b.dma(scr["g"][rows(tok0, t), n0:n0 + 512], e, q="pool")

    def lane_ap(self, buf, bb, tok_first, step, nt):
        c = self.c
        D = c["D"]
        h = buf.h
        off = h.offset + (bb * c["S"] + tok_first) * D
        return View(buf, bass.AP(h.tensor, off, [[64, c["RH"]], [step * D, nt], [1, 64]]))

    def rwkv_scan(self, i, j):
        b, c = self.b, self.c
        NB, RH, L, T, S = c["NB"], c["RH"], c["L"], c["T"], c["S"]
        NL = 2 * NB * RH
        TC = 32
        scr = self.rscr
        vsrc = scr["vf"] if j == 0 else scr["v"]
        with b.scope():
            St = b.sbuf("sS", [NL, 64, 64], F32)
            tmp = b.sbuf("stmp", [NL, 64, 64], F32)
            vk = [b.sbuf(f"svk{k}", [NL, 64, 64], F32) for k in range(2)]
            sz = b.sbuf("ssz", [NL, 64], F32)
            inb = [{nm: b.sbuf(f"s{nm}{k}", [NL, TC, 64], F32) for nm in ("R", "K", "V", "W", "A")} for k in range(2)]
            Zb = b.sbuf("sZ", [NL, TC, 64], F32)
            T2 = b.sbuf("sT2", [NL, TC, 64], F32)
            yb = [b.sbuf(f"sy{k}", [NL, TC, 64], F32) for k in range(2)]
            ssq = b.sbuf("sssq", [NL, TC], F32)
            kk_t = b.sbuf("skk", [NL, 64], F32)
            ka_t = b.sbuf("ska", [NL, 64], F32)
            oka_t = b.sbuf("soka", [NL, 64], F32)
            for z in range(2):
                for bb in range(NB):
                    p0 = (z * NB + bb) * RH
                    b.dma(kk_t[p0:p0 + RH, :], View(self.inp["rwkv_k_k"], self.inp["rwkv_k_k"].h[j].rearrange("(h n) -> h n", n=64)))
                    b.dma(ka_t[p0:p0 + RH, :], View(self.inp["rwkv_k_a"], self.inp["rwkv_k_a"].h[j].rearrange("(h n) -> h n", n=64)))
            b.ts("dve", oka_t, ka_t, -1.0, ALU.mult, 1.0, ALU.add)
            b.memset("dve", St, 0.0)
            shp3 = [NL, 64, 64]
            shpc = [NL, TC, 64]
            nch = S // TC

            def chunk_src(ci, z):
                s0 = ci * TC
                if z == 0:
                    return s0, 1
                if s0 < L:
                    return L - 1 - s0, -1
                return L + T - 1 - (s0 - L), -1

            def load_chunk(ci):
                bufs = inb[ci % 2]
                for z in range(2):
                    tf, step = chunk_src(ci, z)
                    for bb in range(NB):
                        p0 = (z * NB + bb) * RH
                        for nm, src in (("R", scr["r"]), ("K", scr["k"]), ("V", vsrc), ("W", scr[f"w{z}"]), ("A", scr[f"a{z}"])):
                            b.dma(bufs[nm][p0:p0 + RH, :, :], self.lane_ap(src, bb, tf, step, TC))

            load_chunk(0)
            kv = 0
            for ci in range(nch):
                if ci + 1 < nch:
                    load_chunk(ci + 1)
                bufs = inb[ci % 2]
                Rb, Kb, Vb, Wb, Ab = (bufs[nm] for nm in ("R", "K", "V", "W", "A"))
                y = yb[ci % 2]
                b.tt("dve", Zb, Kb, kk_t.v.us(1).bc(shpc), ALU.mult)
                b.tt("dve", T2, Zb, Zb, ALU.mult)
                b.red(ssq, T2)
                b.act(ssq, ssq, AF.Sqrt)
                b.ts("dve", ssq, ssq, 1e-12, ALU.max)
                b.op("dve", lambda en: en.reciprocal(out=ssq.h[:], in_=ssq.h[:]), w=[ssq], r=[ssq])
                b.ts("dve", ssq, ssq, -1.0, ALU.mult)
                b.tt("dve", Zb, Zb, ssq.v.us(2).bc(shpc), ALU.mult)
                b.tt("dve", T2, Ab, ka_t.v.us(1).bc(shpc), ALU.mult)
                b.tt("dve", T2, T2, oka_t.v.us(1).bc(shpc), ALU.add)
                b.tt("dve", Kb, Kb, T2, ALU.mult)
                b.stt("dve", Ab, Zb, -1.0, Ab, ALU.mult, ALU.mult)
                for t in range(TC):
                    zt = Zb[:, t, :].us(1).bc(shp3)
                    wt = Wb[:, t, :].us(1).bc(shp3)
                    bt = Ab[:, t, :].us(1).bc(shp3)
                    kt_ = Kb[:, t, :].us(1).bc(shp3)
                    rt = Rb[:, t, :].us(1).bc(shp3)
                    vt = Vb[:, t, :].us(2).bc(shp3)
                    vkb = vk[kv % 2]
                    kv += 1
                    b.tt("pool", vkb, vt, kt_, ALU.mult)
                    b.tt("dve", tmp, St, zt, ALU.mult)
                    b.red(sz, tmp)
                    b.tt("dve", St, St, wt, ALU.mult)
                    b.tt("dve", tmp, sz.v.us(2).bc(shp3), bt, ALU.mult)
                    b.tt("dve", St, St, tmp, ALU.add)
                    b.tt("dve", St, St, vkb, ALU.add)
                    b.tt("dve", tmp, St, rt, ALU.mult)
                    b.red(y[:, t, :], tmp)
                for z in range(2):
                    tf, step = chunk_src(ci, z)
                    for bb in range(NB):
                        p0 = (z * NB + bb) * RH
                        b.dma(self.lane_ap(scr[f"y{z}"], bb, tf, step, TC), y[p0:p0 + RH, :, :], q="act")

    def rwkv_scan_chunked(self, i, j):
        b, c = self.b, self.c
        NB, RH, L, T, S, D = c["NB"], c["RH"], c["L"], c["T"], c["S"], c["D"]
        C = CHUNK
        HH = min(8, RH)
        HW = HH * 64
        NL = 2 * HH
        GL = 8
        NGR = NL // GL
        scr = self.rscr
        vsrc = scr["vf"] if j == 0 else scr["v"]
        nch = S // C
        with b.scope():
            cm = b.sbuf("kcm", [64, 7, 64], F32)
            b.dma(cm, self.inp["rconst"])
            cm2 = b.sbuf("kcm2", [64, 2, 128], F32)
            b.dma(cm2, self.inp["rconst2"])
            idb = b.sbuf("kidb", [64, 64], BF16)
            b.copy("dve", idb, cm[:, 6, :])
            onec = b.sbuf("kone", [64, 1], F32)
            b.memset("dve", onec, 1.0)
            kkB = b.sbuf("kkkB", [64, D], F32)
            kaB = b.sbuf("kkaB", [64, D], F32)
            okaB = b.sbuf("kokaB", [64, D], F32)
            self.load_bc(kkB, self.inp["rwkv_k_k"], self.inp["rwkv_k_k"].h[j, :])
            self.load_bc(kaB, self.inp["rwkv_k_a"], self.inp["rwkv_k_a"].h[j, :])
            b.ts("dve", okaB, kaB, -1.0, ALU.mult, 1.0, ALU.add)
            ST = b.sbuf("kST", [64, NL, 64], F32)
            STb = b.sbuf("kSTb", [64, NL, 64], BF16)
            names = ("R", "K", "V", "A", "W")
            ld = [{nm: b.sbuf(f"k{nm}{k}", [64, 2, HW], F32) for nm in names} for k in range(2)]
            T1 = b.sbuf("kT1", [64, 2, HW], F32)
            T2 = b.sbuf("kT2", [64, 2, HW], F32)
            ssq = b.sbuf("kssq", [64, 2 * HH], F32)
            tm = [{nm: b.sbuf(f"k{nm}b{k}", [64, 2, HW], BF16) for nm in ("Zt", "Rt", "Bt", "Kt", "Vb", "Bh", "Kh")}
                  for k in range(2)]
            FMs = [b.sbuf(f"kFM{k}", [64, NL, 4, 64], BF16) for k in range(2)]
            WcCs = [b.sbuf(f"kWcC{k}", [64, NL], F32) for k in range(2)]
            yts = [b.sbuf(f"kyt{k}", [64, 2, HW], F32) for k in range(2)]
            sets = []
            for k in range(2):
                sets.append(dict(
                    X=b.psum(f"kX{k}", [64, GL, 128], F32), Y=b.psum(f"kY{k}", [64, GL, 64], F32),
                    PX=[b.sbuf(f"kPX{k}{q}", [64, GL, 128], F32) for q in range(2)],
                    Q=[b.sbuf(f"kQ{k}{q}", [64, GL, 64], F32) for q in range(2)],
                    SAbr=b.sbuf(f"kSAbr{k}", [64, GL, 64], BF16), SAk=b.sbuf(f"kSAk{k}", [64, GL, 128], BF16),
                    WT=b.sbuf(f"kWT{k}", [64, GL, 64], F32), UT=b.sbuf(f"kUT{k}", [64, GL, 64], BF16)))
            pt = b.psum("kpt", [64, 512], F32)
            ptp = b.psum("kptp", [64, 16, 64], BF16)
            shp = [64, 2, HW]
            lshp = [64, GL, 64]

            NCc, NCl = L // C, T // C

            def chunk_src(cs, z):
                if z == 0:
                    cn = cs
                else:
                    cn = (NCc - 1 - cs) if cs < NCc else (NCc + NCl - 1 - (cs - NCc))
                return cn * C, 1

            def dram_rows(buf, bb, tf, step, c0):
                return buf[bb * S + tf: bb * S + tf + C, c0:c0 + HW]

            for bb in range(NB):
                for hq in range(RH // HH):
                    c0 = hq * HW
                    b.memset("dve", ST, 0.0)
                    b.memset("pool", STb, 0.0)
                    kkb = kkB[:, c0:c0 + HW].us(1).bc(shp)
                    kab = kaB[:, c0:c0 + HW].us(1).bc(shp)
                    okab = okaB[:, c0:c0 + HW].us(1).bc(shp)

                    def load(cs):
                        bufs = ld[cs % 2]
                        for z in range(2):
                            tf, step = chunk_src(cs, z)
                            for nm, src in (("R", scr["r"]), ("K", scr["k"]), ("V", vsrc), ("A", scr[f"a{z}"]), ("W", scr[f"w{z}"])):
                                b.dma(bufs[nm][:, z, :], dram_rows(src, bb, tf, step, c0), q="sp" if nm in ("R", "K", "V") else "act")

                    def tri(kind, src, outs):
                        for z in range(2):
                            for n0 in range(0, HW, 512):
                                n = min(512, HW - n0)
                                b.mm(pt[:, 0:n], [(cm[:, kind * 2 + z, :], src[:, z, n0:n0 + n])])
                                for dst, sc in outs:
                                    b.act(dst[:, z, n0:n0 + n], pt[:, 0:n], AF.Exp, scale=sc)

                    def prep(cs):
                        bufs = ld[cs % 2]
                        Rl, Kl, Vl, Al, Wl = (bufs[nm] for nm in names)
                        o = tm[cs % 2]
                        FM, WcC = FMs[cs % 2], WcCs[cs % 2]
                        h3 = lambda t_: View(t_, t_.h[:].rearrange("p z (h n) -> p (z h) n", n=64))
                        b.tt("pool", T1, Kl, kkb, ALU.mult)
                        b.tt("pool", T2, T1, T1, ALU.mult)
                        b.red(ssq, h3(T2))
                        b.act(ssq, ssq, AF.Sqrt)
                        b.ts("dve", ssq, ssq, 1e-12, ALU.max)
                        b.op("dve", lambda en: en.reciprocal(out=ssq.h[:], in_=ssq.h[:]), w=[ssq], r=[ssq])
                        b.tt("dve", h3(T1), h3(T1), ssq.v.us(2).bc([64, 2 * HH, 64]), ALU.mult)
                        tri(0, Wl, [(T2, 1.0)])
                        b.stt("dve", o["Zt"], T1, -1.0, T2, ALU.mult, ALU.mult)
                        b.tt("dve", T1, T1, Al, ALU.mult)
                        b.tt("pool", Al, Al, kab, ALU.mult)
                        b.tt("pool", Al, Al, okab, ALU.add)
                        b.tt("pool", Kl, Kl, Al, ALU.mult)
                        tri(1, Wl, [(Al, 1.0), (T2, -1.0)])
                        b.tt("dve", o["Rt"], Rl, Al, ALU.mult)
                        b.tt("dve", o["Bt"], T1, T2, ALU.mult)
                        b.tt("pool", o["Kt"], Kl, T2, ALU.mult)
                        tri(2, Wl, [(Al, 1.0)])
                        b.tt("dve", o["Bh"], T1, Al, ALU.mult)
                        b.tt("pool", o["Kh"], Kl, Al, ALU.mult)
                        b.copy("pool", o["Vb"], Vl)
                        for z in range(2):
                            for hh in range(HH):
                                ln = hh * 2 + z
                                b.mm(pt[:, ln:ln + 1], [(Wl[:, z, hh * 64:(hh + 1) * 64], onec)])
                        b.act(WcC, pt[:, 0:NL], AF.Exp)
                        ke = 0
                        for xi, nm in enumerate(("Zt", "Rt", "Bt", "Kt")):
                            for h0 in range(0, HH, 8):
                                nh = min(8, HH - h0)
                                for hh in range(nh):
                                    for z in range(2):
                                        b.transpose(ptp[:, hh * 2 + z, :], o[nm][:, z, (h0 + hh) * 64:(h0 + hh + 1) * 64], idb)
                                b.copy(("act", "dve")[ke % 2], FM[:, h0 * 2:(h0 + nh) * 2, xi, :], ptp[:, 0:nh * 2, :])
                                ke += 1

                    def lanes(cs):
                        o = tm[cs % 2]
                        FM, WcC = FMs[cs % 2], WcCs[cs % 2]
                        yt = yts[cs % 2]
                        y5 = yt.h[:].rearrange("p z (h v) -> p h z v", v=64)
                        tok = lambda nm, l: o[nm][:, l % 2, (l // 2) * 64:(l // 2 + 1) * 64]
                        for g in range(NGR):
                            s_ = sets[g % 2]
                            X, Y, PX, Q, SAbr, SAk, WT, UT = (s_[k_] for k_ in ("X", "Y", "PX", "Q", "SAbr", "SAk", "WT", "UT"))
                            l0 = g * GL
                            fm = lambda l, xi: FM[:, l0 + l, xi, :]
                            zr = lambda l: View(FM, FM.h[:, l0 + l, 0:2, :].rearrange("p a t -> p (a t)"))
                            lz = lambda v_: View(v_.buf, v_.ap.rearrange("p (h z) t -> p h z t", z=2))
                            mk = lambda k_: cm[:, 2 * k_:2 * k_ + 2, :].us(1).bc([64, GL // 2, 2, 64])
                            m2 = cm2.v.us(1).bc([64, GL // 2, 2, 128])
                            for l in range(GL):
                                b.mm(X[:, l, :], [(fm(l, 2), zr(l))])
                            b.tt("dve", lz(PX[0][:, :, 0:64]), lz(X[:, :, 0:64]), mk(0), ALU.mult)
                            b.tt("dve", lz(SAbr.v), lz(X[:, :, 64:128]), mk(1), ALU.mult)
                            b.copy("pool", PX[0][:, :, 64:128], cm[:, 6, :].us(1).bc(lshp))
                            for l in range(GL):
                                b.mm(X[:, l, :], [(fm(l, 3), zr(l))])
                            b.tt("dve", lz(SAk.v), lz(X.v), m2, ALU.mult)
                            for l in range(GL):
                                b.mm(Y[:, l, :], [(fm(l, 0), fm(l, 2))])
                            b.tt("dve", lz(Q[0].v), lz(Y.v), mk(2), ALU.mult)
                            for k in range(6):
                                pc, pn = PX[k % 2], PX[(k + 1) % 2]
                                qc, qn = Q[k % 2], Q[(k + 1) % 2]
                                lastk = k == 5
                                for l in range(GL):
                                    if lastk:
                                        b.mm(X[:, l, 64:128], [(qc[:, l, :], pc[:, l, 64:128])])
                                    else:
                                        b.mm(X[:, l, :], [(qc[:, l, :], pc[:, l, :])])
                                if not lastk:
                                    for l in range(GL):
                                        b.mm(Y[:, l, :], [(pc[:, l, 0:64], qc[:, l, :])])
                                    b.copy("act", pn[:, :, 0:64], X[:, :, 0:64])
                                    b.copy("act", qn, Y)
                                b.tt("dve", pn[:, :, 64:128], pc[:, :, 64:128], X[:, :, 64:128], ALU.add)
                            Tm = PX[0]
                            for l in range(GL):
                                b.mm(Y[:, l, :], [(fm(l, 0), STb[:, l0 + l, :]), (SAk[:, l, 0:64], tok("Vb", l0 + l))])
                            b.copy("act", WT, Y)
                            for l in range(GL):
                                b.mm(Y[:, l, :], [(Tm[:, l, 64:128], WT[:, l, :])])
                            b.copy("dve", UT, Y)
                            for l in range(GL):
                                b.mm(Y[:, l, :], [(fm(l, 1), STb[:, l0 + l, :]), (SAbr[:, l, :], UT[:, l, :]),
                                                  (SAk[:, l, 64:128], tok("Vb", l0 + l))])
                            hh0 = l0 // 2
                            b.copy("act", View(yt, y5[:, hh0:hh0 + GL // 2, :, :]),
                                   View(Y, Y.h[:].rearrange("p (h z) v -> p h z v", z=2)))
                            for l in range(GL):
                                b.mm(Y[:, l, :], [(tok("Bh", l0 + l), UT[:, l, :]), (tok("Kh", l0 + l), tok("Vb", l0 + l))])
                            b.tt("pool", ST[:, l0:l0 + GL, :], ST[:, l0:l0 + GL, :], WcC[:, l0:l0 + GL].us(2).bc(lshp), ALU.mult)
                            b.tt("dve", ST[:, l0:l0 + GL, :], ST[:, l0:l0 + GL, :], Y, ALU.add)
                            b.copy("act", STb[:, l0:l0 + GL, :], ST[:, l0:l0 + GL, :])
                        for z in range(2):
                            tf, step = chunk_src(cs, z)
                            b.dma(dram_rows(scr[f"y{z}"], bb, tf, step, c0), yt[:, z, :], q="pool")

                    load(0)
                    prep(0)
                    for cs in range(nch):
                        if cs + 1 < nch:
                            load(cs + 1)
                            prep(cs + 1)
                        lanes(cs)

    def rwkv_readout(self, i, j):
        b, c = self.b, self.c
        D, KT, RH = c["D"], c["KT"], c["RH"]
        last = i == c["DEPTH"] - 1
        scr = self.rscr
        vsrc = scr["vf"] if j == 0 else scr["v"]
        shp = [128, RH, 64]
        with b.scope():
            names = ("y0", "y1", "r", "k", "v", "g", "a0", "a1")
            ld = {nm: b.sbuf("q" + nm, [128, D], F32) for nm in names}
            srcs = dict(y0=scr["y0"], y1=scr["y1"], r=scr["r"], k=scr["k"], v=vsrc, g=scr["g"], a0=scr["a0"], a1=scr["a1"])
            lnw = b.sbuf("qlnw", [128, D], F32)
            lnb = b.sbuf("qlnb", [128, D], F32)
            kab = b.sbuf("qka", [128, D], F32)
            ka2 = b.sbuf("qka2", [128, D], F32)
            rkb = b.sbuf("qrk", [128, D], F32)
            G = b.sbuf("qG", [128, D], F32)
            self.load_bc(lnw, self.inp["rwkv_ln_w"], self.inp["rwkv_ln_w"].h[j, :])
            self.load_bc(lnb, self.inp["rwkv_ln_b"], self.inp["rwkv_ln_b"].h[j, :])
            self.load_bc(kab, self.inp["rwkv_k_a"], self.inp["rwkv_k_a"].h[j, :])
            self.load_bc(rkb, self.inp["rwkv_r_k"], self.inp["rwkv_r_k"].h[j].rearrange("h n -> (h n)"))
            b.ts("dve", ka2, kab, -2.0, ALU.mult, 2.0, ALU.add)
            st1 = b.sbuf("qst1", [128, RH], F32)
            st2 = b.sbuf("qst2", [128, RH], F32)
            gneps = b.sbuf("qeps", [128, 1], F32)
            b.memset("dve", gneps, 64 * 1e-5)
            ob = b.sbuf("qob", [128, D], BF16)
            oT = b.sbuf("qoT", [128, KT, 512], BF16)
            wch = [b.sbuf(f"qw{k}", [128, KT, 512], BF16) for k in range(2)]
            ot = [b.sbuf(f"qot{k}", [128, 512], F32) for k in range(2)]
            og = [b.sbuf(f"qog{k}", [128, 512], F32) for k in range(2)]
            tps = [b.psum(f"qtp{k}", [128, 4, 128], BF16) for k in range(2)]
            ps = [b.psum(f"qps{k}", [128, 512], F32) for k in range(4)]
            cur_g = None
            kw = 0
            kp = 0
            v3 = lambda t_: View(t_, t_.h[:].rearrange("p (h n) -> p h n", n=64))
            for (r0, ntok, mg, is_ctx, bb) in self.groups(512, include_ctx=not last):
                nt = ntok // 128
                if mg != cur_g:
                    mr = self.modrows
                    self.load_bc(G, mr, mr.h[i, mg, 2 * D:3 * D])
                    cur_g = mg
                for t in range(nt):
                    rr = slice(r0 + t * 128, r0 + (t + 1) * 128)
                    for nm in names:
                        b.dma(ld[nm], srcs[nm][rr, :], q="sp" if nm in ("y0", "r", "v", "a0") else "act")
                    y0, y1, r_, k_, v_, g_, a0, a1 = (ld[nm] for nm in names)
                    b.tt("dve", y0, y0, y1, ALU.add)
                    b.red(st1, v3(y0))
                    b.ts("dve", st1, st1, 1.0 / 64, ALU.mult)
                    b.tt("dve", v3(y0), v3(y0), st1.v.us(2).bc(shp), ALU.subtract)
                    b.tt("pool", y1, y0, y0, ALU.mult)
                    b.red(st2, v3(y1))
                    b.act(st2, st2, AF.Sqrt, bias=gneps[:, 0:1], scale=1.0 / 64)
                    b.op("dve", lambda en: en.reciprocal(out=st2.h[:], in_=st2.h[:]), w=[st2], r=[st2])
                    b.tt("dve", v3(y0), v3(y0), st2.v.us(2).bc(shp), ALU.mult)
                    b.tt("dve", y0, y0, lnw, ALU.mult)
                    b.tt("dve", y0, y0, lnb, ALU.add)
                    b.tt("pool", a0, a0, a1, ALU.add)
                    b.tt("pool", a0, a0, kab, ALU.mult)
                    b.tt("pool", a0, a0, ka2, ALU.add)
                    b.tt("pool", k_, k_, a0, ALU.mult)
                    b.tt("pool", r_, r_, k_, ALU.mult)
                    b.tt("dve", r_, r_, rkb, ALU.mult)
                    b.red(st1, v3(r_))
                    b.tt("dve", v3(v_), v3(v_), st1.v.us(2).bc(shp), ALU.mult)
                    b.tt("dve", y0, y0, v_, ALU.add)
                    b.tt("dve", ob, y0, g_, ALU.mult)
                    self.transpose_tile(ob, oT, t * 128, tps, t)
                for n0 in range(0, D, 512):
                    w = wch[kw % 2]
                    kw += 1
                    b.dma(w, self.wb[("w_o", j)][:, :, n0:n0 + 512])
                    for t in range(nt):
                        p = ps[kp % 4]
                        kp += 1
                        b.mm(p, [(oT[:, kt, t * 128:(t + 1) * 128], w[:, kt, :]) for kt in range(KT)])
                        o, gg = ot[t % 2], og[t % 2]
                        rows = self.xs[r0 + t * 128:r0 + (t + 1) * 128, n0:n0 + 512]
                        b.dma(o, rows, q="act")
                        b.tt("dve", gg, p, G[:, n0:n0 + 512], ALU.mult)
                        b.tt("pool", o, o, gg, ALU.add)
                        b.dma(rows, o, q="pool")

    def build_hT(self, i, sub, bb, hT):
        b, c = self.b, self.c
        D, L, T, S = c["D"], c["L"], c["T"], c["S"]
        base = bb * S
        with b.scope():
            A = b.sbuf("hA", [128, D], F32)
            Bsh = b.sbuf("hB", [128, D], F32)
            xt = [b.sbuf(f"hx{k}", [128, D], F32) for k in range(2)]
            hf = b.sbuf("hhf", [128, D], F32)
            tmpm = hf
            hb = [b.sbuf(f"hhb{k}", [128, D], BF16) for k in range(2)]
            st = b.sbuf("hst", [128, 2], F32)
            tps = [b.psum(f"htp{k}", [128, 4, 128], BF16) for k in range(2)]
            for (g, t0, n) in ((c["NB"], 0, L), (bb, L, T)):
                self.mod_tiles(i, sub, g, A, Bsh, None, tmpm)
                for t in range(n // 128):
                    x, h = xt[t % 2], hb[t % 2]
                    b.dma(x, self.xs[base + t0 + t * 128: base + t0 + (t + 1) * 128, :])
                    self.norm_tile(x, A, Bsh, h, h, hf, st)
                    self.transpose_tile(h, hT, t0 + t * 128, tps, t)

    def mlstm(self, i):
        c = self.c
        j = i // 2
        if not hasattr(self, "mscr"):
            b = self.b
            R, D = c["R"], c["D"]
            self.mscr = dict(v=b.dram("ms_v", [R, D], BF16), o=b.dram("ms_o", [R, D], F32),
                             h0=b.dram("ms_h0", [R, D], F32), h1=b.dram("ms_h1", [R, D], F32),
                             dec=b.dram("ms_dec", [(32 + c["MH"]) * (c["S"] // CHUNK)], F32))
        for bb in range(c["NB"]):
            self.mlstm_seq(i, j, bb)
        self.mlstm_readout(i, j)

    def mlstm_seq(self, i, j, bb):
        b, c = self.b, self.c
        D, KT, L, T, S, MH, DK, DV, QK = c["D"], c["KT"], c["L"], c["T"], c["S"], c["MH"], c["DK"], c["DV"], c["QK"]
        assert DK == 128
        base = bb * S
        scr = self.mscr
        w_in = self.wb[("m_in", j)]
        NL = 2 * MH
        NLP = 32 + MH
        NC = S // CHUNK
        NCc = L // CHUNK
        NCl = T // CHUNK
        NR = T // GRID_W
        G0 = 2 * QK + 2 * D
        NG = 4 * MH
        with b.scope():
            qT = b.sbuf("mqT", [128, MH, S], BF16)
            kT = b.sbuf("mkT", [128, MH, S], BF16)
            GTall = b.sbuf("mGT", [NG, S], F32)
            with b.scope():
                hT = b.sbuf("mhT", [128, KT, S], BF16)
                self.build_hT(i, 0, bb, hT)
                NCT = 2 * QK // 128
                cw = b.sbuf("mcw", [128, NCT, 10], F32)
                with b.scope():
                    crow = b.sbuf("mcrow", [10, 2 * QK], F32)
                    b.dma(crow[0:9, :], View(self.inp["mlstm_conv_w"], self.inp["mlstm_conv_w"].h[j].rearrange("a b c -> (a b) c")))
                    b.dma(crow[9:10, :], View(self.inp["mlstm_conv_b"], self.inp["mlstm_conv_b"].h[j:j + 1, :]))
                    pcw = b.psum("mpcw", [128, 512], F32)
                    for ct in range(NCT):
                        b.transpose(pcw[:, 0:10], crow[:, ct * 128:(ct + 1) * 128], self.ident_f[0:10, 0:10])
                        b.copy("dve", cw[:, ct, :], pcw[:, 0:10])
                pre = b.sbuf("mpre", [128, S], F32)
                acc = b.sbuf("macc", [128, S], F32)
                wq = [b.sbuf(f"mwq{k}", [128, KT, 128], BF16) for k in range(2)]
                pp = [b.psum(f"mpp{k}", [128, 512], F32) for k in range(2)]
                kp = 0
                for ct in range(NCT):
                    w = wq[ct % 2]
                    b.dma(w, w_in[:, :, ct * 128:(ct + 1) * 128])
                    for t0 in range(0, S, 512):
                        n = min(512, S - t0)
                        p = pp[kp % 2]
                        kp += 1
                        b.mm(p[:, 0:n], [(w[:, kt, :], hT[:, kt, t0:t0 + n]) for kt in range(KT)])
                        b.copy("act", pre[:, t0:t0 + n], p[:, 0:n])
                    wv = lambda dy, dx: cw[:, ct, dy * 3 + dx: dy * 3 + dx + 1]
                    b.ts("dve", acc, pre, wv(1, 1), ALU.mult, cw[:, ct, 9:10], ALU.add)
                    b.stt("dve", acc[:, 1:L], pre[:, 0:L - 1], wv(1, 0), acc[:, 1:L], ALU.mult, ALU.add)
                    b.stt("dve", acc[:, 0:L - 1], pre[:, 1:L], wv(1, 2), acc[:, 0:L - 1], ALU.mult, ALU.add)
                    a3 = lambda t_: t_.h[:, L:S].rearrange("p (r c) -> p r c", c=GRID_W)
                    for dy in range(3):
                        for dx in range(3):
                            if dy == 1 and dx == 1:
                                continue
                            r_lo, r_hi = max(0, 1 - dy), min(NR, NR + 1 - dy)
                            c_lo, c_hi = max(0, 1 - dx), min(GRID_W, GRID_W + 1 - dx)
                            dst = View(acc, a3(acc)[:, r_lo:r_hi, c_lo:c_hi])
                            src = View(pre, a3(pre)[:, r_lo + dy - 1:r_hi + dy - 1, c_lo + dx - 1:c_hi + dx - 1])
                            b.stt("dve", dst, src, wv(dy, dx), dst, ALU.mult, ALU.add)
                    if ct < NCT // 2:
                        b.act(pre, acc, AF.Silu)
                        b.ts("pool", qT[:, ct, :], pre, float(DK) ** -0.5, ALU.mult)
                    else:
                        b.act(kT[:, ct - NCT // 2, :], acc, AF.Silu)
                NCW = 256
                wch = [b.sbuf(f"mwch{k}", [128, KT, NCW], BF16) for k in range(2)]
                evb = [b.sbuf(f"mevb{k}", [128, NCW], BF16) for k in range(2)]
                evf = [b.sbuf(f"mevf{k}", [128, NCW], F32) for k in range(2)]
                pv = [b.psum(f"mpv{k}", [128, 512], F32) for k in range(3)]
                kw = 0
                kq = 0
                for which in ("v", "o"):
                    c0 = 2 * QK + (0 if which == "v" else D)
                    for n0 in range(0, D, NCW):
                        w = wch[kw % 2]
                        kw += 1
                        b.dma(w, w_in[:, :, c0 + n0:c0 + n0 + NCW])
                        for t in range(S // 128):
                            p = pv[kq % 3]
                            b.mm(p[:, 0:NCW], [(hT[:, kt, t * 128:(t + 1) * 128], w[:, kt, :]) for kt in range(KT)])
                            rows = slice(base + t * 128, base + (t + 1) * 128)
                            if which == "v":
                                e = evb[kq % 2]
                                b.copy("act", e, p[:, 0:NCW])
                                b.dma(scr["v"][rows, n0:n0 + NCW], e, q="pool")
                            else:
                                e = evf[kq % 2]
                                b.act(e, p[:, 0:NCW], AF.Sigmoid)
                                b.dma(scr["o"][rows, n0:n0 + NCW], e, q="pool")
                            kq += 1
                wg = b.sbuf("mwg", [128, KT, NG], BF16)
                b.dma(wg, w_in[:, :, G0:G0 + NG])
                bg = b.sbuf("mbg", [128, NG], F32)
                self.load_bc(bg, self.inp["mlstm_b_gate"], self.inp["mlstm_b_gate"].h[j].rearrange("z f h -> (z f h)"))
                gt = [b.sbuf(f"mgt{k}", [128, NG], F32) for k in range(2)]
                pgt = b.psum("mpgt", [128, 512], F32)
                for t in range(S // 128):
                    p = pv[t % 3]
                    g = gt[t % 2]
                    b.mm(p[:, 0:NG], [(hT[:, kt, t * 128:(t + 1) * 128], wg[:, kt, :]) for kt in range(KT)])
                    b.tt("dve", g, p[:, 0:NG], bg, ALU.add)
                    b.act(g, g, AF.Tanh, scale=1.0 / GATE_CAP)
                    b.ts("dve", g, g, GATE_CAP, ALU.mult)
                    b.transpose(pgt[0:NG, 0:128], g, self.ident_f)
                    b.copy("dve", GTall[:, t * 128:(t + 1) * 128], pgt[0:NG, 0:128])
            self.mnegM = b.sbuf("mnegM", [NLP, S], F32)
            self.mcol = [b.sbuf(f"mcol{k}", [64, NC, NLP], F32) for k in range(4)]
            self.mdecB = b.sbuf("mdecB", [128, NLP * NC], F32)
            with b.scope():
                bufs = [b.sbuf(f"gb{k}", [NLP, S], F32) for k in range(7)]
                IGn, LFn, IG, LF, X2, natA, natB = bufs
                b.memset("dve", IGn, 0.0)
                b.memset("pool", LFn, 0.0)
                for z in range(2):
                    b.dma(IGn[z * 32:z * 32 + MH, :], GTall[z * 2 * MH:z * 2 * MH + MH, :])
                    b.dma(LFn[z * 32:z * 32 + MH, :], GTall[z * 2 * MH + MH:(z + 1) * 2 * MH, :])

                def rev(t_, s0, n):
                    h = t_.h[32:NLP, :]
                    return View(t_, bass.AP(h.tensor, h.offset + s0 + n - 1, [list(h.ap[0]), [-1, n]]))

                def to_scan(dst, src, eng):
                    b.copy(eng, dst[0:32, :], src[0:32, :])
                    for (s0, n) in ((0, L), (L, T)):
                        b.copy(eng, dst[32:NLP, s0:s0 + n], rev(src, s0, n))

                to_scan(IG, IGn, "dve")
                to_scan(LF, LFn, "pool")
                b.act(LF, LF, AF.Exp, scale=-1.0)
                b.ts("dve", LF, LF, 1.0, ALU.add)
                b.act(LF, LF, AF.Ln)
                b.ts("dve", LF, LF, -1.0, ALU.mult)
                one = b.sbuf("gone", [NLP, 1], F32)
                b.memset("dve", one, 1.0)
                Gc = IGn
                onesb = one.v.bc([NLP, S])
                b.op("dve", lambda en: en.tensor_tensor_scan(out=Gc.h[:], data0=onesb.ap, data1=LF.h[:], initial=0.0,
                                                             op0=ALU.mult, op1=ALU.add), w=[Gc], r=[one, LF])
                c3 = lambda t_: View(t_, t_.h[:].rearrange("p (c t) -> p c t", t=CHUNK))
                shp = [NLP, NC, CHUNK]
                Gs = b.sbuf("gGs", [NLP, NC], F32)
                b.memset("dve", Gs[:, 0:1], 0.0)
                b.copy("dve", Gs[:, 1:NC], c3(Gc)[:, 0:NC - 1, CHUNK - 1])
                bcum = Gc
                b.tt("dve", c3(bcum), c3(Gc), Gs.v.us(2).bc(shp), ALU.subtract)
                gq = IG
                b.tt("dve", gq, IG, bcum, ALU.subtract)
                cur, nxt = LF, LFn
                b.copy("dve", cur, gq)
                s_ = 1
                while s_ < CHUNK:
                    b.copy("pool", c3(nxt)[:, :, 0:s_], c3(cur)[:, :, 0:s_])
                    b.tt("dve", c3(nxt)[:, :, s_:CHUNK], c3(cur)[:, :, s_:CHUNK], c3(cur)[:, :, 0:CHUNK - s_], ALU.max)
                    cur, nxt = nxt, cur
                    s_ *= 2
                cm, spare = cur, nxt
                gmax = b.sbuf("ggmax", [NLP, NC], F32)
                bend = b.sbuf("gbend", [NLP, NC], F32)
                b.copy("dve", gmax, c3(cm)[:, :, CHUNK - 1])
                b.copy("dve", bend, c3(bcum)[:, :, CHUNK - 1])
                mnext = b.sbuf("gmnext", [NLP, NC], F32)
                b.op("dve", lambda en: en.tensor_tensor_scan(out=mnext.h[:], data0=gmax.h[:], data1=bend.h[:], initial=0.0,
                                                             op0=ALU.max, op1=ALU.add), w=[mnext], r=[gmax, bend])
                mst = b.sbuf("gmst", [NLP, NC], F32)
                b.memset("dve", mst[:, 0:1], 0.0)
                b.copy("dve", mst[:, 1:NC], mnext[:, 0:NC - 1])
                M = spare
                b.tt("dve", c3(M), c3(cm), mst.v.us(2).bc(shp), ALU.max)
                Mlast = b.sbuf("gMlast", [NLP, NC], F32)
                b.tt("dve", Mlast, mst, gmax, ALU.max)
                dec = b.sbuf("gdec", [NLP, NC], F32)
                b.tt("dve", dec, mst, Mlast, ALU.subtract)
                b.act(dec, dec, AF.Exp)
                b.dma(View(scr["dec"], scr["dec"].h[:].rearrange("(l c) -> l c", c=NC)), dec)
                t1_ = cm
                b.tt("dve", c3(t1_), mst.v.us(2).bc(shp), c3(M), ALU.subtract)
                b.act(t1_, t1_, AF.Exp)
                t2_ = bcum
                b.tt("dve", t2_, bcum, M, ALU.add)
                b.act(t2_, t2_, AF.Exp, scale=-1.0)
                t3_ = X2
                b.tt("dve", c3(t3_), c3(gq), Mlast.v.us(2).bc(shp), ALU.subtract)
                b.act(t3_, t3_, AF.Exp)
                b.ts("dve", M, M, -1.0, ALU.mult)
                tabs = [gq, t1_, t2_, t3_, M]

                def to_nat(dst, src, eng):
                    b.copy(eng, dst[0:32, :], src[0:32, :])
                    for (s0, n) in ((0, L), (L, T)):
                        b.copy(eng, dst[32:NLP, s0:s0 + n], rev(src, s0, n))

                to_nat(self.mnegM, M, "pool")
                pct = [b.psum(f"gpct{k}", [64, 512], F32) for k in range(2)]
                per = 512 // NLP
                kk_ = 0
                for k in range(4):
                    nat = (natA, natB)[k % 2]
                    to_nat(nat, tabs[k], ("dve", "pool")[k % 2])
                    for c0 in range(0, NC, per):
                        nn = min(per, NC - c0)
                        p = pct[kk_ % 2]
                        kk_ += 1
                        for q in range(nn):
                            b.transpose(p[:, q * NLP:(q + 1) * NLP], nat[:, (c0 + q) * CHUNK:(c0 + q + 1) * CHUNK],
                                        self.ident_f[0:NLP, 0:NLP])
                        b.copy("dve", View(self.mcol[k], self.mcol[k].h[:, c0:c0 + nn, :]),
                               View(p, p.h[:, 0:nn * NLP].rearrange("p (c l) -> p c l", l=NLP)))
                b.dma(self.mdecB, View(scr["dec"], scr["dec"].h[:].partition_broadcast(128)))
            self.mlstm_chunks(bb, qT, kT)

    def mlstm_chunks(self, bb, qT, kT):
        b, c = self.b, self.c
        D, L, T, S, MH, DV = c["D"], c["L"], c["T"], c["S"], c["MH"], c["DV"]
        base = bb * S
        scr = self.mscr
        NL = 2 * MH
        NLP = 32 + MH
        NC, NCc, NCl = S // CHUNK, L // CHUNK, T // CHUNK
        DV1 = DV + 1
        negM, col, decB = self.mnegM, self.mcol, self.mdecB
        with b.scope():
            Cst = b.sbuf("cC", [128, NL, DV1], F32)
            Cbf = b.sbuf("cCb", [128, NL, DV1], BF16)
            b.memset("dve", Cst, 0.0)
            b.memset("pool", Cbf, 0.0)
            sel = b.sbuf("csel", [NLP, NLP, 64], F32)
            b.dma(sel, self.inp["msel"][0:NLP, 0:NLP, :])
            mneg = b.sbuf("cmneg", [64, 2, 64], F32)
            b.dma(mneg, self.inp["mmask"])
            vx = [b.sbuf(f"cvx{k}", [64, MH, DV1], BF16) for k in range(4)]
            for t_ in vx:
                b.memset("dve", t_, 1.0)
            ho = [b.sbuf(f"cho{k}", [64, D], F32) for k in range(4)]
            Dt = [b.sbuf(f"cDt{k}", [64, 64], F32) for k in range(2)]
            SpT = [b.sbuf(f"cSp{k}", [64, 64], BF16) for k in range(2)]
            t1 = [b.sbuf(f"ct1{k}", [64, DV1], F32) for k in range(2)]
            nd = [b.sbuf(f"cnd{k}", [64, DV1], F32) for k in range(2)]
            dd = [b.sbuf(f"cdd{k}", [64, 1], F32) for k in range(2)]
            wk = [b.sbuf(f"cwk{k}", [64, 128], BF16) for k in range(2)]
            pE = [b.psum(f"cpE{k}", [64, 512], F32) for k in range(2)]
            pnum = b.psum("cpnum", [64, 512], F32)
            pqC = b.psum("cpqC", [64, 512], F32)
            pkk = b.psum("cpkk", [64, 128], BF16)
            pCu = b.psum("cpCu", [128, 512], F32)
            it = 0
            for cs in range(NC):
                for z in range(2):
                    if z == 0:
                        cn = cs
                    else:
                        cn = (NCc - 1 - cs) if cs < NCc else (NCc + NCl - 1 - (cs - NCc))
                    tok0 = cn * CHUNK
                    vt = vx[(cs * 2 + z) % 4]
                    hout = ho[(cs * 2 + z) % 4]
                    rows = slice(base + tok0, base + tok0 + CHUNK)
                    b.dma(vt[:, :, 0:DV], View(scr["v"], scr["v"].h[rows, :].rearrange("t (h e) -> t h e", e=DV)))
                    for h in range(MH):
                        lane = z * MH + h
                        lp = z * 32 + h
                        k2 = it % 2
                        it += 1
                        qs = qT[:, h, tok0:tok0 + CHUNK]
                        ks = kT[:, h, tok0:tok0 + CHUNK]
                        pe_ = pE[k2]
                        b.mm(pe_[:, 0:64], [(sel[:, lp, :], negM[:, tok0:tok0 + CHUNK]),
                                            (self.ident_f[0:64, 0:64], mneg[:, z, :])])
                        b.mm(pe_[:, 64:128], [(ks, qs)])
                        b.act(Dt[k2], pe_[:, 0:64], AF.Exp, bias=col[0][:, cn, lp:lp + 1])
                        b.tt("dve", SpT[k2], pe_[:, 64:128], Dt[k2], ALU.mult)
                        b.mm(pnum[:, 0:DV1], [(SpT[k2], vt[:, h, :])])
                        b.mm(pqC[:, 0:DV1], [(qs, Cbf[:, lane, :])])
                        b.act(t1[k2], pqC[:, 0:DV1], AF.Copy, scale=col[1][:, cn, lp:lp + 1])
                        b.tt("dve", nd[k2], t1[k2], pnum[:, 0:DV1], ALU.add)
                        b.act(dd[k2], nd[k2][:, DV:DV1], AF.Abs)
                        b.tt("dve", dd[k2], dd[k2], col[2][:, cn, lp:lp + 1], ALU.max)
                        b.op("dve", lambda en: en.reciprocal(out=dd[k2].h[:], in_=dd[k2].h[:]), w=[dd[k2]], r=[dd[k2]])
                        b.ts("dve", hout[:, h * DV:(h + 1) * DV], nd[k2][:, 0:DV], dd[k2][:, 0:1], ALU.mult)
                        b.transpose(pkk, ks, self.ident_b)
                        b.ts("dve", wk[k2], pkk, col[3][:, cn, lp:lp + 1], ALU.mult)
                        b.mm(pCu[:, 0:DV1], [(wk[k2], vt[:, h, :])])
                        b.stt("dve", Cst[:, lane, :], Cst[:, lane, :], decB[:, lp * NC + cs:lp * NC + cs + 1],
                              pCu[:, 0:DV1], ALU.mult, ALU.add)
                        b.copy("act", Cbf[:, lane, :], Cst[:, lane, :])
                    b.dma(scr[f"h{z}"][rows, :], hout, q="pool")

    def mlstm_readout(self, i, j):
        b, c = self.b, self.c
        D, KT, MH, DV = c["D"], c["KT"], c["MH"], c["DV"]
        last = i == c["DEPTH"] - 1
        scr = self.mscr
        shp = [128, MH, DV]
        with b.scope():
            names = ("h0", "h1", "o")
            ld = [{nm: b.sbuf(f"u{nm}{k}", [128, D], F32) for nm in names} for k in range(2)]
            nw = b.sbuf("unw", [128, D], F32)
            G = b.sbuf("uG", [128, D], F32)
            self.load_bc(nw, self.inp["mlstm_norm_w"], self.inp["mlstm_norm_w"].h[j, :])
            sq = b.sbuf("usq", [128, D], F32)
            st1 = b.sbuf("ust1", [128, MH], F32)
            st2 = b.sbuf("ust2", [128, MH], F32)
            ob = b.sbuf("uob", [128, D], BF16)
            oT = b.sbuf("uoT", [128, KT, 512], BF16)
            wch = [b.sbuf(f"uw{k}", [128, KT, 512], BF16) for k in range(2)]
            ot = [b.sbuf(f"uot{k}", [128, 512], F32) for k in range(2)]
            og = [b.sbuf(f"uog{k}", [128, 512], F32) for k in range(2)]
            tps = [b.psum(f"utp{k}", [128, 4, 128], BF16) for k in range(2)]
            ps = [b.psum(f"ups{k}", [128, 512], F32) for k in range(4)]
            cur_g = None
            kw = kp = kl = 0
            v3 = lambda t_: View(t_, t_.h[:].rearrange("p (h n) -> p h n", n=DV))
            for (r0, ntok, mg, is_ctx, bb) in self.groups(512, include_ctx=not last):
                nt = ntok // 128
                if mg != cur_g:
                    mr = self.modrows
                    self.load_bc(G, mr, mr.h[i, mg, 2 * D:3 * D])
                    cur_g = mg
                for t in range(nt):
                    rr = slice(r0 + t * 128, r0 + (t + 1) * 128)
                    l_ = ld[kl % 2]
                    kl += 1
                    b.dma(l_["h0"], scr["h0"][rr, :])
                    b.dma(l_["h1"], scr["h1"][rr, :], q="act")
                    b.dma(l_["o"], scr["o"][rr, :])
                    h0, h1, o_ = l_["h0"], l_["h1"], l_["o"]
                    b.tt("dve", h0, h0, h1, ALU.add)
                    b.red(st1, v3(h0))
                    b.ts("dve", st1, st1, 1.0 / DV, ALU.mult)
                    b.tt("dve", v3(h0), v3(h0), st1.v.us(2).bc(shp), ALU.subtract)
                    b.tt("pool", sq, h0, h0, ALU.mult)
                    b.red(st2, v3(sq))
                    b.act(st2, st2, AF.Sqrt, bias=self.eps_t[:, 0:1], scale=1.0 / DV)
                    b.op("dve", lambda en: en.reciprocal(out=st2.h[:], in_=st2.h[:]), w=[st2], r=[st2])
                    b.tt("dve", v3(h0), v3(h0), st2.v.us(2).bc(shp), ALU.mult)
                    b.tt("pool", h0, h0, nw, ALU.mult)
                    b.tt("dve", ob, h0, o_, ALU.mult)
                    self.transpose_tile(ob, oT, t * 128, tps, t)
                for n0 in range(0, D, 512):
                    w = wch[kw % 2]
                    kw += 1
                    b.dma(w, self.wb[("m_out", j)][:, :, n0:n0 + 512])
                    for t in range(nt):
                        p = ps[kp % 4]
                        kp += 1
                        b.mm(p, [(oT[:, kt, t * 128:(t + 1) * 128], w[:, kt, :]) for kt in range(KT)])
                        o, gg = ot[t % 2], og[t % 2]
                        rows = self.xs[r0 + t * 128:r0 + (t + 1) * 128, n0:n0 + 512]
                        b.dma(o, rows, q="act")
                        b.tt("dve", gg, p, G[:, n0:n0 + 512], ALU.mult)
                        b.tt("pool", o, o, gg, ALU.add)
                        b.dma(rows, o, q="pool")


def host_inputs(cfg, inputs, ncores):
    c = cfg
    NB = c["NB"]
    x = np.asarray(inputs["x"], dtype=np.float32)
    ctx = np.asarray(inputs["ctx"], dtype=np.float32)
    cc = np.asarray(inputs["c"], dtype=np.float32)
    c_ctx = np.asarray(inputs["c_ctx"], dtype=np.float32)
    shared = {}
    for name in input_shapes(c):
        if name in ("xs0", "ccT", "ident", "msel", "mmask", "rconst", "rconst2"):
            continue
        a = np.ascontiguousarray(np.asarray(inputs[name], dtype=np.float32))
        shp = input_shapes(c)[name]
        if list(a.shape) != shp:
            a = np.zeros(shp, np.float32)
        shared[name] = a
    shared["ident"] = np.eye(128, dtype=np.float32)
    shared["msel"], shared["mmask"], shared["rconst"], shared["rconst2"] = const_tables()
    maps = []
    for k in range(ncores):
        xs0 = np.concatenate([np.concatenate([ctx[k * NB + j], x[k * NB + j]], axis=0) for j in range(NB)], axis=0)
        rows = np.concatenate([cc[k * NB:(k + 1) * NB], c_ctx[None, :]], axis=0)
        ccT = np.ascontiguousarray(rows.T.reshape(c["KT"], 128, NB + 1).transpose(1, 0, 2))
        m = dict(shared)
        m["xs0"] = np.ascontiguousarray(xs0)
        m["ccT"] = ccT
        maps.append(m)
    return maps


def const_tables():
    msel = np.zeros((64, 64, 64), np.float32)
    for l in range(64):
        msel[l, l, :] = 1.0
    jj, tt = np.meshgrid(np.arange(64), np.arange(64), indexing="ij")
    mmask = np.zeros((64, 2, 64), np.float32)
    mmask[:, 0, :] = np.where(jj <= tt, 0.0, -30000.0)
    mmask[:, 1, :] = np.where(jj >= tt, 0.0, -30000.0)
    rconst = np.zeros((64, 7, 64), np.float32)
    rconst[:, 0, :] = (jj < tt)
    rconst[:, 1, :] = (jj > tt)
    rconst[:, 2, :] = (jj <= tt)
    rconst[:, 3, :] = (jj >= tt)
    rconst[:, 4, :] = (jj > tt)
    rconst[:, 5, :] = (jj < tt)
    rconst[:, 6, :] = (jj == tt)
    rconst2 = np.zeros((64, 2, 128), np.float32)
    for z in range(2):
        rconst2[:, z, 0:64] = rconst[:, 0 + z, :]
        rconst2[:, z, 64:128] = rconst[:, 2 + z, :]
    return msel, mmask, rconst, rconst2


_NC_CACHE = {}


def run_net(cfg, inputs, ncores, **netkw):
    key = (tuple(sorted(cfg.items())), tuple(sorted(netkw.items())))
    if key not in _NC_CACHE:
        net = Net(cfg, **netkw)
        _NC_CACHE[key] = net.build()
    nc = _NC_CACHE[key]
    maps = host_inputs(cfg, inputs, ncores)
    res = run_bass_kernel_spmd(nc, maps, core_ids=list(range(ncores)))
    outs = [r["out"].reshape(cfg["NB"], cfg["T"], cfg["D"]) for r in res.results]
    return np.concatenate(outs, axis=0)


def kernel(**inputs):
    cfg = make_cfg()
    return run_net(cfg, inputs, 8).astype(np.float32)
```

```python
import numpy as np
from contextlib import ExitStack, contextmanager
import concourse.bass as bass
import concourse.mybir as mybir
from concourse.bass_utils import run_bass_kernel_spmd

F32 = mybir.dt.float32
BF16 = mybir.dt.bfloat16
AF = mybir.ActivationFunctionType
ALU = mybir.AluOpType
AX = mybir.AxisListType

SEM_LIMIT = 30000
NORM_EPS = 1e-6
GRID_W = 64
CHUNK = 64
GATE_CAP = 15.0


class Counter:
    def __init__(self, b, name):
        self.b = b
        self.name = name
        self.n = 0
        self.sem, self.val = b.take_sem(f"{name}_0")

    def next(self, inc):
        if self.val + inc > SEM_LIMIT:
            self.n += 1
            self.sem, self.val = self.b.take_sem(f"{self.name}_{self.n}")
        self.val += inc
        return (self.sem, self.val)


class Rec:
    def __init__(self):
        self.w = {}
        self.r = {}


def _merge(d, ev):
    s, v = ev
    if d.get(s, -1) < v:
        d[s] = v


class Buf:
    def __init__(self, b, handle, name, space):
        self.b = b
        self.h = handle
        self.name = name
        self.space = space
        self.rec = Rec()
        self.subs = {}
        self.dctr = None

    def recs(self):
        return [self.rec] + [s.rec for s in self.subs.values()]

    def sub(self, key):
        if key not in self.subs:
            self.subs[key] = SubBuf(self, key)
        return self.subs[key]

    def __getitem__(self, idx):
        return View(self, self.h[idx])

    @property
    def v(self):
        return View(self, self.h[:])

    def dma_counter(self):
        if self.dctr is None:
            self.dctr = Counter(self.b, "d_" + self.name)
        return self.dctr


class SubBuf:
    def __init__(self, parent, key):
        self.parent = parent
        self.rec = Rec()
        self.name = f"{parent.name}.{key}"
        self.space = parent.space
        self.h = parent.h
        self.b = parent.b

    def recs(self):
        return [self.rec, self.parent.rec]

    def __getitem__(self, idx):
        return View(self, self.h[idx])

    @property
    def v(self):
        return View(self, self.h[:])

    def dma_counter(self):
        return self.parent.dma_counter()


class View:
    def __init__(self, buf, ap):
        self.buf = buf
        self.ap = ap

    def __getitem__(self, idx):
        return View(self.buf, self.ap[idx])

    def re(self, s, **kw):
        return View(self.buf, self.ap.rearrange(s, **kw))

    def bc(self, shape):
        return View(self.buf, self.ap.broadcast_to(list(shape)))

    def us(self, axis):
        return View(self.buf, self.ap.unsqueeze(axis))


def _v(x):
    return x if isinstance(x, View) else x.v


class B:
    ENG = ("pe", "act", "dve", "pool", "sp")

    def __init__(self):
        self.nc = bass.Bass("TRN2", target_bir_lowering=False)
        nc = self.nc
        self.es = ExitStack()
        self.eng = {"pe": nc.tensor, "act": nc.scalar, "dve": nc.vector, "pool": nc.gpsimd, "sp": nc.sync}
        self.ctr = {}
        self.known = {e: {} for e in self.ENG}
        for e in self.ENG:
            self.ctr[e] = Counter(self, "c_" + e)
        self.all_bufs = []
        self.ninst = 0
        self.stack = self.es
        self.sem_pool = []

    def take_sem(self, name):
        pool = self.__dict__.setdefault("sem_pool", [])
        while pool:
            sem, val = pool.pop()
            if val < SEM_LIMIT // 2:
                return sem, val
        self.nsem = self.__dict__.get("nsem", 0) + 1
        return self.es.enter_context(self.nc.semaphore(f"{name}_{self.nsem}")), 0

    def dram(self, name, shape, dtype, kind="Internal"):
        t = self.nc.dram_tensor(name, list(shape), dtype, kind=kind).ap()
        bf = Buf(self, t, name, "dram")
        self.all_bufs.append(bf)
        return bf

    def sbuf(self, name, shape, dtype):
        self.uid = getattr(self, "uid", 0) + 1
        name = f"{name}_{self.uid}"
        t = self.stack.enter_context(self.nc.sbuf_tensor(name, list(shape), dtype))
        bf = Buf(self, t, name, "sbuf")
        self.all_bufs.append(bf)
        return bf

    def psum(self, name, shape, dtype):
        self.uid = getattr(self, "uid", 0) + 1
        name = f"{name}_{self.uid}"
        t = self.stack.enter_context(self.nc.psum_tensor(name, list(shape), dtype))
        bf = Buf(self, t, name, "psum")
        self.all_bufs.append(bf)
        return bf

    @contextmanager
    def scope(self):
        prev = self.stack
        nb = len(self.all_bufs)
        with ExitStack() as st:
            self.stack = st
            yield
            self.barrier()
            for x in self.all_bufs[nb:]:
                if x.space != "dram" and x.dctr is not None:
                    self.sem_pool.append((x.dctr.sem, x.dctr.val))
            self.all_bufs = self.all_bufs[:nb] + [x for x in self.all_bufs[nb:] if x.space == "dram"]
            self.stack = prev

    def _deps(self, reads, writes):
        d = {}
        for x in reads:
            for rec in x.buf.recs():
                for ev in rec.w.items():
                    _merge(d, ev)
        for x in writes:
            for rec in x.buf.recs():
                for ev in rec.w.items():
                    _merge(d, ev)
                for ev in rec.r.items():
                    _merge(d, ev)
        return d

    def _wait(self, e, deps, skip_own=False):
        eng = self.eng[e]
        kn = self.known[e]
        own = self.ctr[e].sem
        for s, v in deps.items():
            if skip_own and s is own:
                continue
            if kn.get(s, -1) >= v:
                continue
            eng.wait_ge(s, v)
            kn[s] = v
            self.ninst += 1

    def _record(self, ev, reads, writes):
        for x in reads:
            _merge(x.buf.rec.r, ev)
        for x in writes:
            x.buf.rec.w = {ev[0]: ev[1]}
            x.buf.rec.r = {}
            if isinstance(x.buf, Buf):
                for s in x.buf.subs.values():
                    s.rec.w = {ev[0]: ev[1]}
                    s.rec.r = {}

    def op(self, e, fn, w=(), r=()):
        w = [_v(x) for x in w]
        r = [_v(x) for x in r]
        deps = self._deps(r, w)
        self._wait(e, deps, skip_own=(e == "pe"))
        inst = fn(self.eng[e])
        ev = self.ctr[e].next(1)
        inst.then_inc(ev[0], 1)
        self.ninst += 1
        self._record(ev, r, w)
        return ev

    def mm(self, out, pairs, start=True, stop=True):
        out = _v(out)
        pairs = [(_v(a), _v(c)) for a, c in pairs]
        reads = [x for p in pairs for x in p]
        deps = self._deps(reads, [out])
        self._wait("pe", deps, skip_own=True)
        n = len(pairs)
        inst = None
        for i, (a, c) in enumerate(pairs):
            inst = self.nc.tensor.matmul(out.ap, a.ap, c.ap, start=(start and i == 0), stop=(stop and i == n - 1))
            self.ninst += 1
        ev = self.ctr["pe"].next(1)
        inst.then_inc(ev[0], 1)
        self._record(ev, reads, [out])
        return ev

    def transpose(self, out, in_, ident):
        out, in_, ident = _v(out), _v(in_), _v(ident)
        return self.op("pe", lambda e: e.transpose(out.ap, in_.ap, ident.ap), w=[out], r=[in_, ident])

    def dma(self, out, in_, q="sp"):
        out, in_ = _v(out), _v(in_)
        deps = self._deps([in_], [out])
        self._wait(q, deps)
        owner = out.buf if out.buf.space != "dram" else in_.buf
        ctr = owner.dma_counter()
        inst = self.eng[q].dma_start(out=out.ap, in_=in_.ap)
        ev = ctr.next(16)
        inst.then_inc(ev[0], 16)
        self.ninst += 1
        self._record(ev, [in_], [out])
        return ev

    def barrier(self):
        d = {}
        for bf in self.all_bufs:
            for rec in bf.recs():
                for ev in rec.w.items():
                    _merge(d, ev)
                for ev in rec.r.items():
                    _merge(d, ev)
        for e in self.ENG:
            c = self.ctr[e]
            if c.val > 0:
                _merge(d, (c.sem, c.val))
        for e in self.ENG:
            self._wait(e, d)

    @staticmethod
    def _sc(x):
        return x.ap if isinstance(x, View) else x

    def act(self, out, in_, func, bias=None, scale=None, accum=None):
        out, in_ = _v(out), _v(in_)
        kw = {}
        rd = [in_]
        wr = [out]
        if bias is not None:
            kw["bias"] = self._sc(bias)
            if isinstance(bias, View):
                rd.append(bias)
        if scale is not None:
            kw["scale"] = self._sc(scale)
            if isinstance(scale, View):
                rd.append(scale)
        if accum is not None:
            kw["accum_out"] = accum.ap
            wr.append(accum)
        return self.op("act", lambda e: e.activation(out=out.ap, in_=in_.ap, func=func, **kw), w=wr, r=rd)

    def tt(self, e, out, a, c, op):
        out, a, c = _v(out), _v(a), _v(c)
        return self.op(e, lambda en: en.tensor_tensor(out=out.ap, in0=a.ap, in1=c.ap, op=op), w=[out], r=[a, c])

    def ts(self, e, out, a, s1, op0, s2=None, op1=None):
        out, a = _v(out), _v(a)
        rd = [a] + [s for s in (s1, s2) if isinstance(s, View)]
        if op1 is None:
            return self.op(e, lambda en: en.tensor_scalar(out=out.ap, in0=a.ap, scalar1=self._sc(s1), scalar2=None,
                                                          op0=op0), w=[out], r=rd)
        return self.op(e, lambda en: en.tensor_scalar(out=out.ap, in0=a.ap, scalar1=self._sc(s1),
                                                      scalar2=self._sc(s2), op0=op0, op1=op1), w=[out], r=rd)

    def stt(self, e, out, a, s, c, op0, op1):
        out, a, c = _v(out), _v(a), _v(c)
        rd = [a, c] + ([s] if isinstance(s, View) else [])
        return self.op(e, lambda en: en.scalar_tensor_tensor(out=out.ap, in0=a.ap, scalar=self._sc(s), in1=c.ap,
                                                             op0=op0, op1=op1), w=[out], r=rd)

    def red(self, out, in_, op=ALU.add, axis=AX.X):
        out, in_ = _v(out), _v(in_)
        return self.op("dve", lambda en: en.tensor_reduce(out=out.ap, in_=in_.ap, axis=axis, op=op), w=[out], r=[in_])

    def copy(self, e, out, in_):
        out, in_ = _v(out), _v(in_)
        if e == "act":
            return self.op("act", lambda en: en.copy(out=out.ap, in_=in_.ap), w=[out], r=[in_])
        return self.op(e, lambda en: en.tensor_copy(out=out.ap, in_=in_.ap), w=[out], r=[in_])

    def memset(self, e, out, val):
        out = _v(out)
        return self.op(e, lambda en: en.memset(out.ap, val), w=[out])


def make_cfg(D=2048, NB=2, L=256, T=2048, DEPTH=4, MH=8):
    c = dict(D=D, NB=NB, L=L, T=T, DEPTH=DEPTH, MH=MH)
    c["KT"] = D // 128
    c["S"] = L + T
    c["R"] = NB * (L + T)
    c["FF"] = ((8 * D + 767) // 768) * 256
    c["FT"] = c["FF"] // 128
    c["RH"] = D // 64
    c["DV"] = D // MH
    c["DK"] = c["DV"] // 2
    c["QK"] = MH * c["DK"]
    c["PROJ"] = 2 * c["QK"] + 2 * D + 4 * MH
    c["NRW"] = (DEPTH + 1) // 2
    c["NML"] = DEPTH // 2
    return c


WEIGHT_SPECS = None


def input_shapes(c):
    D, DEPTH, NRW, NML = c["D"], c["DEPTH"], c["NRW"], c["NML"]
    return {
        "xs0": [c["R"], D], "ccT": [128, c["KT"], c["NB"] + 1], "ident": [128, 128],
        "msel": [64, 64, 64], "mmask": [64, 2, 64], "rconst": [64, 7, 64], "rconst2": [64, 2, 128],
        "mod_w": [DEPTH, D, 6 * D], "mod_b": [DEPTH, 6 * D], "norm_g": [DEPTH, 2, D], "final_g": [D],
        "rwkv_mu": [NRW, 6, D], "rwkv_w_r": [NRW, D, D], "rwkv_w_k": [NRW, D, D], "rwkv_w_v": [NRW, D, D],
        "rwkv_w_o": [NRW, D, D], "rwkv_w0": [NRW, 2, D], "rwkv_w1": [NRW, 2, D, 96], "rwkv_w2": [NRW, 2, 96, D],
        "rwkv_a0": [NRW, 2, D], "rwkv_a1": [NRW, 2, D, 96], "rwkv_a2": [NRW, 2, 96, D],
        "rwkv_g1": [NRW, D, 256], "rwkv_g2": [NRW, 256, D], "rwkv_k_k": [NRW, D], "rwkv_k_a": [NRW, D],
        "rwkv_r_k": [NRW, c["RH"], 64], "rwkv_ln_w": [NRW, D], "rwkv_ln_b": [NRW, D],
        "rwkv_v0": [max(NRW - 1, 1), D], "rwkv_v1": [max(NRW - 1, 1), D, 64], "rwkv_v2": [max(NRW - 1, 1), 64, D],
        "mlstm_w_in": [NML, D, c["PROJ"]], "mlstm_b_gate": [NML, 2, 2, c["MH"]],
        "mlstm_conv_w": [NML, 3, 3, 2 * c["QK"]], "mlstm_conv_b": [NML, 2 * c["QK"]],
        "mlstm_norm_w": [NML, D], "mlstm_w_out": [NML, D, D],
        "ffn_w_in": [DEPTH, D, 2 * c["FF"]], "ffn_w_out": [DEPTH, c["FF"], D],
    }


class Net:
    def __init__(self, cfg, skip_rwkv=False, skip_mlstm=False, chunked=True):
        self.c = cfg
        self.chunked = chunked
        self.mpar = globals().get("MPAR", 2)
        self.b = B()
        self.skip_rwkv = skip_rwkv
        self.skip_mlstm = skip_mlstm
        self.cvt_rr = 0

    def declare(self):
        b, c = self.b, self.c
        self.inp = {}
        for name, shp in input_shapes(c).items():
            self.inp[name] = b.dram(name, shp, F32, kind="ExternalInput")
        self.out = b.dram("out", [c["NB"] * c["T"], c["D"]], F32, kind="ExternalOutput")
        self.xs = b.dram("xs", [c["R"], c["D"]], F32)
        self.modrows = b.dram("modrows", [c["DEPTH"], c["NB"] + 1, 6 * c["D"]], F32)

    def consts(self):
        b = self.b
        self.ident_f = b.sbuf("ident_f", [128, 128], F32)
        self.ident_b = b.sbuf("ident_b", [128, 128], BF16)
        b.dma(self.ident_f, self.inp["ident"])
        b.copy("dve", self.ident_b, self.ident_f)
        self.eps_t = b.sbuf("eps_t", [128, 4], F32)
        b.memset("dve", self.eps_t[:, 0:1], NORM_EPS)

    def groups(self, gmax=512, include_ctx=True):
        c = self.c
        gs = []
        for bb in range(c["NB"]):
            base = bb * c["S"]
            if include_ctx:
                t = 0
                while t < c["L"]:
                    n = min(gmax, c["L"] - t)
                    gs.append((base + t, n, c["NB"], True, bb))
                    t += n
            t = 0
            while t < c["T"]:
                n = min(gmax, c["T"] - t)
                gs.append((base + c["L"] + t, n, bb, False, bb))
                t += n
        return gs

    def phase_mod(self):
        b, c = self.b, self.c
        D, KT, NG = c["D"], c["KT"], c["NB"] + 1
        with b.scope():
            ccT = b.sbuf("ccT", [128, KT, NG], F32)
            csT = b.sbuf("csT", [128, KT, NG], F32)
            b.dma(ccT, self.inp["ccT"])
            b.act(csT, ccT, AF.Silu)
            wt = [b.sbuf(f"modw{i}", [128, KT, 512], F32) for i in range(2)]
            bias = b.sbuf("modbias", [NG, 6 * D], F32)
            rows = b.sbuf("modrow_sb", [NG, 6 * D], F32)
            ps = [b.psum(f"modps{i}", [128, 512], F32) for i in range(2)]
            k = 0
            for i in range(c["DEPTH"]):
                b.dma(bias, View(self.inp["mod_b"], self.inp["mod_b"].h[i, :].partition_broadcast(NG)))
                for n0 in range(0, 6 * D, 512):
                    w = wt[k % 2]
                    p = ps[k % 2]
                    k += 1
                    src = self.inp["mod_w"].h[i, :, n0:n0 + 512].rearrange("(kt p) n -> p kt n", p=128)
                    b.dma(w, View(self.inp["mod_w"], src))
                    b.mm(p[0:NG, :], [(csT[:, kt, :], w[:, kt, :]) for kt in range(KT)])
                    b.tt("dve", rows[:, n0:n0 + 512], p[0:NG, :], bias[:, n0:n0 + 512], ALU.add)
                b.dma(self.modrows[i], rows, q="pool")

    def cvt_weight(self, name, src_ap, kdim, n, stage, ncmax=4096):
        b = self.b
        pk = min(kdim, 128)
        ktn = kdim // pk
        dst = b.dram(name, [pk, ktn, n], BF16)
        srcv = src_ap.rearrange("(kt p) n -> p kt n", p=pk)
        if n >= ncmax:
            ktc, ncw = 1, ncmax
        else:
            ktc, ncw = max(1, min(ktn, ncmax // n)), n
        for k0 in range(0, ktn, ktc):
            kk = min(ktc, ktn - k0)
            for n0 in range(0, n, ncw):
                nn = min(ncw, n - n0)
                f, h = stage[self.cvt_rr % len(stage)]
                eng = ("dve", "pool", "act")[self.cvt_rr % 3]
                self.cvt_rr += 1
                fv = View(f, f.h[0:pk, 0:kk * nn].rearrange("p (k n) -> p k n", k=kk))
                hv = View(h, h.h[0:pk, 0:kk * nn].rearrange("p (k n) -> p k n", k=kk))
                b.dma(fv, View(self.srcbuf, srcv[:, k0:k0 + kk, n0:n0 + nn]))
                b.copy(eng, hv, fv)
                b.dma(View(dst, dst.h[:, k0:k0 + kk, n0:n0 + nn]), hv, q="pool" if eng != "pool" else "sp")
        return dst

    def phase_prep(self):
        b, c = self.b, self.c
        D, FF = c["D"], c["FF"]
        self.wb = {}
        with b.scope():
            stage = [(b.sbuf(f"cvf{i}", [128, 4096], F32), b.sbuf(f"cvh{i}", [128, 4096], BF16)) for i in range(3)]

            def cv(key, inname, idx, kdim, n):
                self.srcbuf = self.inp[inname]
                ap = self.inp[inname].h
                for j in idx:
                    ap = ap[j]
                self.wb[key] = self.cvt_weight("wb_" + "_".join(str(x) for x in key), ap, kdim, n, stage)

            KT = c["KT"]
            for i in range(c["DEPTH"]):
                dst = b.dram(f"wb_ffn_in_{i}", [2 * c["FT"], 128, KT * 128], BF16)
                self.wb[("ffn_in", i)] = dst
                src = self.inp["ffn_w_in"]
                JB = 4096 // (KT * 128)
                for j0 in range(0, 2 * c["FT"], JB):
                    nj = min(JB, 2 * c["FT"] - j0)
                    f, h = stage[self.cvt_rr % len(stage)]
                    eng = ("dve", "pool", "act")[self.cvt_rr % 3]
                    self.cvt_rr += 1
                    for jj in range(nj):
                        sv = src.h[i, :, (j0 + jj) * 128:(j0 + jj + 1) * 128].rearrange("(kt p) n -> p kt n", p=128)
                        fv = View(f, f.h[:, jj * KT * 128:(jj + 1) * KT * 128].rearrange("p (k n) -> p k n", k=KT))
                        b.dma(fv, View(src, sv), q=("sp", "act")[jj % 2])
                    b.copy(eng, h[:, 0:nj * KT * 128], f[:, 0:nj * KT * 128])
                    b.dma(View(dst, dst.h[j0:j0 + nj].rearrange("j p x -> p j x")),
                          View(h, h.h[:, 0:nj * KT * 128].rearrange("p (j x) -> p j x", j=nj)), q="pool" if eng != "pool" else "sp")
                cv(("ffn_out", i), "ffn_w_out", (i,), FF, D)
            if not self.skip_rwkv:
                for j in range(c["NRW"]):
                    for nm in ("w_r", "w_k", "w_v", "w_o"):
                        cv((nm, j), "rwkv_" + nm, (j,), D, D)
                    for z in range(2):
                        cv(("w1", j, z), "rwkv_w1", (j, z), D, 96)
                        cv(("w2", j, z), "rwkv_w2", (j, z), 96, D)
                        cv(("a1", j, z), "rwkv_a1", (j, z), D, 96)
                        cv(("a2", j, z), "rwkv_a2", (j, z), 96, D)
                    cv(("g1", j), "rwkv_g1", (j,), D, 256)
                    cv(("g2", j), "rwkv_g2", (j,), 256, D)
                    if j > 0:
                        cv(("v1", j), "rwkv_v1", (j - 1,), D, 64)
                        cv(("v2", j), "rwkv_v2", (j - 1,), 64, D)
            if not self.skip_mlstm:
                for j in range(c["NML"]):
                    cv(("m_in", j), "mlstm_w_in", (j,), D, c["PROJ"])
                    cv(("m_out", j), "mlstm_w_out", (j,), D, D)

    def load_bc(self, dst, src_buf, src_ap, q="sp"):
        P = _v(dst).ap.shape[0]
        self.b.dma(dst, View(src_buf, src_ap.partition_broadcast(P)), q=q)

    def mod_tiles(self, i, sub, g, A, Bsh, G, tmp):
        b, c = self.b, self.c
        D = c["D"]
        o = 3 * D * sub
        mr = self.modrows
        self.load_bc(Bsh, mr, mr.h[i, g, o:o + D])
        self.load_bc(tmp, mr, mr.h[i, g, o + D:o + 2 * D])
        self.load_bc(A, self.inp["norm_g"], self.inp["norm_g"].h[i, sub, :])
        b.stt("dve", A, tmp, 1.0, A, ALU.add, ALU.mult)
        if G is not None:
            self.load_bc(G, mr, mr.h[i, g, o + 2 * D:o + 3 * D])

    def norm_tile(self, xt, A, Bsh, hb, junk, hf, st):
        b, D = self.b, self.c["D"]
        b.memset("pool", st[:, 0:1], 0.0)
        b.act(junk, xt, AF.Square, accum=st[:, 0:1])
        b.act(st[:, 1:2], st[:, 0:1], AF.Sqrt, bias=self.eps_t[:, 0:1], scale=1.0 / D)
        b.op("dve", lambda en: en.reciprocal(out=st.h[:, 1:2], in_=st.h[:, 1:2]), w=[st[:, 1:2]], r=[st[:, 1:2]])
        b.stt("dve", hf, xt, st[:, 1:2], A, ALU.mult, ALU.mult)
        if Bsh is None:
            return
        b.tt("dve", hb, hf, Bsh, ALU.add)

    def transpose_tile(self, hb, dstT, tok0, tps, k):
        b, KT = self.b, self.c["KT"]
        for k0 in range(0, KT, 4):
            tp = tps[(k + k0 // 4) % len(tps)]
            for q in range(4):
                b.transpose(tp[:, q, :], hb[:, (k0 + q) * 128:(k0 + q + 1) * 128], self.ident_b)
            eng = "act" if (k0 // 4) % 2 == 0 else "dve"
            b.copy(eng, dstT[:, k0:k0 + 4, tok0:tok0 + 128], tp[:, 0:4, :])

    def ffn(self, i):
        b, c = self.b, self.c
        D, KT, FT, FF = c["D"], c["KT"], c["FT"], c["FF"]
        last = i == c["DEPTH"] - 1
        w_in, w_out = self.wb[("ffn_in", i)], self.wb[("ffn_out", i)]
        JH = FT // 2
        NCW = 512
        with b.scope():
            A = b.sbuf("fA", [128, D], F32)
            Bsh = b.sbuf("fB", [128, D], F32)
            G = b.sbuf("fG", [128, D], F32)
            xt = [b.sbuf(f"fx{k}", [128, D], F32) for k in range(2)]
            hf = b.sbuf("fhf", [128, D], F32)
            tmpm = hf
            hb = [b.sbuf(f"fhb{k}", [128, D], BF16) for k in range(2)]
            st = b.sbuf("fst", [128, 2], F32)
            hT = b.sbuf("fhT", [128, KT, 512], BF16)
            actT = b.sbuf("factT", [128, FT, 512], BF16)
            wg = [b.sbuf(f"fwg{k}", [128, KT, 128], BF16) for k in range(3)]
            wu = [b.sbuf(f"fwu{k}", [128, KT, 128], BF16) for k in range(3)]
            wo = [b.sbuf(f"fwo{k}", [128, JH, NCW], BF16) for k in range(2)]
            sg = [b.sbuf(f"fsg{k}", [128, 512], F32) for k in range(2)]
            ot = [b.sbuf(f"fot{k}", [128, NCW], F32) for k in range(2)]
            tps = [b.psum(f"ftp{k}", [128, 4, 128], BF16) for k in range(2)]
            pg = [b.psum(f"fpg{k}", [128, 512], F32) for k in range(6)]
            cur_g = None
            kx = 0
            for (r0, ntok, mg, is_ctx, bb) in self.groups(512, include_ctx=not last):
                nt = ntok // 128
                if mg != cur_g:
                    self.mod_tiles(i, 1, mg, A, Bsh, G, tmpm)
                    cur_g = mg
                for t in range(nt):
                    x = xt[kx % 2]
                    h = hb[kx % 2]
                    kx += 1
                    b.dma(x, self.xs[r0 + t * 128:r0 + (t + 1) * 128, :])
                    self.norm_tile(x, A, Bsh, h, h, hf, st)
                    self.transpose_tile(h, hT, t * 128, tps, t)
                for j in range(FT):
                    g_w, u_w = wg[j % 3], wu[j % 3]
                    b.dma(g_w, View(w_in, w_in.h[j].rearrange("p (k n) -> p k n", k=KT)))
                    b.dma(u_w, View(w_in, w_in.h[FT + j].rearrange("p (k n) -> p k n", k=KT)))
                    p_g, p_u = pg[(j % 2) * 2], pg[(j % 2) * 2 + 1]
                    b.mm(p_g[:, 0:ntok], [(g_w[:, kt, :], hT[:, kt, 0:ntok]) for kt in range(KT)])
                    b.mm(p_u[:, 0:ntok], [(u_w[:, kt, :], hT[:, kt, 0:ntok]) for kt in range(KT)])
                    s = sg[j % 2]
                    b.act(s[:, 0:ntok], p_g[:, 0:ntok], AF.Silu)
                    b.tt("dve", actT[:, j, 0:ntok], s[:, 0:ntok], p_u[:, 0:ntok], ALU.mult)
                ko = 0
                for n0 in range(0, D, NCW):
                    for half in range(2):
                        w = wo[ko % 2]
                        ko += 1
                        b.dma(w, w_out[:, half * JH:(half + 1) * JH, n0:n0 + NCW])
                        for t in range(nt):
                            b.mm(pg[2 + t], [(actT[:, half * JH + jj, t * 128:(t + 1) * 128], w[:, jj, :])
                                             for jj in range(JH)], start=(half == 0), stop=(half == 1))
                    for t in range(nt):
                        o = ot[t % 2]
                        rows = self.xs[r0 + t * 128:r0 + (t + 1) * 128, n0:n0 + NCW]
                        b.dma(o, rows, q="act")
                        b.tt("dve", sg[t % 2][:, 0:NCW], pg[2 + t], G[:, n0:n0 + NCW], ALU.mult)
                        b.tt("pool", o, o, sg[t % 2][:, 0:NCW], ALU.add)
                        b.dma(rows, o, q="pool")

    def final(self):
        b, c = self.b, self.c
        D = c["D"]
        with b.scope():
            A = b.sbuf("nA", [128, D], F32)
            self.load_bc(A, self.inp["final_g"], self.inp["final_g"].h[:])
            xt = [b.sbuf(f"nx{k}", [128, D], F32) for k in range(2)]
            hf = [b.sbuf(f"nh{k}", [128, D], F32) for k in range(2)]
            junk = b.sbuf("njunk", [128, D], BF16)
            st = b.sbuf("nst", [128, 2], F32)
            k = 0
            for bb in range(c["NB"]):
                for t in range(c["T"] // 128):
                    r0 = bb * c["S"] + c["L"] + t * 128
                    x, h = xt[k % 2], hf[k % 2]
                    k += 1
                    b.dma(x, self.xs[r0:r0 + 128, :])
                    self.norm_tile(x, A, None, None, junk, h, st)
                    b.dma(self.out[bb * c["T"] + t * 128: bb * c["T"] + (t + 1) * 128, :], h, q="pool")

    def build(self):
        b, c = self.b, self.c
        self.declare()
        self.consts()
        for r0 in range(0, c["R"], 512):
            r1 = min(c["R"], r0 + 512)
            b.dma(self.xs[r0:r1, :], self.inp["xs0"][r0:r1, :], q=("sp", "act", "pool")[(r0 // 512) % 3])
        self.phase_mod()
        self.phase_prep()
        for i in range(c["DEPTH"]):
            self.mixer(i)
            self.ffn(i)
        self.final()
        b.barrier()
        return b.nc

    def mixer(self, i):
        if i % 2 == 0:
            if not self.skip_rwkv:
                self.rwkv(i)
        else:
            if not self.skip_mlstm:
                self.mlstm(i)

    def rwkv(self, i):
        c = self.c
        j = i // 2
        if not hasattr(self, "rscr"):
            b = self.b
            R, D = c["R"], c["D"]
            self.rscr = {k: b.dram("rs_" + k, [R, D], F32) for k in
                         ("r", "k", "v", "w0", "w1", "a0", "a1", "g", "y0", "y1", "vf")}
        for bb in range(c["NB"]):
            self.rwkv_proj(i, j, bb)
        if self.chunked:
            self.rwkv_scan_chunked(i, j)
        else:
            self.rwkv_scan(i, j)
        self.rwkv_readout(i, j)

    def rwkv_proj(self, i, j, bb):
        b, c = self.b, self.c
        D, KT, L, T, S = c["D"], c["KT"], c["L"], c["T"], c["S"]
        base = bb * S
        scr = self.rscr
        vdst = scr["vf"] if j == 0 else scr["v"]
        with b.scope():
            hT = b.sbuf("rhT", [128, KT, S], BF16)
            with b.scope():
                A = b.sbuf("rA", [128, D], F32)
                Bsh = b.sbuf("rB", [128, D], F32)
                tmpm = b.sbuf("rtm", [128, D], F32)
                xt = [b.sbuf(f"rx{k}", [128, D], F32) for k in range(2)]
                hf = b.sbuf("rhf", [128, D], F32)
                hb = [b.sbuf(f"rhb{k}", [128, D], BF16) for k in range(2)]
                junk = b.sbuf("rjunk", [128, D], BF16)
                st = b.sbuf("rst", [128, 2], F32)
                tps = [b.psum(f"rtp{k}", [128, 4, 128], BF16) for k in range(2)]
                for (g, t0, n) in ((c["NB"], 0, L), (bb, L, T)):
                    self.mod_tiles(i, 0, g, A, Bsh, None, tmpm)
                    for t in range(n // 128):
                        x, h = xt[t % 2], hb[t % 2]
                        b.dma(x, self.xs[base + t0 + t * 128: base + t0 + (t + 1) * 128, :])
                        self.norm_tile(x, A, Bsh, h, junk, hf, st)
                        self.transpose_tile(h, hT, t0 + t * 128, tps, t)
            with b.scope():
                NM = 6
                mu_rows = b.sbuf("rmur", [NM * KT, 128], F32)
                mu = b.sbuf("rmu", [128, NM * KT], F32)
                omu = b.sbuf("romu", [128, NM * KT], F32)
                b.dma(mu_rows, View(self.inp["rwkv_mu"], self.inp["rwkv_mu"].h[j].rearrange("m (kt p) -> (m kt) p", p=128)))
                pmu = b.psum("rpmu", [128, 128], F32)
                b.transpose(pmu[:, 0:NM * KT], mu_rows, self.ident_f[0:NM * KT, 0:NM * KT])
                b.copy("dve", mu, pmu[:, 0:NM * KT])
                b.ts("dve", omu, mu, -1.0, ALU.mult, 1.0, ALU.add)
                xm = [b.sbuf(f"rxm{k}", [128, KT, 512], BF16) for k in range(2)]
                wch = [b.sbuf(f"rwch{k}", [128, KT, 512], BF16) for k in range(2)]
                brow = b.sbuf("rbrow", [128, D], F32)
                l1w = b.sbuf("rl1w", [128, KT, 256], BF16)
                l2w = b.sbuf("rl2w", [128, 2, D], BF16)
                l1 = b.sbuf("rl1", [128, 2, 512], BF16)
                ev = [b.sbuf(f"rev{k}", [128, 512], F32) for k in range(3)]
                vft = b.sbuf("rvft", [128, 512], F32)
                ps = [b.psum(f"rps{k}", [128, 512], F32) for k in range(4)]
                pl = [b.psum(f"rpl{k}", [128, 512], F32) for k in range(2)]
                state = dict(kx=0, kw=0, kp=0, ke=0)
                blocks = [(0, L, True)] + [(L + t0, min(512, T - t0), False) for t0 in range(0, T, 512)]
                MIX = {"r": 0, "w": 1, "k": 2, "v": 3, "a": 4, "g": 5}
                NEG_EXP_HALF = -float(np.exp(-0.5))

                def build_xm(m, tok0, n, is_ctx):
                    x = xm[state["kx"] % 2]
                    state["kx"] += 1
                    for kt in range(KT):
                        col = m * KT + kt
                        eng = "dve"
                        b.ts(eng, x[:, kt, 0:n], hT[:, kt, tok0:tok0 + n], omu[:, col:col + 1], ALU.mult)
                        muv = mu[:, col:col + 1]
                        if is_ctx:
                            if kt < KT // 2:
                                dst, src = x[:, kt, 1:n], hT[:, kt, tok0:tok0 + n - 1]
                            else:
                                dst, src = x[:, kt, 0:n - 1], hT[:, kt, tok0 + 1:tok0 + n]
                        else:
                            q = kt // (KT // 4)
                            t0 = tok0 - L
                            if q == 0:
                                dst = View(x, x.h[:, kt, 0:n].rearrange("p (r c) -> p r c", c=GRID_W)[:, :, 1:GRID_W])
                                src = View(hT, hT.h[:, kt, tok0:tok0 + n].rearrange("p (r c) -> p r c", c=GRID_W)[:, :, 0:GRID_W - 1])
                            elif q == 1:
                                dst = View(x, x.h[:, kt, 0:n].rearrange("p (r c) -> p r c", c=GRID_W)[:, :, 0:GRID_W - 1])
                                src = View(hT, hT.h[:, kt, tok0:tok0 + n].rearrange("p (r c) -> p r c", c=GRID_W)[:, :, 1:GRID_W])
                            elif q == 2:
                                lo = GRID_W if t0 == 0 else 0
                                dst, src = x[:, kt, lo:n], hT[:, kt, tok0 + lo - GRID_W:tok0 + n - GRID_W]
                            else:
                                hi = n - GRID_W if t0 + n == T else n
                                dst, src = x[:, kt, 0:hi], hT[:, kt, tok0 + GRID_W:tok0 + hi + GRID_W]
                        b.stt("dve", dst, src, muv, dst, ALU.mult, ALU.add)
                    return x

                def load_w(key, n0):
                    w = wch[state["kw"] % 2]
                    state["kw"] += 1
                    b.dma(w, self.wb[key][:, :, n0:n0 + 512])
                    return w

                def nextps():
                    p = ps[state["kp"] % 4]
                    state["kp"] += 1
                    return p

                def nextev():
                    e = ev[state["ke"] % 3]
                    state["ke"] += 1
                    return e

                def rows(tok0, t):
                    return slice(base + tok0 + t * 128, base + tok0 + (t + 1) * 128)

                def lora_hidden(x, n, key, width, func):
                    b.dma(l1w[:, :, 0:width], self.wb[key])
                    for mt in range((width + 127) // 128):
                        wd = min(128, width - mt * 128)
                        p = pl[mt % 2]
                        b.mm(p[0:wd, 0:n], [(l1w[:, kt, mt * 128:mt * 128 + wd], x[:, kt, 0:n]) for kt in range(KT)])
                        if func is None:
                            b.copy("act", l1[0:wd, mt, 0:n], p[0:wd, 0:n])
                        else:
                            b.act(l1[0:wd, mt, 0:n], p[0:wd, 0:n], func)

                def lora_out(t, n0, width):
                    p = nextps()
                    nmt = (width + 127) // 128
                    b.mm(p, [(l1[0:min(128, width - mt * 128), mt, t * 128:(t + 1) * 128],
                              l2w[0:min(128, width - mt * 128), mt, n0:n0 + 512]) for mt in range(nmt)])
                    return p

                def load_l2(key, width):
                    src = self.wb[key]
                    nmt = (width + 127) // 128
                    pk = min(width, 128)
                    b.dma(l2w[0:pk, 0:nmt, :], src)

                for (tok0, n, is_ctx) in blocks:
                    nt = n // 128
                    for nm, key, dst in (("r", ("w_r", j), scr["r"]), ("k", ("w_k", j), scr["k"]), ("v", ("w_v", j), vdst)):
                        x = build_xm(MIX[nm], tok0, n, is_ctx)
                        vres = (nm == "v" and j > 0)
                        if vres:
                            lora_hidden(x, n, ("v1", j), 64, None)
                            load_l2(("v2", j), 64)
                            self.load_bc(brow, self.inp["rwkv_v0"], self.inp["rwkv_v0"].h[j - 1, :])
                        for n0 in range(0, D, 512):
                            w = load_w(key, n0)
                            for t in range(nt):
                                p = nextps()
                                b.mm(p, [(x[:, kt, t * 128:(t + 1) * 128], w[:, kt, :]) for kt in range(KT)])
                                e = nextev()
                                b.copy("act", e, p)
                                if vres:
                                    p2 = lora_out(t, n0, 64)
                                    e2 = nextev()
                                    b.tt("dve", e2, p2, brow[:, n0:n0 + 512], ALU.add)
                                    b.act(e2, e2, AF.Sigmoid)
                                    b.dma(vft, scr["vf"][rows(tok0, t), n0:n0 + 512], q="act")
                                    b.tt("dve", vft, vft, e, ALU.subtract)
                                    b.tt("dve", vft, vft, e2, ALU.mult)
                                    b.tt("dve", e, e, vft, ALU.add)
                                b.dma(dst[rows(tok0, t), n0:n0 + 512], e, q="pool")
                    for nm in ("w", "a"):
                        x = build_xm(MIX[nm], tok0, n, is_ctx)
                        for z in range(2):
                            lora_hidden(x, n, (nm + "1", j, z), 96, AF.Tanh if nm == "w" else None)
                            load_l2((nm + "2", j, z), 96)
                            src0 = self.inp["rwkv_w0" if nm == "w" else "rwkv_a0"]
                            self.load_bc(brow, src0, src0.h[j, z, :])
                            for n0 in range(0, D, 512):
                                for t in range(nt):
                                    p2 = lora_out(t, n0, 96)
                                    e = nextev()
                                    b.tt("dve", e, p2, brow[:, n0:n0 + 512], ALU.add)
                                    b.act(e, e, AF.Sigmoid)
                                    if nm == "w":
                                        if self.chunked:
                                            b.ts("dve", e, e, NEG_EXP_HALF, ALU.mult)
                                        else:
                                            b.act(e, e, AF.Exp, scale=NEG_EXP_HALF)
                                    b.dma(scr[nm + str(z)][rows(tok0, t), n0:n0 + 512], e, q="pool")
                    x = build_xm(MIX["g"], tok0, n, is_ctx)
                    lora_hidden(x, n, ("g1", j), 256, AF.Sigmoid)
                    load_l2(("g2", j), 256)
                    for n0 in range(0, D, 512):
                        for t in range(nt):
                            p2 = lora_out(t, n0, 256)
                            e = nextev()
                            b.copy("act", e, p2)
                            b.dma(scr["g"][rows(tok0, t), n0:n0 + 512], e, q="pool")

    def lane_ap(self, buf, bb, tok_first, step, nt):
        c = self.c
        D = c["D"]
        h = buf.h
        off = h.offset + (bb * c["S"] + tok_first) * D
        return View(buf, bass.AP(h.tensor, off, [[64, c["RH"]], [step * D, nt], [1, 64]]))

    def rwkv_scan(self, i, j):
        b, c = self.b, self.c
        NB, RH, L, T, S = c["NB"], c["RH"], c["L"], c["T"], c["S"]
        NL = 2 * NB * RH
        TC = 32
        scr = self.rscr
        vsrc = scr["vf"] if j == 0 else scr["v"]
        with b.scope():
            St = b.sbuf("sS", [NL, 64, 64], F32)
            tmp = b.sbuf("stmp", [NL, 64, 64], F32)
            vk = [b.sbuf(f"svk{k}", [NL, 64, 64], F32) for k in range(2)]
            sz = b.sbuf("ssz", [NL, 64], F32)
            inb = [{nm: b.sbuf(f"s{nm}{k}", [NL, TC, 64], F32) for nm in ("R", "K", "V", "W", "A")} for k in range(2)]
            Zb = b.sbuf("sZ", [NL, TC, 64], F32)
            T2 = b.sbuf("sT2", [NL, TC, 64], F32)
            yb = [b.sbuf(f"sy{k}", [NL, TC, 64], F32) for k in range(2)]
            ssq = b.sbuf("sssq", [NL, TC], F32)
            kk_t = b.sbuf("skk", [NL, 64], F32)
            ka_t = b.sbuf("ska", [NL, 64], F32)
            oka_t = b.sbuf("soka", [NL, 64], F32)
            for z in range(2):
                for bb in range(NB):
                    p0 = (z * NB + bb) * RH
                    b.dma(kk_t[p0:p0 + RH, :], View(self.inp["rwkv_k_k"], self.inp["rwkv_k_k"].h[j].rearrange("(h n) -> h n", n=64)))
                    b.dma(ka_t[p0:p0 + RH, :], View(self.inp["rwkv_k_a"], self.inp["rwkv_k_a"].h[j].rearrange("(h n) -> h n", n=64)))
            b.ts("dve", oka_t, ka_t, -1.0, ALU.mult, 1.0, ALU.add)
            b.memset("dve", St, 0.0)
            shp3 = [NL, 64, 64]
            shpc = [NL, TC, 64]
            nch = S // TC

            def chunk_src(ci, z):
                s0 = ci * TC
                if z == 0:
                    return s0, 1
                if s0 < L:
                    return L - 1 - s0, -1
                return L + T - 1 - (s0 - L), -1

            def load_chunk(ci):
                bufs = inb[ci % 2]
                for z in range(2):
                    tf, step = chunk_src(ci, z)
                    for bb in range(NB):
                        p0 = (z * NB + bb) * RH
                        for nm, src in (("R", scr["r"]), ("K", scr["k"]), ("V", vsrc), ("W", scr[f"w{z}"]), ("A", scr[f"a{z}"])):
                            b.dma(bufs[nm][p0:p0 + RH, :, :], self.lane_ap(src, bb, tf, step, TC))

            load_chunk(0)
            kv = 0
            for ci in range(nch):
                if ci + 1 < nch:
                    load_chunk(ci + 1)
                bufs = inb[ci % 2]
                Rb, Kb, Vb, Wb, Ab = (bufs[nm] for nm in ("R", "K", "V", "W", "A"))
                y = yb[ci % 2]
                b.tt("dve", Zb, Kb, kk_t.v.us(1).bc(shpc), ALU.mult)
                b.tt("dve", T2, Zb, Zb, ALU.mult)
                b.red(ssq, T2)
                b.act(ssq, ssq, AF.Sqrt)
                b.ts("dve", ssq, ssq, 1e-12, ALU.max)
                b.op("dve", lambda en: en.reciprocal(out=ssq.h[:], in_=ssq.h[:]), w=[ssq], r=[ssq])
                b.ts("dve", ssq, ssq, -1.0, ALU.mult)
                b.tt("dve", Zb, Zb, ssq.v.us(2).bc(shpc), ALU.mult)
                b.tt("dve", T2, Ab, ka_t.v.us(1).bc(shpc), ALU.mult)
                b.tt("dve", T2, T2, oka_t.v.us(1).bc(shpc), ALU.add)
                b.tt("dve", Kb, Kb, T2, ALU.mult)
                b.stt("dve", Ab, Zb, -1.0, Ab, ALU.mult, ALU.mult)
                for t in range(TC):
                    zt = Zb[:, t, :].us(1).bc(shp3)
                    wt = Wb[:, t, :].us(1).bc(shp3)
                    bt = Ab[:, t, :].us(1).bc(shp3)
                    kt_ = Kb[:, t, :].us(1).bc(shp3)
                    rt = Rb[:, t, :].us(1).bc(shp3)
                    vt = Vb[:, t, :].us(2).bc(shp3)
                    vkb = vk[kv % 2]
                    kv += 1
                    b.tt("pool", vkb, vt, kt_, ALU.mult)
                    b.tt("dve", tmp, St, zt, ALU.mult)
                    b.red(sz, tmp)
                    b.tt("dve", St, St, wt, ALU.mult)
                    b.tt("dve", tmp, sz.v.us(2).bc(shp3), bt, ALU.mult)
                    b.tt("dve", St, St, tmp, ALU.add)
                    b.tt("dve", St, St, vkb, ALU.add)
                    b.tt("dve", tmp, St, rt, ALU.mult)
                    b.red(y[:, t, :], tmp)
                for z in range(2):
                    tf, step = chunk_src(ci, z)
                    for bb in range(NB):
                        p0 = (z * NB + bb) * RH
                        b.dma(self.lane_ap(scr[f"y{z}"], bb, tf, step, TC), y[p0:p0 + RH, :, :], q="act")

    def rwkv_scan_chunked(self, i, j):
        b, c = self.b, self.c
        NB, RH, L, T, S, D = c["NB"], c["RH"], c["L"], c["T"], c["S"], c["D"]
        C = CHUNK
        HH = min(8, RH)
        HW = HH * 64
        NL = 2 * HH
        GL = 8
        NGR = NL // GL
        scr = self.rscr
        vsrc = scr["vf"] if j == 0 else scr["v"]
        nch = S // C
        with b.scope():
            cm = b.sbuf("kcm", [64, 7, 64], F32)
            b.dma(cm, self.inp["rconst"])
            cm2 = b.sbuf("kcm2", [64, 2, 128], F32)
            b.dma(cm2, self.inp["rconst2"])
            idb = b.sbuf("kidb", [64, 64], BF16)
            b.copy("dve", idb, cm[:, 6, :])
            onec = b.sbuf("kone", [64, 1], F32)
            b.memset("dve", onec, 1.0)
            kkB = b.sbuf("kkkB", [64, D], F32)
            kaB = b.sbuf("kkaB", [64, D], F32)
            okaB = b.sbuf("kokaB", [64, D], F32)
            self.load_bc(kkB, self.inp["rwkv_k_k"], self.inp["rwkv_k_k"].h[j, :])
            self.load_bc(kaB, self.inp["rwkv_k_a"], self.inp["rwkv_k_a"].h[j, :])
            b.ts("dve", okaB, kaB, -1.0, ALU.mult, 1.0, ALU.add)
            ST = b.sbuf("kST", [64, NL, 64], F32)
            STb = b.sbuf("kSTb", [64, NL, 64], BF16)
            names = ("R", "K", "V", "A", "W")
            ld = [{nm: b.sbuf(f"k{nm}{k}", [64, 2, HW], F32) for nm in names} for k in range(2)]
            T1 = b.sbuf("kT1", [64, 2, HW], F32)
            T2 = b.sbuf("kT2", [64, 2, HW], F32)
            ssq = b.sbuf("kssq", [64, 2 * HH], F32)
            tm = [{nm: b.sbuf(f"k{nm}b{k}", [64, 2, HW], BF16) for nm in ("Zt", "Rt", "Bt", "Kt", "Vb", "Bh", "Kh")}
                  for k in range(2)]
            FMs = [b.sbuf(f"kFM{k}", [64, NL, 4, 64], BF16) for k in range(2)]
            WcCs = [b.sbuf(f"kWcC{k}", [64, NL], F32) for k in range(2)]
            yts = [b.sbuf(f"kyt{k}", [64, 2, HW], F32) for k in range(2)]
            sets = []
            for k in range(2):
                sets.append(dict(
                    X=b.psum(f"kX{k}", [64, GL, 128], F32), Y=b.psum(f"kY{k}", [64, GL, 64], F32),
                    PX=[b.sbuf(f"kPX{k}{q}", [64, GL, 128], F32) for q in range(2)],
                    Q=[b.sbuf(f"kQ{k}{q}", [64, GL, 64], F32) for q in range(2)],
                    SAbr=b.sbuf(f"kSAbr{k}", [64, GL, 64], BF16), SAk=b.sbuf(f"kSAk{k}", [64, GL, 128], BF16),
                    WT=b.sbuf(f"kWT{k}", [64, GL, 64], F32), UT=b.sbuf(f"kUT{k}", [64, GL, 64], BF16)))
            pt = b.psum("kpt", [64, 512], F32)
            ptp = b.psum("kptp", [64, 16, 64], BF16)
            shp = [64, 2, HW]
            lshp = [64, GL, 64]

            NCc, NCl = L // C, T // C

            def chunk_src(cs, z):
                if z == 0:
                    cn = cs
                else:
                    cn = (NCc - 1 - cs) if cs < NCc else (NCc + NCl - 1 - (cs - NCc))
                return cn * C, 1

            def dram_rows(buf, bb, tf, step, c0):
                return buf[bb * S + tf: bb * S + tf + C, c0:c0 + HW]

            for bb in range(NB):
                for hq in range(RH // HH):
                    c0 = hq * HW
                    b.memset("dve", ST, 0.0)
                    b.memset("pool", STb, 0.0)
                    kkb = kkB[:, c0:c0 + HW].us(1).bc(shp)
                    kab = kaB[:, c0:c0 + HW].us(1).bc(shp)
                    okab = okaB[:, c0:c0 + HW].us(1).bc(shp)

                    def load(cs):
                        bufs = ld[cs % 2]
                        for z in range(2):
                            tf, step = chunk_src(cs, z)
                            for nm, src in (("R", scr["r"]), ("K", scr["k"]), ("V", vsrc), ("A", scr[f"a{z}"]), ("W", scr[f"w{z}"])):
                                b.dma(bufs[nm][:, z, :], dram_rows(src, bb, tf, step, c0), q="sp" if nm in ("R", "K", "V") else "act")

                    def tri(kind, src, outs):
                        for z in range(2):
                            for n0 in range(0, HW, 512):
                                n = min(512, HW - n0)
                                b.mm(pt[:, 0:n], [(cm[:, kind * 2 + z, :], src[:, z, n0:n0 + n])])
                                for dst, sc in outs:
                                    b.act(dst[:, z, n0:n0 + n], pt[:, 0:n], AF.Exp, scale=sc)

                    def prep_gen(cs):
                        bufs = ld[cs % 2]
                        Rl, Kl, Vl, Al, Wl = (bufs[nm] for nm in names)
                        o = tm[cs % 2]
                        FM, WcC = FMs[cs % 2], WcCs[cs % 2]
                        h3 = lambda t_: View(t_, t_.h[:].rearrange("p z (h n) -> p (z h) n", n=64))
                        b.tt("pool", T1, Kl, kkb, ALU.mult)
                        b.tt("pool", T2, T1, T1, ALU.mult)
                        b.red(ssq, h3(T2))
                        b.act(ssq, ssq, AF.Sqrt)
                        b.ts("dve", ssq, ssq, 1e-12, ALU.max)
                        b.op("dve", lambda en: en.reciprocal(out=ssq.h[:], in_=ssq.h[:]), w=[ssq], r=[ssq])
                        b.tt("dve", h3(T1), h3(T1), ssq.v.us(2).bc([64, 2 * HH, 64]), ALU.mult)
                        yield
                        tri(0, Wl, [(T2, 1.0)])
                        yield
                        b.stt("dve", o["Zt"], T1, -1.0, T2, ALU.mult, ALU.mult)
                        b.tt("dve", T1, T1, Al, ALU.mult)
                        yield
                        b.tt("pool", Al, Al, kab, ALU.mult)
                        b.tt("pool", Al, Al, okab, ALU.add)
                        b.tt("pool", Kl, Kl, Al, ALU.mult)
                        yield
                        tri(1, Wl, [(Al, 1.0), (T2, -1.0)])
                        yield
                        b.tt("dve", o["Rt"], Rl, Al, ALU.mult)
                        b.tt("dve", o["Bt"], T1, T2, ALU.mult)
                        b.tt("pool", o["Kt"], Kl, T2, ALU.mult)
                        yield
                        tri(2, Wl, [(Al, 1.0)])
                        yield
                        b.tt("dve", o["Bh"], T1, Al, ALU.mult)
                        b.tt("pool", o["Kh"], Kl, Al, ALU.mult)
                        b.copy("pool", o["Vb"], Vl)
                        yield
                        for z in range(2):
                            for hh in range(HH):
                                ln = hh * 2 + z
                                b.mm(pt[:, ln:ln + 1], [(Wl[:, z, hh * 64:(hh + 1) * 64], onec)])
                        b.act(WcC, pt[:, 0:NL], AF.Exp)
                        yield
                        ke = 0
                        for xi, nm in enumerate(("Zt", "Rt", "Bt", "Kt")):
                            for h0 in range(0, HH, 8):
                                nh = min(8, HH - h0)
                                for hh in range(nh):
                                    for z in range(2):
                                        b.transpose(ptp[:, hh * 2 + z, :], o[nm][:, z, (h0 + hh) * 64:(h0 + hh + 1) * 64], idb)
                                b.copy(("act", "dve")[ke % 2], FM[:, h0 * 2:(h0 + nh) * 2, xi, :], ptp[:, 0:nh * 2, :])
                                ke += 1
                                yield

                    def lane_gen(cs, g):
                        o = tm[cs % 2]
                        FM, WcC = FMs[cs % 2], WcCs[cs % 2]
                        yt = yts[cs % 2]
                        y5 = yt.h[:].rearrange("p z (h v) -> p h z v", v=64)
                        tok = lambda nm, l: o[nm][:, l % 2, (l // 2) * 64:(l // 2 + 1) * 64]
                        if True:
                            s_ = sets[g % 2]
                            X, Y, PX, Q, SAbr, SAk, WT, UT = (s_[k_] for k_ in ("X", "Y", "PX", "Q", "SAbr", "SAk", "WT", "UT"))
                            l0 = g * GL
                            fm = lambda l, xi: FM[:, l0 + l, xi, :]
                            zr = lambda l: View(FM, FM.h[:, l0 + l, 0:2, :].rearrange("p a t -> p (a t)"))
                            lz = lambda v_: View(v_.buf, v_.ap.rearrange("p (h z) t -> p h z t", z=2))
                            mk = lambda k_: cm[:, 2 * k_:2 * k_ + 2, :].us(1).bc([64, GL // 2, 2, 64])
                            m2 = cm2.v.us(1).bc([64, GL // 2, 2, 128])
                            for l in range(GL):
                                b.mm(X[:, l, :], [(fm(l, 2), zr(l))])
                            b.tt("dve", lz(PX[0][:, :, 0:64]), lz(X[:, :, 0:64]), mk(0), ALU.mult)
                            b.tt("dve", lz(SAbr.v), lz(X[:, :, 64:128]), mk(1), ALU.mult)
                            b.copy("pool", PX[0][:, :, 64:128], cm[:, 6, :].us(1).bc(lshp))
                            yield
                            for l in range(GL):
                                b.mm(X[:, l, :], [(fm(l, 3), zr(l))])
                            b.tt("dve", lz(SAk.v), lz(X.v), m2, ALU.mult)
                            yield
                            for l in range(GL):
                                b.mm(Y[:, l, :], [(fm(l, 0), fm(l, 2))])
                            b.tt("dve", lz(Q[0].v), lz(Y.v), mk(2), ALU.mult)
                            yield
                            for k in range(6):
                                pc, pn = PX[k % 2], PX[(k + 1) % 2]
                                qc, qn = Q[k % 2], Q[(k + 1) % 2]
                                lastk = k == 5
                                for l in range(GL):
                                    if lastk:
                                        b.mm(X[:, l, 64:128], [(qc[:, l, :], pc[:, l, 64:128])])
                                    else:
                                        b.mm(X[:, l, :], [(qc[:, l, :], pc[:, l, :])])
                                if not lastk:
                                    for l in range(GL):
                                        b.mm(Y[:, l, :], [(pc[:, l, 0:64], qc[:, l, :])])
                                    b.copy("act", pn[:, :, 0:64], X[:, :, 0:64])
                                    b.copy("act", qn, Y)
                                b.tt("dve", pn[:, :, 64:128], pc[:, :, 64:128], X[:, :, 64:128], ALU.add)
                                yield
                            Tm = PX[0]
                            for l in range(GL):
                                b.mm(Y[:, l, :], [(fm(l, 0), STb[:, l0 + l, :]), (SAk[:, l, 0:64], tok("Vb", l0 + l))])
                            b.copy("act", WT, Y)
                            yield
                            for l in range(GL):
                                b.mm(Y[:, l, :], [(Tm[:, l, 64:128], WT[:, l, :])])
                            b.copy("dve", UT, Y)
                            yield
                            for l in range(GL):
                                b.mm(Y[:, l, :], [(fm(l, 1), STb[:, l0 + l, :]), (SAbr[:, l, :], UT[:, l, :]),
                                                  (SAk[:, l, 64:128], tok("Vb", l0 + l))])
                            hh0 = l0 // 2
                            b.copy("act", View(yt, y5[:, hh0:hh0 + GL // 2, :, :]),
                                   View(Y, Y.h[:].rearrange("p (h z) v -> p h z v", z=2)))
                            yield
                            for l in range(GL):
                                b.mm(Y[:, l, :], [(tok("Bh", l0 + l), UT[:, l, :]), (tok("Kh", l0 + l), tok("Vb", l0 + l))])
                            b.tt("pool", ST[:, l0:l0 + GL, :], ST[:, l0:l0 + GL, :], WcC[:, l0:l0 + GL].us(2).bc(lshp), ALU.mult)
                            b.tt("dve", ST[:, l0:l0 + GL, :], ST[:, l0:l0 + GL, :], Y, ALU.add)
                            b.copy("act", STb[:, l0:l0 + GL, :], ST[:, l0:l0 + GL, :])
                            yield

                    def drive(gens):
                        gens = list(gens)
                        while gens:
                            for g_ in list(gens):
                                try:
                                    next(g_)
                                except StopIteration:
                                    gens.remove(g_)

                    load(0)
                    drive([prep_gen(0)])
                    for cs in range(nch):
                        gens = [lane_gen(cs, g) for g in range(NGR)]
                        if cs + 1 < nch:
                            load(cs + 1)
                            gens.append(prep_gen(cs + 1))
                        drive(gens)
                        yt = yts[cs % 2]
                        for z in range(2):
                            tf, step = chunk_src(cs, z)
                            b.dma(dram_rows(scr[f"y{z}"], bb, tf, step, c0), yt[:, z, :], q="pool")

    def rwkv_readout(self, i, j):
        b, c = self.b, self.c
        D, KT, RH = c["D"], c["KT"], c["RH"]
        last = i == c["DEPTH"] - 1
        scr = self.rscr
        vsrc = scr["vf"] if j == 0 else scr["v"]
        shp = [128, RH, 64]
        with b.scope():
            names = ("y0", "y1", "r", "k", "v", "g", "a0", "a1")
            ld = {nm: b.sbuf("q" + nm, [128, D], F32) for nm in names}
            srcs = dict(y0=scr["y0"], y1=scr["y1"], r=scr["r"], k=scr["k"], v=vsrc, g=scr["g"], a0=scr["a0"], a1=scr["a1"])
            lnw = b.sbuf("qlnw", [128, D], F32)
            lnb = b.sbuf("qlnb", [128, D], F32)
            kab = b.sbuf("qka", [128, D], F32)
            ka2 = b.sbuf("qka2", [128, D], F32)
            rkb = b.sbuf("qrk", [128, D], F32)
            G = b.sbuf("qG", [128, D], F32)
            self.load_bc(lnw, self.inp["rwkv_ln_w"], self.inp["rwkv_ln_w"].h[j, :])
            self.load_bc(lnb, self.inp["rwkv_ln_b"], self.inp["rwkv_ln_b"].h[j, :])
            self.load_bc(kab, self.inp["rwkv_k_a"], self.inp["rwkv_k_a"].h[j, :])
            self.load_bc(rkb, self.inp["rwkv_r_k"], self.inp["rwkv_r_k"].h[j].rearrange("h n -> (h n)"))
            b.ts("dve", ka2, kab, -2.0, ALU.mult, 2.0, ALU.add)
            st1 = b.sbuf("qst1", [128, RH], F32)
            st2 = b.sbuf("qst2", [128, RH], F32)
            gneps = b.sbuf("qeps", [128, 1], F32)
            b.memset("dve", gneps, 64 * 1e-5)
            ob = b.sbuf("qob", [128, D], BF16)
            oT = b.sbuf("qoT", [128, KT, 512], BF16)
            wch = [b.sbuf(f"qw{k}", [128, KT, 512], BF16) for k in range(2)]
            ot = [b.sbuf(f"qot{k}", [128, 512], F32) for k in range(2)]
            og = [b.sbuf(f"qog{k}", [128, 512], F32) for k in range(2)]
            tps = [b.psum(f"qtp{k}", [128, 4, 128], BF16) for k in range(2)]
            ps = [b.psum(f"qps{k}", [128, 512], F32) for k in range(4)]
            cur_g = None
            kw = 0
            kp = 0
            v3 = lambda t_: View(t_, t_.h[:].rearrange("p (h n) -> p h n", n=64))
            for (r0, ntok, mg, is_ctx, bb) in self.groups(512, include_ctx=not last):
                nt = ntok // 128
                if mg != cur_g:
                    mr = self.modrows
                    self.load_bc(G, mr, mr.h[i, mg, 2 * D:3 * D])
                    cur_g = mg
                for t in range(nt):
                    rr = slice(r0 + t * 128, r0 + (t + 1) * 128)
                    for nm in names:
                        b.dma(ld[nm], srcs[nm][rr, :], q="sp" if nm in ("y0", "r", "v", "a0") else "act")
                    y0, y1, r_, k_, v_, g_, a0, a1 = (ld[nm] for nm in names)
                    b.tt("dve", y0, y0, y1, ALU.add)
                    b.red(st1, v3(y0))
                    b.ts("dve", st1, st1, 1.0 / 64, ALU.mult)
                    b.tt("dve", v3(y0), v3(y0), st1.v.us(2).bc(shp), ALU.subtract)
                    b.tt("pool", y1, y0, y0, ALU.mult)
                    b.red(st2, v3(y1))
                    b.act(st2, st2, AF.Sqrt, bias=gneps[:, 0:1], scale=1.0 / 64)
                    b.op("dve", lambda en: en.reciprocal(out=st2.h[:], in_=st2.h[:]), w=[st2], r=[st2])
                    b.tt("dve", v3(y0), v3(y0), st2.v.us(2).bc(shp), ALU.mult)
                    b.tt("dve", y0, y0, lnw, ALU.mult)
                    b.tt("dve", y0, y0, lnb, ALU.add)
                    b.tt("pool", a0, a0, a1, ALU.add)
                    b.tt("pool", a0, a0, kab, ALU.mult)
                    b.tt("pool", a0, a0, ka2, ALU.add)
                    b.tt("pool", k_, k_, a0, ALU.mult)
                    b.tt("pool", r_, r_, k_, ALU.mult)
                    b.tt("dve", r_, r_, rkb, ALU.mult)
                    b.red(st1, v3(r_))
                    b.tt("dve", v3(v_), v3(v_), st1.v.us(2).bc(shp), ALU.mult)
                    b.tt("dve", y0, y0, v_, ALU.add)
                    b.tt("dve", ob, y0, g_, ALU.mult)
                    self.transpose_tile(ob, oT, t * 128, tps, t)
                for n0 in range(0, D, 512):
                    w = wch[kw % 2]
                    kw += 1
                    b.dma(w, self.wb[("w_o", j)][:, :, n0:n0 + 512])
                    for t in range(nt):
                        p = ps[kp % 4]
                        kp += 1
                        b.mm(p, [(oT[:, kt, t * 128:(t + 1) * 128], w[:, kt, :]) for kt in range(KT)])
                        o, gg = ot[t % 2], og[t % 2]
                        rows = self.xs[r0 + t * 128:r0 + (t + 1) * 128, n0:n0 + 512]
                        b.dma(o, rows, q="act")
                        b.tt("dve", gg, p, G[:, n0:n0 + 512], ALU.mult)
                        b.tt("pool", o, o, gg, ALU.add)
                        b.dma(rows, o, q="pool")

    def build_hT(self, i, sub, bb, hT):
        b, c = self.b, self.c
        D, L, T, S = c["D"], c["L"], c["T"], c["S"]
        base = bb * S
        with b.scope():
            A = b.sbuf("hA", [128, D], F32)
            Bsh = b.sbuf("hB", [128, D], F32)
            xt = [b.sbuf(f"hx{k}", [128, D], F32) for k in range(2)]
            hf = b.sbuf("hhf", [128, D], F32)
            tmpm = hf
            hb = [b.sbuf(f"hhb{k}", [128, D], BF16) for k in range(2)]
            st = b.sbuf("hst", [128, 2], F32)
            tps = [b.psum(f"htp{k}", [128, 4, 128], BF16) for k in range(2)]
            for (g, t0, n) in ((c["NB"], 0, L), (bb, L, T)):
                self.mod_tiles(i, sub, g, A, Bsh, None, tmpm)
                for t in range(n // 128):
                    x, h = xt[t % 2], hb[t % 2]
                    b.dma(x, self.xs[base + t0 + t * 128: base + t0 + (t + 1) * 128, :])
                    self.norm_tile(x, A, Bsh, h, h, hf, st)
                    self.transpose_tile(h, hT, t0 + t * 128, tps, t)

    def mlstm(self, i):
        c = self.c
        j = i // 2
        if not hasattr(self, "mscr"):
            b = self.b
            R, D = c["R"], c["D"]
            self.mscr = dict(v=b.dram("ms_v", [R, D], BF16), o=b.dram("ms_o", [R, D], F32),
                             h0=b.dram("ms_h0", [R, D], F32), h1=b.dram("ms_h1", [R, D], F32),
                             dec=b.dram("ms_dec", [(32 + c["MH"]) * (c["S"] // CHUNK)], F32))
        for bb in range(c["NB"]):
            self.mlstm_seq(i, j, bb)
        self.mlstm_readout(i, j)

    def mlstm_seq(self, i, j, bb):
        b, c = self.b, self.c
        D, KT, L, T, S, MH, DK, DV, QK = c["D"], c["KT"], c["L"], c["T"], c["S"], c["MH"], c["DK"], c["DV"], c["QK"]
        assert DK == 128
        base = bb * S
        scr = self.mscr
        w_in = self.wb[("m_in", j)]
        NL = 2 * MH
        NLP = 32 + MH
        NC = S // CHUNK
        NCc = L // CHUNK
        NCl = T // CHUNK
        NR = T // GRID_W
        G0 = 2 * QK + 2 * D
        NG = 4 * MH
        with b.scope():
            qT = b.sbuf("mqT", [128, MH, S], BF16)
            kT = b.sbuf("mkT", [128, MH, S], BF16)
            GTall = b.sbuf("mGT", [NG, S], F32)
            with b.scope():
                hT = b.sbuf("mhT", [128, KT, S], BF16)
                self.build_hT(i, 0, bb, hT)
                NCT = 2 * QK // 128
                cw = b.sbuf("mcw", [128, NCT, 10], F32)
                with b.scope():
                    crow = b.sbuf("mcrow", [10, 2 * QK], F32)
                    b.dma(crow[0:9, :], View(self.inp["mlstm_conv_w"], self.inp["mlstm_conv_w"].h[j].rearrange("a b c -> (a b) c")))
                    b.dma(crow[9:10, :], View(self.inp["mlstm_conv_b"], self.inp["mlstm_conv_b"].h[j:j + 1, :]))
                    pcw = b.psum("mpcw", [128, 512], F32)
                    for ct in range(NCT):
                        b.transpose(pcw[:, 0:10], crow[:, ct * 128:(ct + 1) * 128], self.ident_f[0:10, 0:10])
                        b.copy("dve", cw[:, ct, :], pcw[:, 0:10])
                pre = b.sbuf("mpre", [128, S], F32)
                acc = b.sbuf("macc", [128, S], F32)
                wq = [b.sbuf(f"mwq{k}", [128, KT, 128], BF16) for k in range(2)]
                pp = [b.psum(f"mpp{k}", [128, 512], F32) for k in range(2)]
                kp = 0
                for ct in range(NCT):
                    w = wq[ct % 2]
                    b.dma(w, w_in[:, :, ct * 128:(ct + 1) * 128])
                    for t0 in range(0, S, 512):
                        n = min(512, S - t0)
                        p = pp[kp % 2]
                        kp += 1
                        b.mm(p[:, 0:n], [(w[:, kt, :], hT[:, kt, t0:t0 + n]) for kt in range(KT)])
                        b.copy("act", pre[:, t0:t0 + n], p[:, 0:n])
                    wv = lambda dy, dx: cw[:, ct, dy * 3 + dx: dy * 3 + dx + 1]
                    b.ts("dve", acc, pre, wv(1, 1), ALU.mult, cw[:, ct, 9:10], ALU.add)
                    b.stt("dve", acc[:, 1:L], pre[:, 0:L - 1], wv(1, 0), acc[:, 1:L], ALU.mult, ALU.add)
                    b.stt("dve", acc[:, 0:L - 1], pre[:, 1:L], wv(1, 2), acc[:, 0:L - 1], ALU.mult, ALU.add)
                    a3 = lambda t_: t_.h[:, L:S].rearrange("p (r c) -> p r c", c=GRID_W)
                    for dy in range(3):
                        for dx in range(3):
                            if dy == 1 and dx == 1:
                                continue
                            r_lo, r_hi = max(0, 1 - dy), min(NR, NR + 1 - dy)
                            c_lo, c_hi = max(0, 1 - dx), min(GRID_W, GRID_W + 1 - dx)
                            dst = View(acc, a3(acc)[:, r_lo:r_hi, c_lo:c_hi])
                            src = View(pre, a3(pre)[:, r_lo + dy - 1:r_hi + dy - 1, c_lo + dx - 1:c_hi + dx - 1])
                            b.stt("dve", dst, src, wv(dy, dx), dst, ALU.mult, ALU.add)
                    if ct < NCT // 2:
                        b.act(pre, acc, AF.Silu)
                        b.ts("pool", qT[:, ct, :], pre, float(DK) ** -0.5, ALU.mult)
                    else:
                        b.act(kT[:, ct - NCT // 2, :], acc, AF.Silu)
                NCW = 256
                wch = [b.sbuf(f"mwch{k}", [128, KT, NCW], BF16) for k in range(2)]
                evb = [b.sbuf(f"mevb{k}", [128, NCW], BF16) for k in range(2)]
                evf = [b.sbuf(f"mevf{k}", [128, NCW], F32) for k in range(2)]
                pv = [b.psum(f"mpv{k}", [128, 512], F32) for k in range(3)]
                kw = 0
                kq = 0
                for which in ("v", "o"):
                    c0 = 2 * QK + (0 if which == "v" else D)
                    for n0 in range(0, D, NCW):
                        w = wch[kw % 2]
                        kw += 1
                        b.dma(w, w_in[:, :, c0 + n0:c0 + n0 + NCW])
                        for t in range(S // 128):
                            p = pv[kq % 3]
                            b.mm(p[:, 0:NCW], [(hT[:, kt, t * 128:(t + 1) * 128], w[:, kt, :]) for kt in range(KT)])
                            rows = slice(base + t * 128, base + (t + 1) * 128)
                            if which == "v":
                                e = evb[kq % 2]
                                b.copy("act", e, p[:, 0:NCW])
                                b.dma(scr["v"][rows, n0:n0 + NCW], e, q="pool")
                            else:
                                e = evf[kq % 2]
                                b.act(e, p[:, 0:NCW], AF.Sigmoid)
                                b.dma(scr["o"][rows, n0:n0 + NCW], e, q="pool")
                            kq += 1
                wg = b.sbuf("mwg", [128, KT, NG], BF16)
                b.dma(wg, w_in[:, :, G0:G0 + NG])
                bg = b.sbuf("mbg", [128, NG], F32)
                self.load_bc(bg, self.inp["mlstm_b_gate"], self.inp["mlstm_b_gate"].h[j].rearrange("z f h -> (z f h)"))
                gt = [b.sbuf(f"mgt{k}", [128, NG], F32) for k in range(2)]
                pgt = b.psum("mpgt", [128, 512], F32)
                for t in range(S // 128):
                    p = pv[t % 3]
                    g = gt[t % 2]
                    b.mm(p[:, 0:NG], [(hT[:, kt, t * 128:(t + 1) * 128], wg[:, kt, :]) for kt in range(KT)])
                    b.tt("dve", g, p[:, 0:NG], bg, ALU.add)
                    b.act(g, g, AF.Tanh, scale=1.0 / GATE_CAP)
                    b.ts("dve", g, g, GATE_CAP, ALU.mult)
                    b.transpose(pgt[0:NG, 0:128], g, self.ident_f)
                    b.copy("dve", GTall[:, t * 128:(t + 1) * 128], pgt[0:NG, 0:128])
            self.mnegM = b.sbuf("mnegM", [NLP, S], F32)
            self.mcol = [b.sbuf(f"mcol{k}", [64, NC, NLP], F32) for k in range(4)]
            self.mdecB = b.sbuf("mdecB", [128, NLP * NC], F32)
            with b.scope():
                bufs = [b.sbuf(f"gb{k}", [NLP, S], F32) for k in range(7)]
                IGn, LFn, IG, LF, X2, natA, natB = bufs
                b.memset("dve", IGn, 0.0)
                b.memset("pool", LFn, 0.0)
                for z in range(2):
                    b.dma(IGn[z * 32:z * 32 + MH, :], GTall[z * 2 * MH:z * 2 * MH + MH, :])
                    b.dma(LFn[z * 32:z * 32 + MH, :], GTall[z * 2 * MH + MH:(z + 1) * 2 * MH, :])

                def rev(t_, s0, n):
                    h = t_.h[32:NLP, :]
                    return View(t_, bass.AP(h.tensor, h.offset + s0 + n - 1, [list(h.ap[0]), [-1, n]]))

                def to_scan(dst, src, eng):
                    b.copy(eng, dst[0:32, :], src[0:32, :])
                    for (s0, n) in ((0, L), (L, T)):
                        b.copy(eng, dst[32:NLP, s0:s0 + n], rev(src, s0, n))

                to_scan(IG, IGn, "dve")
                to_scan(LF, LFn, "pool")
                b.act(LF, LF, AF.Exp, scale=-1.0)
                b.ts("dve", LF, LF, 1.0, ALU.add)
                b.act(LF, LF, AF.Ln)
                b.ts("dve", LF, LF, -1.0, ALU.mult)
                one = b.sbuf("gone", [NLP, 1], F32)
                b.memset("dve", one, 1.0)
                Gc = IGn
                onesb = one.v.bc([NLP, S])
                b.op("dve", lambda en: en.tensor_tensor_scan(out=Gc.h[:], data0=onesb.ap, data1=LF.h[:], initial=0.0,
                                                             op0=ALU.mult, op1=ALU.add), w=[Gc], r=[one, LF])
                c3 = lambda t_: View(t_, t_.h[:].rearrange("p (c t) -> p c t", t=CHUNK))
                shp = [NLP, NC, CHUNK]
                Gs = b.sbuf("gGs", [NLP, NC], F32)
                b.memset("dve", Gs[:, 0:1], 0.0)
                b.copy("dve", Gs[:, 1:NC], c3(Gc)[:, 0:NC - 1, CHUNK - 1])
                bcum = Gc
                b.tt("dve", c3(bcum), c3(Gc), Gs.v.us(2).bc(shp), ALU.subtract)
                gq = IG
                b.tt("dve", gq, IG, bcum, ALU.subtract)
                cur, nxt = LF, LFn
                b.copy("dve", cur, gq)
                s_ = 1
                while s_ < CHUNK:
                    b.copy("pool", c3(nxt)[:, :, 0:s_], c3(cur)[:, :, 0:s_])
                    b.tt("dve", c3(nxt)[:, :, s_:CHUNK], c3(cur)[:, :, s_:CHUNK], c3(cur)[:, :, 0:CHUNK - s_], ALU.max)
                    cur, nxt = nxt, cur
                    s_ *= 2
                cm, spare = cur, nxt
                gmax = b.sbuf("ggmax", [NLP, NC], F32)
                bend = b.sbuf("gbend", [NLP, NC], F32)
                b.copy("dve", gmax, c3(cm)[:, :, CHUNK - 1])
                b.copy("dve", bend, c3(bcum)[:, :, CHUNK - 1])
                mnext = b.sbuf("gmnext", [NLP, NC], F32)
                b.op("dve", lambda en: en.tensor_tensor_scan(out=mnext.h[:], data0=gmax.h[:], data1=bend.h[:], initial=0.0,
                                                             op0=ALU.max, op1=ALU.add), w=[mnext], r=[gmax, bend])
                mst = b.sbuf("gmst", [NLP, NC], F32)
                b.memset("dve", mst[:, 0:1], 0.0)
                b.copy("dve", mst[:, 1:NC], mnext[:, 0:NC - 1])
                M = spare
                b.tt("dve", c3(M), c3(cm), mst.v.us(2).bc(shp), ALU.max)
                Mlast = b.sbuf("gMlast", [NLP, NC], F32)
                b.tt("dve", Mlast, mst, gmax, ALU.max)
                dec = b.sbuf("gdec", [NLP, NC], F32)
                b.tt("dve", dec, mst, Mlast, ALU.subtract)
                b.act(dec, dec, AF.Exp)
                b.dma(View(scr["dec"], scr["dec"].h[:].rearrange("(l c) -> l c", c=NC)), dec)
                t1_ = cm
                b.tt("dve", c3(t1_), mst.v.us(2).bc(shp), c3(M), ALU.subtract)
                b.act(t1_, t1_, AF.Exp)
                t2_ = bcum
                b.tt("dve", t2_, bcum, M, ALU.add)
                b.act(t2_, t2_, AF.Exp, scale=-1.0)
                t3_ = X2
                b.tt("dve", c3(t3_), c3(gq), Mlast.v.us(2).bc(shp), ALU.subtract)
                b.act(t3_, t3_, AF.Exp)
                b.ts("dve", M, M, -1.0, ALU.mult)
                tabs = [gq, t1_, t2_, t3_, M]

                def to_nat(dst, src, eng):
                    b.copy(eng, dst[0:32, :], src[0:32, :])
                    for (s0, n) in ((0, L), (L, T)):
                        b.copy(eng, dst[32:NLP, s0:s0 + n], rev(src, s0, n))

                to_nat(self.mnegM, M, "pool")
                pct = [b.psum(f"gpct{k}", [64, 512], F32) for k in range(2)]
                per = 512 // NLP
                kk_ = 0
                for k in range(4):
                    nat = (natA, natB)[k % 2]
                    to_nat(nat, tabs[k], ("dve", "pool")[k % 2])
                    for c0 in range(0, NC, per):
                        nn = min(per, NC - c0)
                        p = pct[kk_ % 2]
                        kk_ += 1
                        for q in range(nn):
                            b.transpose(p[:, q * NLP:(q + 1) * NLP], nat[:, (c0 + q) * CHUNK:(c0 + q + 1) * CHUNK],
                                        self.ident_f[0:NLP, 0:NLP])
                        b.copy("dve", View(self.mcol[k], self.mcol[k].h[:, c0:c0 + nn, :]),
                               View(p, p.h[:, 0:nn * NLP].rearrange("p (c l) -> p c l", l=NLP)))
                b.dma(self.mdecB, View(scr["dec"], scr["dec"].h[:].partition_broadcast(128)))
            self.mlstm_chunks(bb, qT, kT)

    def mlstm_chunks(self, bb, qT, kT):
        b, c = self.b, self.c
        D, L, T, S, MH, DV = c["D"], c["L"], c["T"], c["S"], c["MH"], c["DV"]
        base = bb * S
        scr = self.mscr
        NL = 2 * MH
        NLP = 32 + MH
        NC, NCc, NCl = S // CHUNK, L // CHUNK, T // CHUNK
        DV1 = DV + 1
        negM, col, decB = self.mnegM, self.mcol, self.mdecB
        with b.scope():
            Cst = b.sbuf("cC", [128, NL, DV1], F32)
            Cbf = b.sbuf("cCb", [128, NL, DV1], BF16)
            b.memset("dve", Cst, 0.0)
            b.memset("pool", Cbf, 0.0)
            sel = b.sbuf("csel", [NLP, NLP, 64], F32)
            b.dma(sel, self.inp["msel"][0:NLP, 0:NLP, :])
            mneg = b.sbuf("cmneg", [64, 2, 64], F32)
            b.dma(mneg, self.inp["mmask"])
            vx = [b.sbuf(f"cvx{k}", [64, MH, DV1], BF16) for k in range(4)]
            for t_ in vx:
                b.memset("dve", t_, 1.0)
            ho = [b.sbuf(f"cho{k}", [64, D], F32) for k in range(4)]
            Dt = [b.sbuf(f"cDt{k}", [64, 64], F32) for k in range(2)]
            SpT = [b.sbuf(f"cSp{k}", [64, 64], BF16) for k in range(2)]
            t1 = [b.sbuf(f"ct1{k}", [64, DV1], F32) for k in range(2)]
            nd = [b.sbuf(f"cnd{k}", [64, DV1], F32) for k in range(2)]
            dd = [b.sbuf(f"cdd{k}", [64, 1], F32) for k in range(2)]
            wk = [b.sbuf(f"cwk{k}", [64, 128], BF16) for k in range(2)]
            pE = [b.psum(f"cpE{k}", [64, 512], F32) for k in range(2)]
            pnum = [b.psum(f"cpnum{k}", [64, 512], F32) for k in range(2)]
            pCus = [b.psum(f"cpqC{k}", [128, 512], F32) for k in range(2)]
            pqC = [View(p_, p_.h[0:64, :]) for p_ in pCus]
            it = 0
            for cs in range(NC):
                for z in range(2):
                    if z == 0:
                        cn = cs
                    else:
                        cn = (NCc - 1 - cs) if cs < NCc else (NCc + NCl - 1 - (cs - NCc))
                    tok0 = cn * CHUNK
                    vt = vx[(cs * 2 + z) % 4]
                    hout = ho[(cs * 2 + z) % 4]
                    rows = slice(base + tok0, base + tok0 + CHUNK)
                    b.dma(vt[:, :, 0:DV], View(scr["v"], scr["v"].h[rows, :].rearrange("t (h e) -> t h e", e=DV)))
                    def head_gen(h, k2):
                        lane = z * MH + h
                        lp = z * 32 + h
                        qs = qT[:, h, tok0:tok0 + CHUNK]
                        ks = kT[:, h, tok0:tok0 + CHUNK]
                        pe_ = pE[k2]
                        pn_, pq_ = pnum[k2], pqC[k2]
                        b.mm(pe_[:, 0:64], [(sel[:, lp, :], negM[:, tok0:tok0 + CHUNK]),
                                            (self.ident_f[0:64, 0:64], mneg[:, z, :])])
                        b.mm(pe_[:, 64:128], [(ks, qs)])
                        yield
                        b.act(Dt[k2], pe_[:, 0:64], AF.Exp, bias=col[0][:, cn, lp:lp + 1])
                        yield
                        b.tt("dve", SpT[k2], pe_[:, 64:128], Dt[k2], ALU.mult)
                        yield
                        b.mm(pn_[:, 0:DV1], [(SpT[k2], vt[:, h, :])])
                        b.mm(pq_[:, 0:DV1], [(qs, Cbf[:, lane, :])])
                        pkk = View(pe_, pe_.h[:, 256:320].bitcast(BF16))
                        pCu = pCus[k2]
                        b.transpose(pkk, ks, self.ident_b)
                        yield
                        b.act(t1[k2], pq_[:, 0:DV1], AF.Copy, scale=col[1][:, cn, lp:lp + 1])
                        b.ts("dve", wk[k2], pkk, col[3][:, cn, lp:lp + 1], ALU.mult)
                        yield
                        b.mm(pCu[:, 0:DV1], [(wk[k2], vt[:, h, :])])
                        b.tt("dve", nd[k2], t1[k2], pn_[:, 0:DV1], ALU.add)
                        yield
                        b.act(dd[k2], nd[k2][:, DV:DV1], AF.Abs)
                        b.stt("dve", Cst[:, lane, :], Cst[:, lane, :], decB[:, lp * NC + cs:lp * NC + cs + 1],
                              pCu[:, 0:DV1], ALU.mult, ALU.add)
                        yield
                        b.copy("act", Cbf[:, lane, :], Cst[:, lane, :])
                        b.tt("dve", dd[k2], dd[k2], col[2][:, cn, lp:lp + 1], ALU.max)
                        b.op("dve", lambda en: en.reciprocal(out=dd[k2].h[:], in_=dd[k2].h[:]), w=[dd[k2]], r=[dd[k2]])
                        b.ts("dve", hout[:, h * DV:(h + 1) * DV], nd[k2][:, 0:DV], dd[k2][:, 0:1], ALU.mult)
                        yield

                    NPAR = self.mpar
                    for h0 in range(0, MH, NPAR):
                        gens = [head_gen(h0 + q_, q_) for q_ in range(min(NPAR, MH - h0))]
                        while gens:
                            for g_ in list(gens):
                                try:
                                    next(g_)
                                except StopIteration:
                                    gens.remove(g_)
                    b.dma(scr[f"h{z}"][rows, :], hout, q="pool")

    def mlstm_readout(self, i, j):
        b, c = self.b, self.c
        D, KT, MH, DV = c["D"], c["KT"], c["MH"], c["DV"]
        last = i == c["DEPTH"] - 1
        scr = self.mscr
        shp = [128, MH, DV]
        with b.scope():
            names = ("h0", "h1", "o")
            ld = [{nm: b.sbuf(f"u{nm}{k}", [128, D], F32) for nm in names} for k in range(2)]
            nw = b.sbuf("unw", [128, D], F32)
            G = b.sbuf("uG", [128, D], F32)
            self.load_bc(nw, self.inp["mlstm_norm_w"], self.inp["mlstm_norm_w"].h[j, :])
            sq = b.sbuf("usq", [128, D], F32)
            st1 = b.sbuf("ust1", [128, MH], F32)
            st2 = b.sbuf("ust2", [128, MH], F32)
            ob = b.sbuf("uob", [128, D], BF16)
            oT = b.sbuf("uoT", [128, KT, 512], BF16)
            wch = [b.sbuf(f"uw{k}", [128, KT, 512], BF16) for k in range(2)]
            ot = [b.sbuf(f"uot{k}", [128, 512], F32) for k in range(2)]
            og = [b.sbuf(f"uog{k}", [128, 512], F32) for k in range(2)]
            tps = [b.psum(f"utp{k}", [128, 4, 128], BF16) for k in range(2)]
            ps = [b.psum(f"ups{k}", [128, 512], F32) for k in range(4)]
            cur_g = None
            kw = kp = kl = 0
            v3 = lambda t_: View(t_, t_.h[:].rearrange("p (h n) -> p h n", n=DV))
            for (r0, ntok, mg, is_ctx, bb) in self.groups(512, include_ctx=not last):
                nt = ntok // 128
                if mg != cur_g:
                    mr = self.modrows
                    self.load_bc(G, mr, mr.h[i, mg, 2 * D:3 * D])
                    cur_g = mg
                for t in range(nt):
                    rr = slice(r0 + t * 128, r0 + (t + 1) * 128)
                    l_ = ld[kl % 2]
                    kl += 1
                    b.dma(l_["h0"], scr["h0"][rr, :])
                    b.dma(l_["h1"], scr["h1"][rr, :], q="act")
                    b.dma(l_["o"], scr["o"][rr, :])
                    h0, h1, o_ = l_["h0"], l_["h1"], l_["o"]
                    b.tt("dve", h0, h0, h1, ALU.add)
                    b.red(st1, v3(h0))
                    b.ts("dve", st1, st1, 1.0 / DV, ALU.mult)
                    b.tt("dve", v3(h0), v3(h0), st1.v.us(2).bc(shp), ALU.subtract)
                    b.tt("pool", sq, h0, h0, ALU.mult)
                    b.red(st2, v3(sq))
                    b.act(st2, st2, AF.Sqrt, bias=self.eps_t[:, 0:1], scale=1.0 / DV)
                    b.op("dve", lambda en: en.reciprocal(out=st2.h[:], in_=st2.h[:]), w=[st2], r=[st2])
                    b.tt("dve", v3(h0), v3(h0), st2.v.us(2).bc(shp), ALU.mult)
                    b.tt("pool", h0, h0, nw, ALU.mult)
                    b.tt("dve", ob, h0, o_, ALU.mult)
                    self.transpose_tile(ob, oT, t * 128, tps, t)
                for n0 in range(0, D, 512):
                    w = wch[kw % 2]
                    kw += 1
                    b.dma(w, self.wb[("m_out", j)][:, :, n0:n0 + 512])
                    for t in range(nt):
                        p = ps[kp % 4]
                        kp += 1
                        b.mm(p, [(oT[:, kt, t * 128:(t + 1) * 128], w[:, kt, :]) for kt in range(KT)])
                        o, gg = ot[t % 2], og[t % 2]
                        rows = self.xs[r0 + t * 128:r0 + (t + 1) * 128, n0:n0 + 512]
                        b.dma(o, rows, q="act")
                        b.tt("dve", gg, p, G[:, n0:n0 + 512], ALU.mult)
                        b.tt("pool", o, o, gg, ALU.add)
                        b.dma(rows, o, q="pool")


def host_inputs(cfg, inputs, ncores):
    c = cfg
    NB = c["NB"]
    x = np.asarray(inputs["x"], dtype=np.float32)
    ctx = np.asarray(inputs["ctx"], dtype=np.float32)
    cc = np.asarray(inputs["c"], dtype=np.float32)
    c_ctx = np.asarray(inputs["c_ctx"], dtype=np.float32)
    shared = {}
    for name in input_shapes(c):
        if name in ("xs0", "ccT", "ident", "msel", "mmask", "rconst", "rconst2"):
            continue
        a = np.ascontiguousarray(np.asarray(inputs[name], dtype=np.float32))
        shp = input_shapes(c)[name]
        if list(a.shape) != shp:
            a = np.zeros(shp, np.float32)
        shared[name] = a
    shared["ident"] = np.eye(128, dtype=np.float32)
    shared["msel"], shared["mmask"], shared["rconst"], shared["rconst2"] = const_tables()
    maps = []
    for k in range(ncores):
        xs0 = np.concatenate([np.concatenate([ctx[k * NB + j], x[k * NB + j]], axis=0) for j in range(NB)], axis=0)
        rows = np.concatenate([cc[k * NB:(k + 1) * NB], c_ctx[None, :]], axis=0)
        ccT = np.ascontiguousarray(rows.T.reshape(c["KT"], 128, NB + 1).transpose(1, 0, 2))
        m = dict(shared)
        m["xs0"] = np.ascontiguousarray(xs0)
        m["ccT"] = ccT
        maps.append(m)
    return maps


def const_tables():
    msel = np.zeros((64, 64, 64), np.float32)
    for l in range(64):
        msel[l, l, :] = 1.0
    jj, tt = np.meshgrid(np.arange(64), np.arange(64), indexing="ij")
    mmask = np.zeros((64, 2, 64), np.float32)
    mmask[:, 0, :] = np.where(jj <= tt, 0.0, -30000.0)
    mmask[:, 1, :] = np.where(jj >= tt, 0.0, -30000.0)
    rconst = np.zeros((64, 7, 64), np.float32)
    rconst[:, 0, :] = (jj < tt)
    rconst[:, 1, :] = (jj > tt)
    rconst[:, 2, :] = (jj <= tt)
    rconst[:, 3, :] = (jj >= tt)
    rconst[:, 4, :] = (jj > tt)
    rconst[:, 5, :] = (jj < tt)
    rconst[:, 6, :] = (jj == tt)
    rconst2 = np.zeros((64, 2, 128), np.float32)
    for z in range(2):
        rconst2[:, z, 0:64] = rconst[:, 0 + z, :]
        rconst2[:, z, 64:128] = rconst[:, 2 + z, :]
    return msel, mmask, rconst, rconst2


_NC_CACHE = {}


def run_net(cfg, inputs, ncores, **netkw):
    key = (tuple(sorted(cfg.items())), tuple(sorted(netkw.items())))
    if key not in _NC_CACHE:
        net = Net(cfg, **netkw)
        _NC_CACHE[key] = net.build()
    nc = _NC_CACHE[key]
    maps = host_inputs(cfg, inputs, ncores)
    res = run_bass_kernel_spmd(nc, maps, core_ids=list(range(ncores)))
    outs = [r["out"].reshape(cfg["NB"], cfg["T"], cfg["D"]) for r in res.results]
    return np.concatenate(outs, axis=0)


def kernel(**inputs):
    cfg = make_cfg()
    return run_net(cfg, inputs, 8).astype(np.float32)
```

```python
import numpy as np
from contextlib import ExitStack, contextmanager
import concourse.bass as bass
import concourse.mybir as mybir
from concourse.bass_utils import run_bass_kernel_spmd

F32 = mybir.dt.float32
BF16 = mybir.dt.bfloat16
AF = mybir.ActivationFunctionType
ALU = mybir.AluOpType
AX = mybir.AxisListType

SEM_LIMIT = 30000
NORM_EPS = 1e-6
GRID_W = 64
CHUNK = 64
GATE_CAP = 15.0


class Counter:
    def __init__(self, b, name):
        self.b = b
        self.name = name
        self.n = 0
        self.sem, self.val = b.take_sem(f"{name}_0")

    def next(self, inc):
        if self.val + inc > SEM_LIMIT:
            self.n += 1
            self.sem, self.val = self.b.take_sem(f"{self.name}_{self.n}")
        self.val += inc
        return (self.sem, self.val)


class Rec:
    def __init__(self):
        self.w = {}
        self.r = {}


def _merge(d, ev):
    s, v = ev
    if d.get(s, -1) < v:
        d[s] = v


class Buf:
    def __init__(self, b, handle, name, space):
        self.b = b
        self.h = handle
        self.name = name
        self.space = space
        self.rec = Rec()
        self.subs = {}
        self.dctr = None

    def recs(self):
        return [self.rec] + [s.rec for s in self.subs.values()]

    def sub(self, key):
        if key not in self.subs:
            self.subs[key] = SubBuf(self, key)
        return self.subs[key]

    def __getitem__(self, idx):
        return View(self, self.h[idx])

    @property
    def v(self):
        return View(self, self.h[:])

    def dma_counter(self):
        if self.dctr is None:
            self.dctr = Counter(self.b, "d_" + self.name)
        return self.dctr


class SubBuf:
    def __init__(self, parent, key):
        self.parent = parent
        self.rec = Rec()
        self.name = f"{parent.name}.{key}"
        self.space = parent.space
        self.h = parent.h
        self.b = parent.b

    def recs(self):
        return [self.rec, self.parent.rec]

    def __getitem__(self, idx):
        return View(self, self.h[idx])

    @property
    def v(self):
        return View(self, self.h[:])

    def dma_counter(self):
        return self.parent.dma_counter()


class View:
    def __init__(self, buf, ap):
        self.buf = buf
        self.ap = ap

    def __getitem__(self, idx):
        return View(self.buf, self.ap[idx])

    def re(self, s, **kw):
        return View(self.buf, self.ap.rearrange(s, **kw))

    def bc(self, shape):
        return View(self.buf, self.ap.broadcast_to(list(shape)))

    def us(self, axis):
        return View(self.buf, self.ap.unsqueeze(axis))


def _v(x):
    return x if isinstance(x, View) else x.v


class B:
    ENG = ("pe", "act", "dve", "pool", "sp")

    def __init__(self):
        self.nc = bass.Bass("TRN2", target_bir_lowering=False)
        nc = self.nc
        self.es = ExitStack()
        self.eng = {"pe": nc.tensor, "act": nc.scalar, "dve": nc.vector, "pool": nc.gpsimd, "sp": nc.sync}
        self.ctr = {}
        self.known = {e: {} for e in self.ENG}
        for e in self.ENG:
            self.ctr[e] = Counter(self, "c_" + e)
        self.all_bufs = []
        self.ninst = 0
        self.stack = self.es
        self.sem_pool = []

    def take_sem(self, name):
        pool = self.__dict__.setdefault("sem_pool", [])
        while pool:
            sem, val = pool.pop()
            if val < SEM_LIMIT // 2:
                return sem, val
        self.nsem = self.__dict__.get("nsem", 0) + 1
        return self.es.enter_context(self.nc.semaphore(f"{name}_{self.nsem}")), 0

    def dram(self, name, shape, dtype, kind="Internal"):
        t = self.nc.dram_tensor(name, list(shape), dtype, kind=kind).ap()
        bf = Buf(self, t, name, "dram")
        self.all_bufs.append(bf)
        return bf

    def sbuf(self, name, shape, dtype):
        self.uid = getattr(self, "uid", 0) + 1
        name = f"{name}_{self.uid}"
        t = self.stack.enter_context(self.nc.sbuf_tensor(name, list(shape), dtype))
        bf = Buf(self, t, name, "sbuf")
        self.all_bufs.append(bf)
        return bf

    def psum(self, name, shape, dtype):
        self.uid = getattr(self, "uid", 0) + 1
        name = f"{name}_{self.uid}"
        t = self.stack.enter_context(self.nc.psum_tensor(name, list(shape), dtype))
        bf = Buf(self, t, name, "psum")
        self.all_bufs.append(bf)
        return bf

    @contextmanager
    def scope(self):
        prev = self.stack
        nb = len(self.all_bufs)
        with ExitStack() as st:
            self.stack = st
            yield
            self.barrier()
            for x in self.all_bufs[nb:]:
                if x.space != "dram" and x.dctr is not None:
                    self.sem_pool.append((x.dctr.sem, x.dctr.val))
            self.all_bufs = self.all_bufs[:nb] + [x for x in self.all_bufs[nb:] if x.space == "dram"]
            self.stack = prev

    def _deps(self, reads, writes):
        d = {}
        for x in reads:
            for rec in x.buf.recs():
                for ev in rec.w.items():
                    _merge(d, ev)
        for x in writes:
            for rec in x.buf.recs():
                for ev in rec.w.items():
                    _merge(d, ev)
                for ev in rec.r.items():
                    _merge(d, ev)
        return d

    def _wait(self, e, deps, skip_own=False):
        eng = self.eng[e]
        kn = self.known[e]
        own = self.ctr[e].sem
        for s, v in deps.items():
            if skip_own and s is own:
                continue
            if kn.get(s, -1) >= v:
                continue
            eng.wait_ge(s, v)
            kn[s] = v
            self.ninst += 1

    def _record(self, ev, reads, writes):
        for x in reads:
            _merge(x.buf.rec.r, ev)
        for x in writes:
            x.buf.rec.w = {ev[0]: ev[1]}
            x.buf.rec.r = {}
            if isinstance(x.buf, Buf):
                for s in x.buf.subs.values():
                    s.rec.w = {ev[0]: ev[1]}
                    s.rec.r = {}

    def op(self, e, fn, w=(), r=()):
        w = [_v(x) for x in w]
        r = [_v(x) for x in r]
        deps = self._deps(r, w)
        self._wait(e, deps, skip_own=(e == "pe"))
        inst = fn(self.eng[e])
        ev = self.ctr[e].next(1)
        inst.then_inc(ev[0], 1)
        self.ninst += 1
        self._record(ev, r, w)
        return ev

    def mm(self, out, pairs, start=True, stop=True):
        out = _v(out)
        pairs = [(_v(a), _v(c)) for a, c in pairs]
        reads = [x for p in pairs for x in p]
        deps = self._deps(reads, [out])
        self._wait("pe", deps, skip_own=True)
        n = len(pairs)
        inst = None
        for i, (a, c) in enumerate(pairs):
            inst = self.nc.tensor.matmul(out.ap, a.ap, c.ap, start=(start and i == 0), stop=(stop and i == n - 1))
            self.ninst += 1
        ev = self.ctr["pe"].next(1)
        inst.then_inc(ev[0], 1)
        self._record(ev, reads, [out])
        return ev

    def transpose(self, out, in_, ident):
        out, in_, ident = _v(out), _v(in_), _v(ident)
        return self.op("pe", lambda e: e.transpose(out.ap, in_.ap, ident.ap), w=[out], r=[in_, ident])

    def dma(self, out, in_, q="sp"):
        out, in_ = _v(out), _v(in_)
        deps = self._deps([in_], [out])
        self._wait(q, deps)
        owner = out.buf if out.buf.space != "dram" else in_.buf
        ctr = owner.dma_counter()
        inst = self.eng[q].dma_start(out=out.ap, in_=in_.ap)
        ev = ctr.next(16)
        inst.then_inc(ev[0], 16)
        self.ninst += 1
        self._record(ev, [in_], [out])
        return ev

    def barrier(self):
        d = {}
        for bf in self.all_bufs:
            for rec in bf.recs():
                for ev in rec.w.items():
                    _merge(d, ev)
                for ev in rec.r.items():
                    _merge(d, ev)
        for e in self.ENG:
            c = self.ctr[e]
            if c.val > 0:
                _merge(d, (c.sem, c.val))
        for e in self.ENG:
            self._wait(e, d)

    @staticmethod
    def _sc(x):
        return x.ap if isinstance(x, View) else x

    def act(self, out, in_, func, bias=None, scale=None, accum=None):
        out, in_ = _v(out), _v(in_)
        kw = {}
        rd = [in_]
        wr = [out]
        if bias is not None:
            kw["bias"] = self._sc(bias)
            if isinstance(bias, View):
                rd.append(bias)
        if scale is not None:
            kw["scale"] = self._sc(scale)
            if isinstance(scale, View):
                rd.append(scale)
        if accum is not None:
            kw["accum_out"] = accum.ap
            wr.append(accum)
        return self.op("act", lambda e: e.activation(out=out.ap, in_=in_.ap, func=func, **kw), w=wr, r=rd)

    def tt(self, e, out, a, c, op):
        out, a, c = _v(out), _v(a), _v(c)
        return self.op(e, lambda en: en.tensor_tensor(out=out.ap, in0=a.ap, in1=c.ap, op=op), w=[out], r=[a, c])

    def ts(self, e, out, a, s1, op0, s2=None, op1=None):
        out, a = _v(out), _v(a)
        rd = [a] + [s for s in (s1, s2) if isinstance(s, View)]
        if op1 is None:
            return self.op(e, lambda en: en.tensor_scalar(out=out.ap, in0=a.ap, scalar1=self._sc(s1), scalar2=None,
                                                          op0=op0), w=[out], r=rd)
        return self.op(e, lambda en: en.tensor_scalar(out=out.ap, in0=a.ap, scalar1=self._sc(s1),
                                                      scalar2=self._sc(s2), op0=op0, op1=op1), w=[out], r=rd)

    def stt(self, e, out, a, s, c, op0, op1):
        out, a, c = _v(out), _v(a), _v(c)
        rd = [a, c] + ([s] if isinstance(s, View) else [])
        return self.op(e, lambda en: en.scalar_tensor_tensor(out=out.ap, in0=a.ap, scalar=self._sc(s), in1=c.ap,
                                                             op0=op0, op1=op1), w=[out], r=rd)

    def red(self, out, in_, op=ALU.add, axis=AX.X):
        out, in_ = _v(out), _v(in_)
        return self.op("dve", lambda en: en.tensor_reduce(out=out.ap, in_=in_.ap, axis=axis, op=op), w=[out], r=[in_])

    def copy(self, e, out, in_):
        out, in_ = _v(out), _v(in_)
        if e == "act":
            return self.op("act", lambda en: en.copy(out=out.ap, in_=in_.ap), w=[out], r=[in_])
        return self.op(e, lambda en: en.tensor_copy(out=out.ap, in_=in_.ap), w=[out], r=[in_])

    def memset(self, e, out, val):
        out = _v(out)
        return self.op(e, lambda en: en.memset(out.ap, val), w=[out])


def make_cfg(D=2048, NB=2, L=256, T=2048, DEPTH=4, MH=8):
    c = dict(D=D, NB=NB, L=L, T=T, DEPTH=DEPTH, MH=MH)
    c["KT"] = D // 128
    c["S"] = L + T
    c["R"] = NB * (L + T)
    c["FF"] = ((8 * D + 767) // 768) * 256
    c["FT"] = c["FF"] // 128
    c["RH"] = D // 64
    c["DV"] = D // MH
    c["DK"] = c["DV"] // 2
    c["QK"] = MH * c["DK"]
    c["PROJ"] = 2 * c["QK"] + 2 * D + 4 * MH
    c["NRW"] = (DEPTH + 1) // 2
    c["NML"] = DEPTH // 2
    return c


WEIGHT_SPECS = None


def input_shapes(c):
    D, DEPTH, NRW, NML = c["D"], c["DEPTH"], c["NRW"], c["NML"]
    return {
        "xs0": [c["R"], D], "ccT": [128, c["KT"], c["NB"] + 1], "ident": [128, 128],
        "msel": [64, 64, 64], "mmask": [64, 2, 64], "rconst": [64, 7, 64], "rconst2": [64, 2, 128],
        "mod_w": [DEPTH, D, 6 * D], "mod_b": [DEPTH, 6 * D], "norm_g": [DEPTH, 2, D], "final_g": [D],
        "rwkv_mu": [NRW, 6, D], "rwkv_w_r": [NRW, D, D], "rwkv_w_k": [NRW, D, D], "rwkv_w_v": [NRW, D, D],
        "rwkv_w_o": [NRW, D, D], "rwkv_w0": [NRW, 2, D], "rwkv_w1": [NRW, 2, D, 96], "rwkv_w2": [NRW, 2, 96, D],
        "rwkv_a0": [NRW, 2, D], "rwkv_a1": [NRW, 2, D, 96], "rwkv_a2": [NRW, 2, 96, D],
        "rwkv_g1": [NRW, D, 256], "rwkv_g2": [NRW, 256, D], "rwkv_k_k": [NRW, D], "rwkv_k_a": [NRW, D],
        "rwkv_r_k": [NRW, c["RH"], 64], "rwkv_ln_w": [NRW, D], "rwkv_ln_b": [NRW, D],
        "rwkv_v0": [max(NRW - 1, 1), D], "rwkv_v1": [max(NRW - 1, 1), D, 64], "rwkv_v2": [max(NRW - 1, 1), 64, D],
        "mlstm_w_in": [NML, D, c["PROJ"]], "mlstm_b_gate": [NML, 2, 2, c["MH"]],
        "mlstm_conv_w": [NML, 3, 3, 2 * c["QK"]], "mlstm_conv_b": [NML, 2 * c["QK"]],
        "mlstm_norm_w": [NML, D], "mlstm_w_out": [NML, D, D],
        "ffn_w_in": [DEPTH, D, 2 * c["FF"]], "ffn_w_out": [DEPTH, c["FF"], D],
    }


class Net:
    def __init__(self, cfg, skip_rwkv=False, skip_mlstm=False, chunked=True):
        self.c = cfg
        self.chunked = chunked
        self.mpar = globals().get("MPAR", 2)
        self.b = B()
        self.skip_rwkv = skip_rwkv
        self.skip_mlstm = skip_mlstm
        self.cvt_rr = 0

    def declare(self):
        b, c = self.b, self.c
        self.inp = {}
        for name, shp in input_shapes(c).items():
            self.inp[name] = b.dram(name, shp, F32, kind="ExternalInput")
        self.out = b.dram("out", [c["NB"] * c["T"], c["D"]], F32, kind="ExternalOutput")
        self.xs = b.dram("xs", [c["R"], c["D"]], F32)
        self.modrows = b.dram("modrows", [c["DEPTH"], c["NB"] + 1, 6 * c["D"]], F32)

    def consts(self):
        b = self.b
        self.ident_f = b.sbuf("ident_f", [128, 128], F32)
        self.ident_b = b.sbuf("ident_b", [128, 128], BF16)
        b.dma(self.ident_f, self.inp["ident"])
        b.copy("dve", self.ident_b, self.ident_f)
        self.eps_t = b.sbuf("eps_t", [128, 4], F32)
        b.memset("dve", self.eps_t[:, 0:1], NORM_EPS)

    def groups(self, gmax=512, include_ctx=True):
        c = self.c
        gs = []
        for bb in range(c["NB"]):
            base = bb * c["S"]
            if include_ctx:
                t = 0
                while t < c["L"]:
                    n = min(gmax, c["L"] - t)
                    gs.append((base + t, n, c["NB"], True, bb))
                    t += n
            t = 0
            while t < c["T"]:
                n = min(gmax, c["T"] - t)
                gs.append((base + c["L"] + t, n, bb, False, bb))
                t += n
        return gs

    def phase_mod(self):
        b, c = self.b, self.c
        D, KT, NG = c["D"], c["KT"], c["NB"] + 1
        with b.scope():
            ccT = b.sbuf("ccT", [128, KT, NG], F32)
            csT = b.sbuf("csT", [128, KT, NG], F32)
            b.dma(ccT, self.inp["ccT"])
            b.act(csT, ccT, AF.Silu)
            wt = [b.sbuf(f"modw{i}", [128, KT, 512], F32) for i in range(2)]
            bias = b.sbuf("modbias", [NG, 6 * D], F32)
            rows = b.sbuf("modrow_sb", [NG, 6 * D], F32)
            ps = [b.psum(f"modps{i}", [128, 512], F32) for i in range(2)]
            k = 0
            for i in range(c["DEPTH"]):
                b.dma(bias, View(self.inp["mod_b"], self.inp["mod_b"].h[i, :].partition_broadcast(NG)))
                for n0 in range(0, 6 * D, 512):
                    w = wt[k % 2]
                    p = ps[k % 2]
                    k += 1
                    src = self.inp["mod_w"].h[i, :, n0:n0 + 512].rearrange("(kt p) n -> p kt n", p=128)
                    b.dma(w, View(self.inp["mod_w"], src))
                    b.mm(p[0:NG, :], [(csT[:, kt, :], w[:, kt, :]) for kt in range(KT)])
                    b.tt("dve", rows[:, n0:n0 + 512], p[0:NG, :], bias[:, n0:n0 + 512], ALU.add)
                b.dma(self.modrows[i], rows, q="pool")

    def cvt_weight(self, name, src_ap, kdim, n, stage, ncmax=4096):
        b = self.b
        pk = min(kdim, 128)
        ktn = kdim // pk
        dst = b.dram(name, [pk, ktn, n], BF16)
        srcv = src_ap.rearrange("(kt p) n -> p kt n", p=pk)
        if n >= ncmax:
            ktc, ncw = 1, ncmax
        else:
            ktc, ncw = max(1, min(ktn, ncmax // n)), n
        for k0 in range(0, ktn, ktc):
            kk = min(ktc, ktn - k0)
            for n0 in range(0, n, ncw):
                nn = min(ncw, n - n0)
                f, h = stage[self.cvt_rr % len(stage)]
                eng = ("dve", "pool", "act")[self.cvt_rr % 3]
                self.cvt_rr += 1
                fv = View(f, f.h[0:pk, 0:kk * nn].rearrange("p (k n) -> p k n", k=kk))
                hv = View(h, h.h[0:pk, 0:kk * nn].rearrange("p (k n) -> p k n", k=kk))
                b.dma(fv, View(self.srcbuf, srcv[:, k0:k0 + kk, n0:n0 + nn]))
                b.copy(eng, hv, fv)
                b.dma(View(dst, dst.h[:, k0:k0 + kk, n0:n0 + nn]), hv, q="pool" if eng != "pool" else "sp")
        return dst

    def phase_prep(self):
        b, c = self.b, self.c
        D, FF = c["D"], c["FF"]
        self.wb = {}
        with b.scope():
            stage = [(b.sbuf(f"cvf{i}", [128, 4096], F32), b.sbuf(f"cvh{i}", [128, 4096], BF16)) for i in range(3)]

            def cv(key, inname, idx, kdim, n):
                self.srcbuf = self.inp[inname]
                ap = self.inp[inname].h
                for j in idx:
                    ap = ap[j]
                self.wb[key] = self.cvt_weight("wb_" + "_".join(str(x) for x in key), ap, kdim, n, stage)

            KT = c["KT"]
            for i in range(c["DEPTH"]):
                dst = b.dram(f"wb_ffn_in_{i}", [2 * c["FT"], 128, KT * 128], BF16)
                self.wb[("ffn_in", i)] = dst
                src = self.inp["ffn_w_in"]
                JB = 4096 // (KT * 128)
                for j0 in range(0, 2 * c["FT"], JB):
                    nj = min(JB, 2 * c["FT"] - j0)
                    f, h = stage[self.cvt_rr % len(stage)]
                    eng = ("dve", "pool", "act")[self.cvt_rr % 3]
                    self.cvt_rr += 1
                    for jj in range(nj):
                        sv = src.h[i, :, (j0 + jj) * 128:(j0 + jj + 1) * 128].rearrange("(kt p) n -> p kt n", p=128)
                        fv = View(f, f.h[:, jj * KT * 128:(jj + 1) * KT * 128].rearrange("p (k n) -> p k n", k=KT))
                        b.dma(fv, View(src, sv), q=("sp", "act")[jj % 2])
                    b.copy(eng, h[:, 0:nj * KT * 128], f[:, 0:nj * KT * 128])
                    b.dma(View(dst, dst.h[j0:j0 + nj].rearrange("j p x -> p j x")),
                          View(h, h.h[:, 0:nj * KT * 128].rearrange("p (j x) -> p j x", j=nj)), q="pool" if eng != "pool" else "sp")
                cv(("ffn_out", i), "ffn_w_out", (i,), FF, D)
            if not self.skip_rwkv:
                for j in range(c["NRW"]):
                    for nm in ("w_r", "w_k", "w_v", "w_o"):
                        cv((nm, j), "rwkv_" + nm, (j,), D, D)
                    for z in range(2):
                        cv(("w1", j, z), "rwkv_w1", (j, z), D, 96)
                        cv(("w2", j, z), "rwkv_w2", (j, z), 96, D)
                        cv(("a1", j, z), "rwkv_a1", (j, z), D, 96)
                        cv(("a2", j, z), "rwkv_a2", (j, z), 96, D)
                    cv(("g1", j), "rwkv_g1", (j,), D, 256)
                    cv(("g2", j), "rwkv_g2", (j,), 256, D)
                    if j > 0:
                        cv(("v1", j), "rwkv_v1", (j - 1,), D, 64)
                        cv(("v2", j), "rwkv_v2", (j - 1,), 64, D)
            if not self.skip_mlstm:
                for j in range(c["NML"]):
                    cv(("m_in", j), "mlstm_w_in", (j,), D, c["PROJ"])
                    cv(("m_out", j), "mlstm_w_out", (j,), D, D)

    def load_bc(self, dst, src_buf, src_ap, q="sp"):
        P = _v(dst).ap.shape[0]
        self.b.dma(dst, View(src_buf, src_ap.partition_broadcast(P)), q=q)

    def mod_tiles(self, i, sub, g, A, Bsh, G, tmp):
        b, c = self.b, self.c
        D = c["D"]
        o = 3 * D * sub
        mr = self.modrows
        self.load_bc(Bsh, mr, mr.h[i, g, o:o + D])
        self.load_bc(tmp, mr, mr.h[i, g, o + D:o + 2 * D])
        self.load_bc(A, self.inp["norm_g"], self.inp["norm_g"].h[i, sub, :])
        b.stt("dve", A, tmp, 1.0, A, ALU.add, ALU.mult)
        if G is not None:
            self.load_bc(G, mr, mr.h[i, g, o + 2 * D:o + 3 * D])

    def norm_tile(self, xt, A, Bsh, hb, junk, hf, st):
        b, D = self.b, self.c["D"]
        b.memset("pool", st[:, 0:1], 0.0)
        b.act(junk, xt, AF.Square, accum=st[:, 0:1])
        b.act(st[:, 1:2], st[:, 0:1], AF.Sqrt, bias=self.eps_t[:, 0:1], scale=1.0 / D)
        b.op("dve", lambda en: en.reciprocal(out=st.h[:, 1:2], in_=st.h[:, 1:2]), w=[st[:, 1:2]], r=[st[:, 1:2]])
        b.stt("dve", hf, xt, st[:, 1:2], A, ALU.mult, ALU.mult)
        if Bsh is None:
            return
        b.tt("dve", hb, hf, Bsh, ALU.add)

    def transpose_tile(self, hb, dstT, tok0, tps, k):
        b, KT = self.b, self.c["KT"]
        for k0 in range(0, KT, 4):
            tp = tps[(k + k0 // 4) % len(tps)]
            for q in range(4):
                b.transpose(tp[:, q, :], hb[:, (k0 + q) * 128:(k0 + q + 1) * 128], self.ident_b)
            eng = "act" if (k0 // 4) % 2 == 0 else "dve"
            b.copy(eng, dstT[:, k0:k0 + 4, tok0:tok0 + 128], tp[:, 0:4, :])

    def ffn(self, i):
        b, c = self.b, self.c
        D, KT, FT, FF = c["D"], c["KT"], c["FT"], c["FF"]
        last = i == c["DEPTH"] - 1
        w_in, w_out = self.wb[("ffn_in", i)], self.wb[("ffn_out", i)]
        JH = FT // 2
        NCW = 512
        with b.scope():
            A = b.sbuf("fA", [128, D], F32)
            Bsh = b.sbuf("fB", [128, D], F32)
            G = b.sbuf("fG", [128, D], F32)
            xt = [b.sbuf(f"fx{k}", [128, D], F32) for k in range(2)]
            hf = b.sbuf("fhf", [128, D], F32)
            tmpm = hf
            hb = [b.sbuf(f"fhb{k}", [128, D], BF16) for k in range(2)]
            st = b.sbuf("fst", [128, 2], F32)
            hT = b.sbuf("fhT", [128, KT, 512], BF16)
            actT = b.sbuf("factT", [128, FT, 512], BF16)
            wg = [b.sbuf(f"fwg{k}", [128, KT, 128], BF16) for k in range(3)]
            wu = [b.sbuf(f"fwu{k}", [128, KT, 128], BF16) for k in range(3)]
            wo = [b.sbuf(f"fwo{k}", [128, JH, NCW], BF16) for k in range(2)]
            sg = [b.sbuf(f"fsg{k}", [128, 512], F32) for k in range(2)]
            ot = [b.sbuf(f"fot{k}", [128, NCW], F32) for k in range(2)]
            tps = [b.psum(f"ftp{k}", [128, 4, 128], BF16) for k in range(2)]
            pg = [b.psum(f"fpg{k}", [128, 512], F32) for k in range(6)]
            cur_g = None
            kx = 0
            for (r0, ntok, mg, is_ctx, bb) in self.groups(512, include_ctx=not last):
                nt = ntok // 128
                if mg != cur_g:
                    self.mod_tiles(i, 1, mg, A, Bsh, G, tmpm)
                    cur_g = mg
                for t in range(nt):
                    x = xt[kx % 2]
                    h = hb[kx % 2]
                    kx += 1
                    b.dma(x, self.xs[r0 + t * 128:r0 + (t + 1) * 128, :])
                    self.norm_tile(x, A, Bsh, h, h, hf, st)
                    self.transpose_tile(h, hT, t * 128, tps, t)
                for j in range(FT):
                    g_w, u_w = wg[j % 3], wu[j % 3]
                    b.dma(g_w, View(w_in, w_in.h[j].rearrange("p (k n) -> p k n", k=KT)))
                    b.dma(u_w, View(w_in, w_in.h[FT + j].rearrange("p (k n) -> p k n", k=KT)))
                    p_g, p_u = pg[(j % 2) * 2], pg[(j % 2) * 2 + 1]
                    b.mm(p_g[:, 0:ntok], [(g_w[:, kt, :], hT[:, kt, 0:ntok]) for kt in range(KT)])
                    b.mm(p_u[:, 0:ntok], [(u_w[:, kt, :], hT[:, kt, 0:ntok]) for kt in range(KT)])
                    s = sg[j % 2]
                    b.act(s[:, 0:ntok], p_g[:, 0:ntok], AF.Silu)
                    b.tt("dve", actT[:, j, 0:ntok], s[:, 0:ntok], p_u[:, 0:ntok], ALU.mult)
                ko = 0
                for n0 in range(0, D, NCW):
                    for half in range(2):
                        w = wo[ko % 2]
                        ko += 1
                        b.dma(w, w_out[:, half * JH:(half + 1) * JH, n0:n0 + NCW])
                        for t in range(nt):
                            b.mm(pg[2 + t], [(actT[:, half * JH + jj, t * 128:(t + 1) * 128], w[:, jj, :])
                                             for jj in range(JH)], start=(half == 0), stop=(half == 1))
                    for t in range(nt):
                        o = ot[t % 2]
                        rows = self.xs[r0 + t * 128:r0 + (t + 1) * 128, n0:n0 + NCW]
                        b.dma(o, rows, q="act")
                        b.tt("dve", sg[t % 2][:, 0:NCW], pg[2 + t], G[:, n0:n0 + NCW], ALU.mult)
                        b.tt("pool", o, o, sg[t % 2][:, 0:NCW], ALU.add)
                        b.dma(rows, o, q="pool")

    def final(self):
        b, c = self.b, self.c
        D = c["D"]
        with b.scope():
            A = b.sbuf("nA", [128, D], F32)
            self.load_bc(A, self.inp["final_g"], self.inp["final_g"].h[:])
            xt = [b.sbuf(f"nx{k}", [128, D], F32) for k in range(2)]
            hf = [b.sbuf(f"nh{k}", [128, D], F32) for k in range(2)]
            junk = b.sbuf("njunk", [128, D], BF16)
            st = b.sbuf("nst", [128, 2], F32)
            k = 0
            for bb in range(c["NB"]):
                for t in range(c["T"] // 128):
                    r0 = bb * c["S"] + c["L"] + t * 128
                    x, h = xt[k % 2], hf[k % 2]
                    k += 1
                    b.dma(x, self.xs[r0:r0 + 128, :])
                    self.norm_tile(x, A, None, None, junk, h, st)
                    b.dma(self.out[bb * c["T"] + t * 128: bb * c["T"] + (t + 1) * 128, :], h, q="pool")

    def build(self):
        b, c = self.b, self.c
        self.declare()
        self.consts()
        for r0 in range(0, c["R"], 512):
            r1 = min(c["R"], r0 + 512)
            b.dma(self.xs[r0:r1, :], self.inp["xs0"][r0:r1, :], q=("sp", "act", "pool")[(r0 // 512) % 3])
        self.phase_mod()
        self.phase_prep()
        for i in range(c["DEPTH"]):
            self.mixer(i)
            self.ffn(i)
        self.final()
        b.barrier()
        return b.nc

    def mixer(self, i):
        if i % 2 == 0:
            if not self.skip_rwkv:
                self.rwkv(i)
        else:
            if not self.skip_mlstm:
                self.mlstm(i)

    def rwkv(self, i):
        c = self.c
        j = i // 2
        if not hasattr(self, "rscr"):
            b = self.b
            R, D = c["R"], c["D"]
            self.rscr = {k: b.dram("rs_" + k, [R, D], F32) for k in
                         ("r", "k", "v", "w0", "w1", "a0", "a1", "g", "y0", "y1", "vf")}
        for bb in range(c["NB"]):
            self.rwkv_proj(i, j, bb)
        if self.chunked:
            self.rwkv_scan_chunked(i, j)
        else:
            self.rwkv_scan(i, j)
        self.rwkv_readout(i, j)

    def rwkv_proj(self, i, j, bb):
        b, c = self.b, self.c
        D, KT, L, T, S = c["D"], c["KT"], c["L"], c["T"], c["S"]
        base = bb * S
        scr = self.rscr
        vdst = scr["vf"] if j == 0 else scr["v"]
        with b.scope():
            hT = b.sbuf("rhT", [128, KT, S], BF16)
            with b.scope():
                A = b.sbuf("rA", [128, D], F32)
                Bsh = b.sbuf("rB", [128, D], F32)
                tmpm = b.sbuf("rtm", [128, D], F32)
                xt = [b.sbuf(f"rx{k}", [128, D], F32) for k in range(2)]
                hf = b.sbuf("rhf", [128, D], F32)
                hb = [b.sbuf(f"rhb{k}", [128, D], BF16) for k in range(2)]
                junk = b.sbuf("rjunk", [128, D], BF16)
                st = b.sbuf("rst", [128, 2], F32)
                tps = [b.psum(f"rtp{k}", [128, 4, 128], BF16) for k in range(2)]
                for (g, t0, n) in ((c["NB"], 0, L), (bb, L, T)):
                    self.mod_tiles(i, 0, g, A, Bsh, None, tmpm)
                    for t in range(n // 128):
                        x, h = xt[t % 2], hb[t % 2]
                        b.dma(x, self.xs[base + t0 + t * 128: base + t0 + (t + 1) * 128, :])
                        self.norm_tile(x, A, Bsh, h, junk, hf, st)
                        self.transpose_tile(h, hT, t0 + t * 128, tps, t)
            with b.scope():
                NM = 6
                mu_rows = b.sbuf("rmur", [NM * KT, 128], F32)
                mu = b.sbuf("rmu", [128, NM * KT], F32)
                omu = b.sbuf("romu", [128, NM * KT], F32)
                b.dma(mu_rows, View(self.inp["rwkv_mu"], self.inp["rwkv_mu"].h[j].rearrange("m (kt p) -> (m kt) p", p=128)))
                pmu = b.psum("rpmu", [128, 128], F32)
                b.transpose(pmu[:, 0:NM * KT], mu_rows, self.ident_f[0:NM * KT, 0:NM * KT])
                b.copy("dve", mu, pmu[:, 0:NM * KT])
                b.ts("dve", omu, mu, -1.0, ALU.mult, 1.0, ALU.add)
                xm = [b.sbuf(f"rxm{k}", [128, KT, 512], BF16) for k in range(2)]
                wch = [b.sbuf(f"rwch{k}", [128, KT, 512], BF16) for k in range(2)]
                brow = b.sbuf("rbrow", [128, D], F32)
                l1w = b.sbuf("rl1w", [128, KT, 256], BF16)
                l2w = b.sbuf("rl2w", [128, 2, D], BF16)
                l1 = b.sbuf("rl1", [128, 2, 512], BF16)
                ev = [b.sbuf(f"rev{k}", [128, 512], F32) for k in range(3)]
                vft = b.sbuf("rvft", [128, 512], F32)
                ps = [b.psum(f"rps{k}", [128, 512], F32) for k in range(4)]
                pl = [b.psum(f"rpl{k}", [128, 512], F32) for k in range(2)]
                state = dict(kx=0, kw=0, kp=0, ke=0)
                blocks = [(0, L, True)] + [(L + t0, min(512, T - t0), False) for t0 in range(0, T, 512)]
                MIX = {"r": 0, "w": 1, "k": 2, "v": 3, "a": 4, "g": 5}
                NEG_EXP_HALF = -float(np.exp(-0.5))

                def build_xm(m, tok0, n, is_ctx):
                    x = xm[state["kx"] % 2]
                    state["kx"] += 1
                    for kt in range(KT):
                        col = m * KT + kt
                        eng = "dve"
                        b.ts(eng, x[:, kt, 0:n], hT[:, kt, tok0:tok0 + n], omu[:, col:col + 1], ALU.mult)
                        muv = mu[:, col:col + 1]
                        if is_ctx:
                            if kt < KT // 2:
                                dst, src = x[:, kt, 1:n], hT[:, kt, tok0:tok0 + n - 1]
                            else:
                                dst, src = x[:, kt, 0:n - 1], hT[:, kt, tok0 + 1:tok0 + n]
                        else:
                            q = kt // (KT // 4)
                            t0 = tok0 - L
                            if q == 0:
                                dst = View(x, x.h[:, kt, 0:n].rearrange("p (r c) -> p r c", c=GRID_W)[:, :, 1:GRID_W])
                                src = View(hT, hT.h[:, kt, tok0:tok0 + n].rearrange("p (r c) -> p r c", c=GRID_W)[:, :, 0:GRID_W - 1])
                            elif q == 1:
                                dst = View(x, x.h[:, kt, 0:n].rearrange("p (r c) -> p r c", c=GRID_W)[:, :, 0:GRID_W - 1])
                                src = View(hT, hT.h[:, kt, tok0:tok0 + n].rearrange("p (r c) -> p r c", c=GRID_W)[:, :, 1:GRID_W])
                            elif q == 2:
                                lo = GRID_W if t0 == 0 else 0
                                dst, src = x[:, kt, lo:n], hT[:, kt, tok0 + lo - GRID_W:tok0 + n - GRID_W]
                            else:
                                hi = n - GRID_W if t0 + n == T else n
                                dst, src = x[:, kt, 0:hi], hT[:, kt, tok0 + GRID_W:tok0 + hi + GRID_W]
                        b.stt("dve", dst, src, muv, dst, ALU.mult, ALU.add)
                    return x

                def load_w(key, n0):
                    w = wch[state["kw"] % 2]
                    state["kw"] += 1
                    b.dma(w, self.wb[key][:, :, n0:n0 + 512])
                    return w

                def nextps():
                    p = ps[state["kp"] % 4]
                    state["kp"] += 1
                    return p

                def nextev():
                    e = ev[state["ke"] % 3]
                    state["ke"] += 1
                    return e

                def rows(tok0, t):
                    return slice(base + tok0 + t * 128, base + tok0 + (t + 1) * 128)

                def lora_hidden(x, n, key, width, func):
                    b.dma(l1w[:, :, 0:width], self.wb[key])
                    for mt in range((width + 127) // 128):
                        wd = min(128, width - mt * 128)
                        p = pl[mt % 2]
                        b.mm(p[0:wd, 0:n], [(l1w[:, kt, mt * 128:mt * 128 + wd], x[:, kt, 0:n]) for kt in range(KT)])
                        if func is None:
                            b.copy("act", l1[0:wd, mt, 0:n], p[0:wd, 0:n])
                        else:
                            b.act(l1[0:wd, mt, 0:n], p[0:wd, 0:n], func)

                def lora_out(t, n0, width):
                    p = nextps()
                    nmt = (width + 127) // 128
                    b.mm(p, [(l1[0:min(128, width - mt * 128), mt, t * 128:(t + 1) * 128],
                              l2w[0:min(128, width - mt * 128), mt, n0:n0 + 512]) for mt in range(nmt)])
                    return p

                def load_l2(key, width):
                    src = self.wb[key]
                    nmt = (width + 127) // 128
                    pk = min(width, 128)
                    b.dma(l2w[0:pk, 0:nmt, :], src)

                for (tok0, n, is_ctx) in blocks:
                    nt = n // 128
                    for nm, key, dst in (("r", ("w_r", j), scr["r"]), ("k", ("w_k", j), scr["k"]), ("v", ("w_v", j), vdst)):
                        x = build_xm(MIX[nm], tok0, n, is_ctx)
                        vres = (nm == "v" and j > 0)
                        if vres:
                            lora_hidden(x, n, ("v1", j), 64, None)
                            load_l2(("v2", j), 64)
                            self.load_bc(brow, self.inp["rwkv_v0"], self.inp["rwkv_v0"].h[j - 1, :])
                        for n0 in range(0, D, 512):
                            w = load_w(key, n0)
                            for t in range(nt):
                                p = nextps()
                                b.mm(p, [(x[:, kt, t * 128:(t + 1) * 128], w[:, kt, :]) for kt in range(KT)])
                                e = nextev()
                                b.copy("act", e, p)
                                if vres:
                                    p2 = lora_out(t, n0, 64)
                                    e2 = nextev()
                                    b.tt("dve", e2, p2, brow[:, n0:n0 + 512], ALU.add)
                                    b.act(e2, e2, AF.Sigmoid)
                                    b.dma(vft, scr["vf"][rows(tok0, t), n0:n0 + 512], q="act")
                                    b.tt("dve", vft, vft, e, ALU.subtract)
                                    b.tt("dve", vft, vft, e2, ALU.mult)
                                    b.tt("dve", e, e, vft, ALU.add)
                                b.dma(dst[rows(tok0, t), n0:n0 + 512], e, q="pool")
                    for nm in ("w", "a"):
                        x = build_xm(MIX[nm], tok0, n, is_ctx)
                        for z in range(2):
                            lora_hidden(x, n, (nm + "1", j, z), 96, AF.Tanh if nm == "w" else None)
                            load_l2((nm + "2", j, z), 96)
                            src0 = self.inp["rwkv_w0" if nm == "w" else "rwkv_a0"]
                            self.load_bc(brow, src0, src0.h[j, z, :])
                            for n0 in range(0, D, 512):
                                for t in range(nt):
                                    p2 = lora_out(t, n0, 96)
                                    e = nextev()
                                    b.tt("dve", e, p2, brow[:, n0:n0 + 512], ALU.add)
                                    b.act(e, e, AF.Sigmoid)
                                    if nm == "w":
                                        if self.chunked:
                                            b.ts("dve", e, e, NEG_EXP_HALF, ALU.mult)
                                        else:
                                            b.act(e, e, AF.Exp, scale=NEG_EXP_HALF)
                                    b.dma(scr[nm + str(z)][rows(tok0, t), n0:n0 + 512], e, q="pool")
                    x = build_xm(MIX["g"], tok0, n, is_ctx)
                    lora_hidden(x, n, ("g1", j), 256, AF.Sigmoid)
                    load_l2(("g2", j), 256)
                    for n0 in range(0, D, 512):
                        for t in range(nt):
                            p2 = lora_out(t, n0, 256)
                            e = nextev()
                            b.copy("act", e, p2)
                            b.dma(scr["g"][rows(tok0, t), n0:n0 + 512], e, q="pool")

    def lane_ap(self, buf, bb, tok_first, step, nt):
        c = self.c
        D = c["D"]
        h = buf.h
        off = h.offset + (bb * c["S"] + tok_first) * D
        return View(buf, bass.AP(h.tensor, off, [[64, c["RH"]], [step * D, nt], [1, 64]]))

    def rwkv_scan(self, i, j):
        b, c = self.b, self.c
        NB, RH, L, T, S = c["NB"], c["RH"], c["L"], c["T"], c["S"]
        NL = 2 * NB * RH
        TC = 32
        scr = self.rscr
        vsrc = scr["vf"] if j == 0 else scr["v"]
        with b.scope():
            St = b.sbuf("sS", [NL, 64, 64], F32)
            tmp = b.sbuf("stmp", [NL, 64, 64], F32)
            vk = [b.sbuf(f"svk{k}", [NL, 64, 64], F32) for k in range(2)]
            sz = b.sbuf("ssz", [NL, 64], F32)
            inb = [{nm: b.sbuf(f"s{nm}{k}", [NL, TC, 64], F32) for nm in ("R", "K", "V", "W", "A")} for k in range(2)]
            Zb = b.sbuf("sZ", [NL, TC, 64], F32)
            T2 = b.sbuf("sT2", [NL, TC, 64], F32)
            yb = [b.sbuf(f"sy{k}", [NL, TC, 64], F32) for k in range(2)]
            ssq = b.sbuf("sssq", [NL, TC], F32)
            kk_t = b.sbuf("skk", [NL, 64], F32)
            ka_t = b.sbuf("ska", [NL, 64], F32)
            oka_t = b.sbuf("soka", [NL, 64], F32)
            for z in range(2):
                for bb in range(NB):
                    p0 = (z * NB + bb) * RH
                    b.dma(kk_t[p0:p0 + RH, :], View(self.inp["rwkv_k_k"], self.inp["rwkv_k_k"].h[j].rearrange("(h n) -> h n", n=64)))
                    b.dma(ka_t[p0:p0 + RH, :], View(self.inp["rwkv_k_a"], self.inp["rwkv_k_a"].h[j].rearrange("(h n) -> h n", n=64)))
            b.ts("dve", oka_t, ka_t, -1.0, ALU.mult, 1.0, ALU.add)
            b.memset("dve", St, 0.0)
            shp3 = [NL, 64, 64]
            shpc = [NL, TC, 64]
            nch = S // TC

            def chunk_src(ci, z):
                s0 = ci * TC
                if z == 0:
                    return s0, 1
                if s0 < L:
                    return L - 1 - s0, -1
                return L + T - 1 - (s0 - L), -1

            def load_chunk(ci):
                bufs = inb[ci % 2]
                for z in range(2):
                    tf, step = chunk_src(ci, z)
                    for bb in range(NB):
                        p0 = (z * NB + bb) * RH
                        for nm, src in (("R", scr["r"]), ("K", scr["k"]), ("V", vsrc), ("W", scr[f"w{z}"]), ("A", scr[f"a{z}"])):
                            b.dma(bufs[nm][p0:p0 + RH, :, :], self.lane_ap(src, bb, tf, step, TC))

            load_chunk(0)
            kv = 0
            for ci in range(nch):
                if ci + 1 < nch:
                    load_chunk(ci + 1)
                bufs = inb[ci % 2]
                Rb, Kb, Vb, Wb, Ab = (bufs[nm] for nm in ("R", "K", "V", "W", "A"))
                y = yb[ci % 2]
                b.tt("dve", Zb, Kb, kk_t.v.us(1).bc(shpc), ALU.mult)
                b.tt("dve", T2, Zb, Zb, ALU.mult)
                b.red(ssq, T2)
                b.act(ssq, ssq, AF.Sqrt)
                b.ts("dve", ssq, ssq, 1e-12, ALU.max)
                b.op("dve", lambda en: en.reciprocal(out=ssq.h[:], in_=ssq.h[:]), w=[ssq], r=[ssq])
                b.ts("dve", ssq, ssq, -1.0, ALU.mult)
                b.tt("dve", Zb, Zb, ssq.v.us(2).bc(shpc), ALU.mult)
                b.tt("dve", T2, Ab, ka_t.v.us(1).bc(shpc), ALU.mult)
                b.tt("dve", T2, T2, oka_t.v.us(1).bc(shpc), ALU.add)
                b.tt("dve", Kb, Kb, T2, ALU.mult)
                b.stt("dve", Ab, Zb, -1.0, Ab, ALU.mult, ALU.mult)
                for t in range(TC):
                    zt = Zb[:, t, :].us(1).bc(shp3)
                    wt = Wb[:, t, :].us(1).bc(shp3)
                    bt = Ab[:, t, :].us(1).bc(shp3)
                    kt_ = Kb[:, t, :].us(1).bc(shp3)
                    rt = Rb[:, t, :].us(1).bc(shp3)
                    vt = Vb[:, t, :].us(2).bc(shp3)
                    vkb = vk[kv % 2]
                    kv += 1
                    b.tt("pool", vkb, vt, kt_, ALU.mult)
                    b.tt("dve", tmp, St, zt, ALU.mult)
                    b.red(sz, tmp)
                    b.tt("dve", St, St, wt, ALU.mult)
                    b.tt("dve", tmp, sz.v.us(2).bc(shp3), bt, ALU.mult)
                    b.tt("dve", St, St, tmp, ALU.add)
                    b.tt("dve", St, St, vkb, ALU.add)
                    b.tt("dve", tmp, St, rt, ALU.mult)
                    b.red(y[:, t, :], tmp)
                for z in range(2):
                    tf, step = chunk_src(ci, z)
                    for bb in range(NB):
                        p0 = (z * NB + bb) * RH
                        b.dma(self.lane_ap(scr[f"y{z}"], bb, tf, step, TC), y[p0:p0 + RH, :, :], q="act")

    def rwkv_scan_chunked(self, i, j):
        b, c = self.b, self.c
        NB, RH, L, T, S, D = c["NB"], c["RH"], c["L"], c["T"], c["S"], c["D"]
        C = CHUNK
        HH = min(8, RH)
        HW = HH * 64
        NL = 2 * HH
        GL = 8
        NGR = NL // GL
        scr = self.rscr
        vsrc = scr["vf"] if j == 0 else scr["v"]
        nch = S // C
        with b.scope():
            cm = b.sbuf("kcm", [64, 7, 64], F32)
            b.dma(cm, self.inp["rconst"])
            cm2 = b.sbuf("kcm2", [64, 2, 128], F32)
            b.dma(cm2, self.inp["rconst2"])
            idb = b.sbuf("kidb", [64, 64], BF16)
            b.copy("dve", idb, cm[:, 6, :])
            onec = b.sbuf("kone", [64, 1], F32)
            b.memset("dve", onec, 1.0)
            kkB = b.sbuf("kkkB", [64, D], F32)
            kaB = b.sbuf("kkaB", [64, D], F32)
            okaB = b.sbuf("kokaB", [64, D], F32)
            self.load_bc(kkB, self.inp["rwkv_k_k"], self.inp["rwkv_k_k"].h[j, :])
            self.load_bc(kaB, self.inp["rwkv_k_a"], self.inp["rwkv_k_a"].h[j, :])
            b.ts("dve", okaB, kaB, -1.0, ALU.mult, 1.0, ALU.add)
            ST = b.sbuf("kST", [64, NL, 64], F32)
            STb = b.sbuf("kSTb", [64, NL, 64], BF16)
            names = ("R", "K", "V", "A", "W")
            ld = [{nm: b.sbuf(f"k{nm}{k}", [64, 2, HW], F32) for nm in names} for k in range(2)]
            T1 = b.sbuf("kT1", [64, 2, HW], F32)
            T2 = b.sbuf("kT2", [64, 2, HW], F32)
            ssq = b.sbuf("kssq", [64, 2 * HH], F32)
            tm = [{nm: b.sbuf(f"k{nm}b{k}", [64, 2, HW], BF16) for nm in ("Zt", "Rt", "Bt", "Kt", "Vb", "Bh", "Kh")}
                  for k in range(2)]
            FMs = [b.sbuf(f"kFM{k}", [64, NL, 4, 64], BF16) for k in range(2)]
            WcCs = [b.sbuf(f"kWcC{k}", [64, NL], F32) for k in range(2)]
            yts = [b.sbuf(f"kyt{k}", [64, 2, HW], F32) for k in range(2)]
            sets = []
            for k in range(2):
                sets.append(dict(
                    X=b.psum(f"kX{k}", [64, GL, 128], F32), Y=b.psum(f"kY{k}", [64, GL, 64], F32),
                    PX=[b.sbuf(f"kPX{k}{q}", [64, GL, 128], F32) for q in range(2)],
                    Q=[b.sbuf(f"kQ{k}{q}", [64, GL, 64], F32) for q in range(2)],
                    SAbr=b.sbuf(f"kSAbr{k}", [64, GL, 64], BF16), SAk=b.sbuf(f"kSAk{k}", [64, GL, 128], BF16),
                    WT=b.sbuf(f"kWT{k}", [64, GL, 64], F32), UT=b.sbuf(f"kUT{k}", [64, GL, 64], BF16)))
            pt = b.psum("kpt", [64, 512], F32)
            ptp = b.psum("kptp", [64, 16, 64], BF16)
            shp = [64, 2, HW]
            lshp = [64, GL, 64]

            NCc, NCl = L // C, T // C

            def chunk_src(cs, z):
                if z == 0:
                    cn = cs
                else:
                    cn = (NCc - 1 - cs) if cs < NCc else (NCc + NCl - 1 - (cs - NCc))
                return cn * C, 1

            def dram_rows(buf, bb, tf, step, c0):
                return buf[bb * S + tf: bb * S + tf + C, c0:c0 + HW]

            for bb in range(NB):
                for hq in range(RH // HH):
                    c0 = hq * HW
                    b.memset("dve", ST, 0.0)
                    b.memset("pool", STb, 0.0)
                    kkb = kkB[:, c0:c0 + HW].us(1).bc(shp)
                    kab = kaB[:, c0:c0 + HW].us(1).bc(shp)
                    okab = okaB[:, c0:c0 + HW].us(1).bc(shp)

                    def load(cs):
                        bufs = ld[cs % 2]
                        for z in range(2):
                            tf, step = chunk_src(cs, z)
                            for nm, src in (("R", scr["r"]), ("K", scr["k"]), ("V", vsrc), ("A", scr[f"a{z}"]), ("W", scr[f"w{z}"])):
                                b.dma(bufs[nm][:, z, :], dram_rows(src, bb, tf, step, c0), q="sp" if nm in ("R", "K", "V") else "act")

                    def tri(kind, src, outs):
                        for z in range(2):
                            for n0 in range(0, HW, 512):
                                n = min(512, HW - n0)
                                b.mm(pt[:, 0:n], [(cm[:, kind * 2 + z, :], src[:, z, n0:n0 + n])])
                                for dst, sc in outs:
                                    b.act(dst[:, z, n0:n0 + n], pt[:, 0:n], AF.Exp, scale=sc)

                    def prep_gen(cs):
                        bufs = ld[cs % 2]
                        Rl, Kl, Vl, Al, Wl = (bufs[nm] for nm in names)
                        o = tm[cs % 2]
                        FM, WcC = FMs[cs % 2], WcCs[cs % 2]
                        h3 = lambda t_: View(t_, t_.h[:].rearrange("p z (h n) -> p (z h) n", n=64))
                        b.tt("pool", T1, Kl, kkb, ALU.mult)
                        b.tt("pool", T2, T1, T1, ALU.mult)
                        b.red(ssq, h3(T2))
                        b.act(ssq, ssq, AF.Sqrt)
                        b.ts("dve", ssq, ssq, 1e-12, ALU.max)
                        b.op("dve", lambda en: en.reciprocal(out=ssq.h[:], in_=ssq.h[:]), w=[ssq], r=[ssq])
                        b.tt("pool", h3(T1), h3(T1), ssq.v.us(2).bc([64, 2 * HH, 64]), ALU.mult)
                        yield
                        tri(0, Wl, [(T2, 1.0)])
                        yield
                        b.stt("dve", o["Zt"], T1, -1.0, T2, ALU.mult, ALU.mult)
                        b.tt("pool", T1, T1, Al, ALU.mult)
                        yield
                        b.tt("pool", Al, Al, kab, ALU.mult)
                        b.tt("pool", Al, Al, okab, ALU.add)
                        b.tt("pool", Kl, Kl, Al, ALU.mult)
                        yield
                        tri(1, Wl, [(Al, 1.0), (T2, -1.0)])
                        yield
                        b.tt("pool", o["Rt"], Rl, Al, ALU.mult)
                        b.tt("dve", o["Bt"], T1, T2, ALU.mult)
                        b.tt("pool", o["Kt"], Kl, T2, ALU.mult)
                        yield
                        tri(2, Wl, [(Al, 1.0)])
                        yield
                        b.tt("pool", o["Bh"], T1, Al, ALU.mult)
                        b.tt("pool", o["Kh"], Kl, Al, ALU.mult)
                        b.copy("pool", o["Vb"], Vl)
                        yield
                        for z in range(2):
                            for hh in range(HH):
                                ln = hh * 2 + z
                                b.mm(pt[:, ln:ln + 1], [(Wl[:, z, hh * 64:(hh + 1) * 64], onec)])
                        b.act(WcC, pt[:, 0:NL], AF.Exp)
                        yield
                        ke = 0
                        for xi, nm in enumerate(("Zt", "Rt", "Bt", "Kt")):
                            for h0 in range(0, HH, 8):
                                nh = min(8, HH - h0)
                                for hh in range(nh):
                                    for z in range(2):
                                        b.transpose(ptp[:, hh * 2 + z, :], o[nm][:, z, (h0 + hh) * 64:(h0 + hh + 1) * 64], idb)
                                b.copy(("act", "dve")[ke % 2], FM[:, h0 * 2:(h0 + nh) * 2, xi, :], ptp[:, 0:nh * 2, :])
                                ke += 1
                                yield

                    def lane_gen(cs, g):
                        o = tm[cs % 2]
                        FM, WcC = FMs[cs % 2], WcCs[cs % 2]
                        yt = yts[cs % 2]
                        y5 = yt.h[:].rearrange("p z (h v) -> p h z v", v=64)
                        tok = lambda nm, l: o[nm][:, l % 2, (l // 2) * 64:(l // 2 + 1) * 64]
                        if True:
                            s_ = sets[g % 2]
                            X, Y, PX, Q, SAbr, SAk, WT, UT = (s_[k_] for k_ in ("X", "Y", "PX", "Q", "SAbr", "SAk", "WT", "UT"))
                            l0 = g * GL
                            fm = lambda l, xi: FM[:, l0 + l, xi, :]
                            zr = lambda l: View(FM, FM.h[:, l0 + l, 0:2, :].rearrange("p a t -> p (a t)"))
                            lz = lambda v_: View(v_.buf, v_.ap.rearrange("p (h z) t -> p h z t", z=2))
                            mk = lambda k_: cm[:, 2 * k_:2 * k_ + 2, :].us(1).bc([64, GL // 2, 2, 64])
                            m2 = cm2.v.us(1).bc([64, GL // 2, 2, 128])
                            for l in range(GL):
                                b.mm(X[:, l, :], [(fm(l, 2), zr(l))])
                            b.tt("dve", lz(PX[0][:, :, 0:64]), lz(X[:, :, 0:64]), mk(0), ALU.mult)
                            b.tt("dve", lz(SAbr.v), lz(X[:, :, 64:128]), mk(1), ALU.mult)
                            b.copy("pool", PX[0][:, :, 64:128], cm[:, 6, :].us(1).bc(lshp))
                            yield
                            for l in range(GL):
                                b.mm(X[:, l, :], [(fm(l, 3), zr(l))])
                            b.tt("dve", lz(SAk.v), lz(X.v), m2, ALU.mult)
                            yield
                            for l in range(GL):
                                b.mm(Y[:, l, :], [(fm(l, 0), fm(l, 2))])
                            b.tt("dve", lz(Q[0].v), lz(Y.v), mk(2), ALU.mult)
                            yield
                            for k in range(6):
                                pc, pn = PX[k % 2], PX[(k + 1) % 2]
                                qc, qn = Q[k % 2], Q[(k + 1) % 2]
                                lastk = k == 5
                                for l in range(GL):
                                    if lastk:
                                        b.mm(X[:, l, 64:128], [(qc[:, l, :], pc[:, l, 64:128])])
                                    else:
                                        b.mm(X[:, l, :], [(qc[:, l, :], pc[:, l, :])])
                                if not lastk:
                                    for l in range(GL):
                                        b.mm(Y[:, l, :], [(pc[:, l, 0:64], qc[:, l, :])])
                                    b.copy("act", pn[:, :, 0:64], X[:, :, 0:64])
                                    b.copy("act", qn, Y)
                                b.tt("dve", pn[:, :, 64:128], pc[:, :, 64:128], X[:, :, 64:128], ALU.add)
                                yield
                            Tm = PX[0]
                            for l in range(GL):
                                b.mm(Y[:, l, :], [(fm(l, 0), STb[:, l0 + l, :]), (SAk[:, l, 0:64], tok("Vb", l0 + l))])
                            b.copy("act", WT, Y)
                            yield
                            for l in range(GL):
                                b.mm(Y[:, l, :], [(Tm[:, l, 64:128], WT[:, l, :])])
                            b.copy("dve", UT, Y)
                            yield
                            for l in range(GL):
                                b.mm(Y[:, l, :], [(fm(l, 1), STb[:, l0 + l, :]), (SAbr[:, l, :], UT[:, l, :]),
                                                  (SAk[:, l, 64:128], tok("Vb", l0 + l))])
                            hh0 = l0 // 2
                            b.copy("act", View(yt, y5[:, hh0:hh0 + GL // 2, :, :]),
                                   View(Y, Y.h[:].rearrange("p (h z) v -> p h z v", z=2)))
                            yield
                            for l in range(GL):
                                b.mm(Y[:, l, :], [(tok("Bh", l0 + l), UT[:, l, :]), (tok("Kh", l0 + l), tok("Vb", l0 + l))])
                            b.tt("pool", ST[:, l0:l0 + GL, :], ST[:, l0:l0 + GL, :], WcC[:, l0:l0 + GL].us(2).bc(lshp), ALU.mult)
                            b.tt("dve", ST[:, l0:l0 + GL, :], ST[:, l0:l0 + GL, :], Y, ALU.add)
                            b.copy("act", STb[:, l0:l0 + GL, :], ST[:, l0:l0 + GL, :])
                            yield

                    def drive(gens):
                        gens = list(gens)
                        while gens:
                            for g_ in list(gens):
                                try:
                                    next(g_)
                                except StopIteration:
                                    gens.remove(g_)

                    load(0)
                    drive([prep_gen(0)])
                    for cs in range(nch):
                        gens = [lane_gen(cs, g) for g in range(NGR)]
                        if cs + 1 < nch:
                            load(cs + 1)
                            gens.append(prep_gen(cs + 1))
                        drive(gens)
                        yt = yts[cs % 2]
                        for z in range(2):
                            tf, step = chunk_src(cs, z)
                            b.dma(dram_rows(scr[f"y{z}"], bb, tf, step, c0), yt[:, z, :], q="pool")

    def rwkv_readout(self, i, j):
        b, c = self.b, self.c
        D, KT, RH = c["D"], c["KT"], c["RH"]
        last = i == c["DEPTH"] - 1
        scr = self.rscr
        vsrc = scr["vf"] if j == 0 else scr["v"]
        shp = [128, RH, 64]
        with b.scope():
            names = ("y0", "y1", "r", "k", "v", "g", "a0", "a1")
            ld = {nm: b.sbuf("q" + nm, [128, D], F32) for nm in names}
            srcs = dict(y0=scr["y0"], y1=scr["y1"], r=scr["r"], k=scr["k"], v=vsrc, g=scr["g"], a0=scr["a0"], a1=scr["a1"])
            lnw = b.sbuf("qlnw", [128, D], F32)
            lnb = b.sbuf("qlnb", [128, D], F32)
            kab = b.sbuf("qka", [128, D], F32)
            ka2 = b.sbuf("qka2", [128, D], F32)
            rkb = b.sbuf("qrk", [128, D], F32)
            G = b.sbuf("qG", [128, D], F32)
            self.load_bc(lnw, self.inp["rwkv_ln_w"], self.inp["rwkv_ln_w"].h[j, :])
            self.load_bc(lnb, self.inp["rwkv_ln_b"], self.inp["rwkv_ln_b"].h[j, :])
            self.load_bc(kab, self.inp["rwkv_k_a"], self.inp["rwkv_k_a"].h[j, :])
            self.load_bc(rkb, self.inp["rwkv_r_k"], self.inp["rwkv_r_k"].h[j].rearrange("h n -> (h n)"))
            b.ts("dve", ka2, kab, -2.0, ALU.mult, 2.0, ALU.add)
            st1 = b.sbuf("qst1", [128, RH], F32)
            st2 = b.sbuf("qst2", [128, RH], F32)
            gneps = b.sbuf("qeps", [128, 1], F32)
            b.memset("dve", gneps, 64 * 1e-5)
            ob = b.sbuf("qob", [128, D], BF16)
            oT = b.sbuf("qoT", [128, KT, 512], BF16)
            wch = [b.sbuf(f"qw{k}", [128, KT, 512], BF16) for k in range(2)]
            ot = [b.sbuf(f"qot{k}", [128, 512], F32) for k in range(2)]
            og = [b.sbuf(f"qog{k}", [128, 512], F32) for k in range(2)]
            tps = [b.psum(f"qtp{k}", [128, 4, 128], BF16) for k in range(2)]
            ps = [b.psum(f"qps{k}", [128, 512], F32) for k in range(4)]
            cur_g = None
            kw = 0
            kp = 0
            v3 = lambda t_: View(t_, t_.h[:].rearrange("p (h n) -> p h n", n=64))
            for (r0, ntok, mg, is_ctx, bb) in self.groups(512, include_ctx=not last):
                nt = ntok // 128
                if mg != cur_g:
                    mr = self.modrows
                    self.load_bc(G, mr, mr.h[i, mg, 2 * D:3 * D])
                    cur_g = mg
                for t in range(nt):
                    rr = slice(r0 + t * 128, r0 + (t + 1) * 128)
                    for nm in names:
                        b.dma(ld[nm], srcs[nm][rr, :], q="sp" if nm in ("y0", "r", "v", "a0") else "act")
                    y0, y1, r_, k_, v_, g_, a0, a1 = (ld[nm] for nm in names)
                    b.tt("dve", y0, y0, y1, ALU.add)
                    b.red(st1, v3(y0))
                    b.ts("dve", st1, st1, 1.0 / 64, ALU.mult)
                    b.tt("dve", v3(y0), v3(y0), st1.v.us(2).bc(shp), ALU.subtract)
                    b.tt("pool", y1, y0, y0, ALU.mult)
                    b.red(st2, v3(y1))
                    b.act(st2, st2, AF.Sqrt, bias=gneps[:, 0:1], scale=1.0 / 64)
                    b.op("dve", lambda en: en.reciprocal(out=st2.h[:], in_=st2.h[:]), w=[st2], r=[st2])
                    b.tt("dve", v3(y0), v3(y0), st2.v.us(2).bc(shp), ALU.mult)
                    b.tt("dve", y0, y0, lnw, ALU.mult)
                    b.tt("dve", y0, y0, lnb, ALU.add)
                    b.tt("pool", a0, a0, a1, ALU.add)
                    b.tt("pool", a0, a0, kab, ALU.mult)
                    b.tt("pool", a0, a0, ka2, ALU.add)
                    b.tt("pool", k_, k_, a0, ALU.mult)
                    b.tt("pool", r_, r_, k_, ALU.mult)
                    b.tt("dve", r_, r_, rkb, ALU.mult)
                    b.red(st1, v3(r_))
                    b.tt("dve", v3(v_), v3(v_), st1.v.us(2).bc(shp), ALU.mult)
                    b.tt("dve", y0, y0, v_, ALU.add)
                    b.tt("dve", ob, y0, g_, ALU.mult)
                    self.transpose_tile(ob, oT, t * 128, tps, t)
                for n0 in range(0, D, 512):
                    w = wch[kw % 2]
                    kw += 1
                    b.dma(w, self.wb[("w_o", j)][:, :, n0:n0 + 512])
                    for t in range(nt):
                        p = ps[kp % 4]
                        kp += 1
                        b.mm(p, [(oT[:, kt, t * 128:(t + 1) * 128], w[:, kt, :]) for kt in range(KT)])
                        o, gg = ot[t % 2], og[t % 2]
                        rows = self.xs[r0 + t * 128:r0 + (t + 1) * 128, n0:n0 + 512]
                        b.dma(o, rows, q="act")
                        b.tt("dve", gg, p, G[:, n0:n0 + 512], ALU.mult)
                        b.tt("pool", o, o, gg, ALU.add)
                        b.dma(rows, o, q="pool")

    def build_hT(self, i, sub, bb, hT):
        b, c = self.b, self.c
        D, L, T, S = c["D"], c["L"], c["T"], c["S"]
        base = bb * S
        with b.scope():
            A = b.sbuf("hA", [128, D], F32)
            Bsh = b.sbuf("hB", [128, D], F32)
            xt = [b.sbuf(f"hx{k}", [128, D], F32) for k in range(2)]
            hf = b.sbuf("hhf", [128, D], F32)
            tmpm = hf
            hb = [b.sbuf(f"hhb{k}", [128, D], BF16) for k in range(2)]
            st = b.sbuf("hst", [128, 2], F32)
            tps = [b.psum(f"htp{k}", [128, 4, 128], BF16) for k in range(2)]
            for (g, t0, n) in ((c["NB"], 0, L), (bb, L, T)):
                self.mod_tiles(i, sub, g, A, Bsh, None, tmpm)
                for t in range(n // 128):
                    x, h = xt[t % 2], hb[t % 2]
                    b.dma(x, self.xs[base + t0 + t * 128: base + t0 + (t + 1) * 128, :])
                    self.norm_tile(x, A, Bsh, h, h, hf, st)
                    self.transpose_tile(h, hT, t0 + t * 128, tps, t)

    def mlstm(self, i):
        c = self.c
        j = i // 2
        if not hasattr(self, "mscr"):
            b = self.b
            R, D = c["R"], c["D"]
            self.mscr = dict(v=b.dram("ms_v", [R, D], BF16), o=b.dram("ms_o", [R, D], F32),
                             h0=b.dram("ms_h0", [R, D], F32), h1=b.dram("ms_h1", [R, D], F32),
                             dec=b.dram("ms_dec", [(32 + c["MH"]) * (c["S"] // CHUNK)], F32))
        for bb in range(c["NB"]):
            self.mlstm_seq(i, j, bb)
        self.mlstm_readout(i, j)

    def mlstm_seq(self, i, j, bb):
        b, c = self.b, self.c
        D, KT, L, T, S, MH, DK, DV, QK = c["D"], c["KT"], c["L"], c["T"], c["S"], c["MH"], c["DK"], c["DV"], c["QK"]
        assert DK == 128
        base = bb * S
        scr = self.mscr
        w_in = self.wb[("m_in", j)]
        NL = 2 * MH
        NLP = 32 + MH
        NC = S // CHUNK
        NCc = L // CHUNK
        NCl = T // CHUNK
        NR = T // GRID_W
        G0 = 2 * QK + 2 * D
        NG = 4 * MH
        with b.scope():
            qT = b.sbuf("mqT", [128, MH, S], BF16)
            kT = b.sbuf("mkT", [128, MH, S], BF16)
            GTall = b.sbuf("mGT", [NG, S], F32)
            with b.scope():
                hT = b.sbuf("mhT", [128, KT, S], BF16)
                self.build_hT(i, 0, bb, hT)
                NCT = 2 * QK // 128
                cw = b.sbuf("mcw", [128, NCT, 10], F32)
                with b.scope():
                    crow = b.sbuf("mcrow", [10, 2 * QK], F32)
                    b.dma(crow[0:9, :], View(self.inp["mlstm_conv_w"], self.inp["mlstm_conv_w"].h[j].rearrange("a b c -> (a b) c")))
                    b.dma(crow[9:10, :], View(self.inp["mlstm_conv_b"], self.inp["mlstm_conv_b"].h[j:j + 1, :]))
                    pcw = b.psum("mpcw", [128, 512], F32)
                    for ct in range(NCT):
                        b.transpose(pcw[:, 0:10], crow[:, ct * 128:(ct + 1) * 128], self.ident_f[0:10, 0:10])
                        b.copy("dve", cw[:, ct, :], pcw[:, 0:10])
                pre = b.sbuf("mpre", [128, S], F32)
                acc = b.sbuf("macc", [128, S], F32)
                wq = [b.sbuf(f"mwq{k}", [128, KT, 128], BF16) for k in range(2)]
                pp = [b.psum(f"mpp{k}", [128, 512], F32) for k in range(2)]
                kp = 0
                for ct in range(NCT):
                    w = wq[ct % 2]
                    b.dma(w, w_in[:, :, ct * 128:(ct + 1) * 128])
                    for t0 in range(0, S, 512):
                        n = min(512, S - t0)
                        p = pp[kp % 2]
                        kp += 1
                        b.mm(p[:, 0:n], [(w[:, kt, :], hT[:, kt, t0:t0 + n]) for kt in range(KT)])
                        b.copy("act", pre[:, t0:t0 + n], p[:, 0:n])
                    wv = lambda dy, dx: cw[:, ct, dy * 3 + dx: dy * 3 + dx + 1]
                    b.ts("dve", acc, pre, wv(1, 1), ALU.mult, cw[:, ct, 9:10], ALU.add)
                    b.stt("dve", acc[:, 1:L], pre[:, 0:L - 1], wv(1, 0), acc[:, 1:L], ALU.mult, ALU.add)
                    b.stt("dve", acc[:, 0:L - 1], pre[:, 1:L], wv(1, 2), acc[:, 0:L - 1], ALU.mult, ALU.add)
                    a3 = lambda t_: t_.h[:, L:S].rearrange("p (r c) -> p r c", c=GRID_W)
                    for dy in range(3):
                        for dx in range(3):
                            if dy == 1 and dx == 1:
                                continue
                            r_lo, r_hi = max(0, 1 - dy), min(NR, NR + 1 - dy)
                            c_lo, c_hi = max(0, 1 - dx), min(GRID_W, GRID_W + 1 - dx)
                            dst = View(acc, a3(acc)[:, r_lo:r_hi, c_lo:c_hi])
                            src = View(pre, a3(pre)[:, r_lo + dy - 1:r_hi + dy - 1, c_lo + dx - 1:c_hi + dx - 1])
                            b.stt("dve", dst, src, wv(dy, dx), dst, ALU.mult, ALU.add)
                    if ct < NCT // 2:
                        b.act(pre, acc, AF.Silu)
                        b.ts("pool", qT[:, ct, :], pre, float(DK) ** -0.5, ALU.mult)
                    else:
                        b.act(kT[:, ct - NCT // 2, :], acc, AF.Silu)
                NCW = 256
                wch = [b.sbuf(f"mwch{k}", [128, KT, NCW], BF16) for k in range(2)]
                evb = [b.sbuf(f"mevb{k}", [128, NCW], BF16) for k in range(2)]
                evf = [b.sbuf(f"mevf{k}", [128, NCW], F32) for k in range(2)]
                pv = [b.psum(f"mpv{k}", [128, 512], F32) for k in range(3)]
                kw = 0
                kq = 0
                for which in ("v", "o"):
                    c0 = 2 * QK + (0 if which == "v" else D)
                    for n0 in range(0, D, NCW):
                        w = wch[kw % 2]
                        kw += 1
                        b.dma(w, w_in[:, :, c0 + n0:c0 + n0 + NCW])
                        for t in range(S // 128):
                            p = pv[kq % 3]
                            b.mm(p[:, 0:NCW], [(hT[:, kt, t * 128:(t + 1) * 128], w[:, kt, :]) for kt in range(KT)])
                            rows = slice(base + t * 128, base + (t + 1) * 128)
                            if which == "v":
                                e = evb[kq % 2]
                                b.copy("act", e, p[:, 0:NCW])
                                b.dma(scr["v"][rows, n0:n0 + NCW], e, q="pool")
                            else:
                                e = evf[kq % 2]
                                b.act(e, p[:, 0:NCW], AF.Sigmoid)
                                b.dma(scr["o"][rows, n0:n0 + NCW], e, q="pool")
                            kq += 1
                wg = b.sbuf("mwg", [128, KT, NG], BF16)
                b.dma(wg, w_in[:, :, G0:G0 + NG])
                bg = b.sbuf("mbg", [128, NG], F32)
                self.load_bc(bg, self.inp["mlstm_b_gate"], self.inp["mlstm_b_gate"].h[j].rearrange("z f h -> (z f h)"))
                gt = [b.sbuf(f"mgt{k}", [128, NG], F32) for k in range(2)]
                pgt = b.psum("mpgt", [128, 512], F32)
                for t in range(S // 128):
                    p = pv[t % 3]
                    g = gt[t % 2]
                    b.mm(p[:, 0:NG], [(hT[:, kt, t * 128:(t + 1) * 128], wg[:, kt, :]) for kt in range(KT)])
                    b.tt("dve", g, p[:, 0:NG], bg, ALU.add)
                    b.act(g, g, AF.Tanh, scale=1.0 / GATE_CAP)
                    b.ts("dve", g, g, GATE_CAP, ALU.mult)
                    b.transpose(pgt[0:NG, 0:128], g, self.ident_f)
                    b.copy("dve", GTall[:, t * 128:(t + 1) * 128], pgt[0:NG, 0:128])
            self.mnegM = b.sbuf("mnegM", [NLP, S], F32)
            self.mcol = [b.sbuf(f"mcol{k}", [64, NC, NLP], F32) for k in range(4)]
            self.mdecB = b.sbuf("mdecB", [128, NLP * NC], F32)
            with b.scope():
                bufs = [b.sbuf(f"gb{k}", [NLP, S], F32) for k in range(7)]
                IGn, LFn, IG, LF, X2, natA, natB = bufs
                b.memset("dve", IGn, 0.0)
                b.memset("pool", LFn, 0.0)
                for z in range(2):
                    b.dma(IGn[z * 32:z * 32 + MH, :], GTall[z * 2 * MH:z * 2 * MH + MH, :])
                    b.dma(LFn[z * 32:z * 32 + MH, :], GTall[z * 2 * MH + MH:(z + 1) * 2 * MH, :])

                def rev(t_, s0, n):
                    h = t_.h[32:NLP, :]
                    return View(t_, bass.AP(h.tensor, h.offset + s0 + n - 1, [list(h.ap[0]), [-1, n]]))

                def to_scan(dst, src, eng):
                    b.copy(eng, dst[0:32, :], src[0:32, :])
                    for (s0, n) in ((0, L), (L, T)):
                        b.copy(eng, dst[32:NLP, s0:s0 + n], rev(src, s0, n))

                to_scan(IG, IGn, "dve")
                to_scan(LF, LFn, "pool")
                b.act(LF, LF, AF.Exp, scale=-1.0)
                b.ts("dve", LF, LF, 1.0, ALU.add)
                b.act(LF, LF, AF.Ln)
                b.ts("dve", LF, LF, -1.0, ALU.mult)
                one = b.sbuf("gone", [NLP, 1], F32)
                b.memset("dve", one, 1.0)
                Gc = IGn
                onesb = one.v.bc([NLP, S])
                b.op("dve", lambda en: en.tensor_tensor_scan(out=Gc.h[:], data0=onesb.ap, data1=LF.h[:], initial=0.0,
                                                             op0=ALU.mult, op1=ALU.add), w=[Gc], r=[one, LF])
                c3 = lambda t_: View(t_, t_.h[:].rearrange("p (c t) -> p c t", t=CHUNK))
                shp = [NLP, NC, CHUNK]
                Gs = b.sbuf("gGs", [NLP, NC], F32)
                b.memset("dve", Gs[:, 0:1], 0.0)
                b.copy("dve", Gs[:, 1:NC], c3(Gc)[:, 0:NC - 1, CHUNK - 1])
                bcum = Gc
                b.tt("dve", c3(bcum), c3(Gc), Gs.v.us(2).bc(shp), ALU.subtract)
                gq = IG
                b.tt("dve", gq, IG, bcum, ALU.subtract)
                cur, nxt = LF, LFn
                b.copy("dve", cur, gq)
                s_ = 1
                while s_ < CHUNK:
                    b.copy("pool", c3(nxt)[:, :, 0:s_], c3(cur)[:, :, 0:s_])
                    b.tt("dve", c3(nxt)[:, :, s_:CHUNK], c3(cur)[:, :, s_:CHUNK], c3(cur)[:, :, 0:CHUNK - s_], ALU.max)
                    cur, nxt = nxt, cur
                    s_ *= 2
                cm, spare = cur, nxt
                gmax = b.sbuf("ggmax", [NLP, NC], F32)
                bend = b.sbuf("gbend", [NLP, NC], F32)
                b.copy("dve", gmax, c3(cm)[:, :, CHUNK - 1])
                b.copy("dve", bend, c3(bcum)[:, :, CHUNK - 1])
                mnext = b.sbuf("gmnext", [NLP, NC], F32)
                b.op("dve", lambda en: en.tensor_tensor_scan(out=mnext.h[:], data0=gmax.h[:], data1=bend.h[:], initial=0.0,
                                                             op0=ALU.max, op1=ALU.add), w=[mnext], r=[gmax, bend])
                mst = b.sbuf("gmst", [NLP, NC], F32)
                b.memset("dve", mst[:, 0:1], 0.0)
                b.copy("dve", mst[:, 1:NC], mnext[:, 0:NC - 1])
                M = spare
                b.tt("dve", c3(M), c3(cm), mst.v.us(2).bc(shp), ALU.max)
                Mlast = b.sbuf("gMlast", [NLP, NC], F32)
                b.tt("dve", Mlast, mst, gmax, ALU.max)
                dec = b.sbuf("gdec", [NLP, NC], F32)
                b.tt("dve", dec, mst, Mlast, ALU.subtract)
                b.act(dec, dec, AF.Exp)
                b.dma(View(scr["dec"], scr["dec"].h[:].rearrange("(l c) -> l c", c=NC)), dec)
                t1_ = cm
                b.tt("dve", c3(t1_), mst.v.us(2).bc(shp), c3(M), ALU.subtract)
                b.act(t1_, t1_, AF.Exp)
                t2_ = bcum
                b.tt("dve", t2_, bcum, M, ALU.add)
                b.act(t2_, t2_, AF.Exp, scale=-1.0)
                t3_ = X2
                b.tt("dve", c3(t3_), c3(gq), Mlast.v.us(2).bc(shp), ALU.subtract)
                b.act(t3_, t3_, AF.Exp)
                b.ts("dve", M, M, -1.0, ALU.mult)
                tabs = [gq, t1_, t2_, t3_, M]

                def to_nat(dst, src, eng):
                    b.copy(eng, dst[0:32, :], src[0:32, :])
                    for (s0, n) in ((0, L), (L, T)):
                        b.copy(eng, dst[32:NLP, s0:s0 + n], rev(src, s0, n))

                to_nat(self.mnegM, M, "pool")
                pct = [b.psum(f"gpct{k}", [64, 512], F32) for k in range(2)]
                per = 512 // NLP
                kk_ = 0
                for k in range(4):
                    nat = (natA, natB)[k % 2]
                    to_nat(nat, tabs[k], ("dve", "pool")[k % 2])
                    for c0 in range(0, NC, per):
                        nn = min(per, NC - c0)
                        p = pct[kk_ % 2]
                        kk_ += 1
                        for q in range(nn):
                            b.transpose(p[:, q * NLP:(q + 1) * NLP], nat[:, (c0 + q) * CHUNK:(c0 + q + 1) * CHUNK],
                                        self.ident_f[0:NLP, 0:NLP])
                        b.copy("dve", View(self.mcol[k], self.mcol[k].h[:, c0:c0 + nn, :]),
                               View(p, p.h[:, 0:nn * NLP].rearrange("p (c l) -> p c l", l=NLP)))
                b.dma(self.mdecB, View(scr["dec"], scr["dec"].h[:].partition_broadcast(128)))
            self.mlstm_chunks(bb, qT, kT)

    def mlstm_chunks(self, bb, qT, kT):
        b, c = self.b, self.c
        D, L, T, S, MH, DV = c["D"], c["L"], c["T"], c["S"], c["MH"], c["DV"]
        base = bb * S
        scr = self.mscr
        NL = 2 * MH
        NLP = 32 + MH
        NC, NCc, NCl = S // CHUNK, L // CHUNK, T // CHUNK
        DV1 = DV + 1
        negM, col, decB = self.mnegM, self.mcol, self.mdecB
        with b.scope():
            Cst = b.sbuf("cC", [128, NL, DV1], F32)
            Cbf = b.sbuf("cCb", [128, NL, DV1], BF16)
            b.memset("dve", Cst, 0.0)
            b.memset("pool", Cbf, 0.0)
            sel = b.sbuf("csel", [NLP, NLP, 64], F32)
            b.dma(sel, self.inp["msel"][0:NLP, 0:NLP, :])
            mneg = b.sbuf("cmneg", [64, 2, 64], F32)
            b.dma(mneg, self.inp["mmask"])
            vx = [b.sbuf(f"cvx{k}", [64, MH, DV1], BF16) for k in range(4)]
            for t_ in vx:
                b.memset("dve", t_, 1.0)
            ho = [b.sbuf(f"cho{k}", [64, D], F32) for k in range(4)]
            Dt = [b.sbuf(f"cDt{k}", [64, 64], F32) for k in range(2)]
            SpT = [b.sbuf(f"cSp{k}", [64, 64], BF16) for k in range(2)]
            t1 = [b.sbuf(f"ct1{k}", [64, DV1], F32) for k in range(2)]
            nd = [b.sbuf(f"cnd{k}", [64, DV1], F32) for k in range(2)]
            dd = [b.sbuf(f"cdd{k}", [64, 1], F32) for k in range(2)]
            wk = [b.sbuf(f"cwk{k}", [64, 128], BF16) for k in range(2)]
            pE = [b.psum(f"cpE{k}", [64, 512], F32) for k in range(2)]
            pnum = [b.psum(f"cpnum{k}", [64, 512], F32) for k in range(2)]
            pCus = [b.psum(f"cpqC{k}", [128, 512], F32) for k in range(2)]
            pqC = [View(p_, p_.h[0:64, :]) for p_ in pCus]
            it = 0
            for cs in range(NC):
                for z in range(2):
                    if z == 0:
                        cn = cs
                    else:
                        cn = (NCc - 1 - cs) if cs < NCc else (NCc + NCl - 1 - (cs - NCc))
                    tok0 = cn * CHUNK
                    vt = vx[(cs * 2 + z) % 4]
                    hout = ho[(cs * 2 + z) % 4]
                    rows = slice(base + tok0, base + tok0 + CHUNK)
                    b.dma(vt[:, :, 0:DV], View(scr["v"], scr["v"].h[rows, :].rearrange("t (h e) -> t h e", e=DV)))
                    def head_gen(h, k2):
                        lane = z * MH + h
                        lp = z * 32 + h
                        qs = qT[:, h, tok0:tok0 + CHUNK]
                        ks = kT[:, h, tok0:tok0 + CHUNK]
                        pe_ = pE[k2]
                        pn_, pq_ = pnum[k2], pqC[k2]
                        b.mm(pe_[:, 0:64], [(sel[:, lp, :], negM[:, tok0:tok0 + CHUNK]),
                                            (self.ident_f[0:64, 0:64], mneg[:, z, :])])
                        b.mm(pe_[:, 64:128], [(ks, qs)])
                        yield
                        b.act(Dt[k2], pe_[:, 0:64], AF.Exp, bias=col[0][:, cn, lp:lp + 1])
                        yield
                        b.tt("dve", SpT[k2], pe_[:, 64:128], Dt[k2], ALU.mult)
                        yield
                        b.mm(pn_[:, 0:DV1], [(SpT[k2], vt[:, h, :])])
                        b.mm(pq_[:, 0:DV1], [(qs, Cbf[:, lane, :])])
                        pkk = View(pe_, pe_.h[:, 256:320].bitcast(BF16))
                        pCu = pCus[k2]
                        b.transpose(pkk, ks, self.ident_b)
                        yield
                        b.act(t1[k2], pq_[:, 0:DV1], AF.Copy, scale=col[1][:, cn, lp:lp + 1])
                        b.ts("dve", wk[k2], pkk, col[3][:, cn, lp:lp + 1], ALU.mult)
                        yield
                        b.mm(pCu[:, 0:DV1], [(wk[k2], vt[:, h, :])])
                        b.tt("dve", nd[k2], t1[k2], pn_[:, 0:DV1], ALU.add)
                        yield
                        b.act(dd[k2], nd[k2][:, DV:DV1], AF.Abs)
                        b.stt("dve", Cst[:, lane, :], Cst[:, lane, :], decB[:, lp * NC + cs:lp * NC + cs + 1],
                              pCu[:, 0:DV1], ALU.mult, ALU.add)
                        yield
                        b.copy("act", Cbf[:, lane, :], Cst[:, lane, :])
                        b.tt("dve", dd[k2], dd[k2], col[2][:, cn, lp:lp + 1], ALU.max)
                        b.op("dve", lambda en: en.reciprocal(out=dd[k2].h[:], in_=dd[k2].h[:]), w=[dd[k2]], r=[dd[k2]])
                        b.ts("dve", hout[:, h * DV:(h + 1) * DV], nd[k2][:, 0:DV], dd[k2][:, 0:1], ALU.mult)
                        yield

                    NPAR = self.mpar
                    for h0 in range(0, MH, NPAR):
                        gens = [head_gen(h0 + q_, q_) for q_ in range(min(NPAR, MH - h0))]
                        while gens:
                            for g_ in list(gens):
                                try:
                                    next(g_)
                                except StopIteration:
                                    gens.remove(g_)
                    b.dma(scr[f"h{z}"][rows, :], hout, q="pool")

    def mlstm_readout(self, i, j):
        b, c = self.b, self.c
        D, KT, MH, DV = c["D"], c["KT"], c["MH"], c["DV"]
        last = i == c["DEPTH"] - 1
        scr = self.mscr
        shp = [128, MH, DV]
        with b.scope():
            names = ("h0", "h1", "o")
            ld = [{nm: b.sbuf(f"u{nm}{k}", [128, D], F32) for nm in names} for k in range(2)]
            nw = b.sbuf("unw", [128, D], F32)
            G = b.sbuf("uG", [128, D], F32)
            self.load_bc(nw, self.inp["mlstm_norm_w"], self.inp["mlstm_norm_w"].h[j, :])
            sq = b.sbuf("usq", [128, D], F32)
            st1 = b.sbuf("ust1", [128, MH], F32)
            st2 = b.sbuf("ust2", [128, MH], F32)
            ob = b.sbuf("uob", [128, D], BF16)
            oT = b.sbuf("uoT", [128, KT, 512], BF16)
            wch = [b.sbuf(f"uw{k}", [128, KT, 512], BF16) for k in range(2)]
            ot = [b.sbuf(f"uot{k}", [128, 512], F32) for k in range(2)]
            og = [b.sbuf(f"uog{k}", [128, 512], F32) for k in range(2)]
            tps = [b.psum(f"utp{k}", [128, 4, 128], BF16) for k in range(2)]
            ps = [b.psum(f"ups{k}", [128, 512], F32) for k in range(4)]
            cur_g = None
            kw = kp = kl = 0
            v3 = lambda t_: View(t_, t_.h[:].rearrange("p (h n) -> p h n", n=DV))
            for (r0, ntok, mg, is_ctx, bb) in self.groups(512, include_ctx=not last):
                nt = ntok // 128
                if mg != cur_g:
                    mr = self.modrows
                    self.load_bc(G, mr, mr.h[i, mg, 2 * D:3 * D])
                    cur_g = mg
                for t in range(nt):
                    rr = slice(r0 + t * 128, r0 + (t + 1) * 128)
                    l_ = ld[kl % 2]
                    kl += 1
                    b.dma(l_["h0"], scr["h0"][rr, :])
                    b.dma(l_["h1"], scr["h1"][rr, :], q="act")
                    b.dma(l_["o"], scr["o"][rr, :])
                    h0, h1, o_ = l_["h0"], l_["h1"], l_["o"]
                    b.tt("dve", h0, h0, h1, ALU.add)
                    b.red(st1, v3(h0))
                    b.ts("dve", st1, st1, 1.0 / DV, ALU.mult)
                    b.tt("dve", v3(h0), v3(h0), st1.v.us(2).bc(shp), ALU.subtract)
                    b.tt("pool", sq, h0, h0, ALU.mult)
                    b.red(st2, v3(sq))
                    b.act(st2, st2, AF.Sqrt, bias=self.eps_t[:, 0:1], scale=1.0 / DV)
                    b.op("dve", lambda en: en.reciprocal(out=st2.h[:], in_=st2.h[:]), w=[st2], r=[st2])
                    b.tt("dve", v3(h0), v3(h0), st2.v.us(2).bc(shp), ALU.mult)
                    b.tt("pool", h0, h0, nw, ALU.mult)
                    b.tt("dve", ob, h0, o_, ALU.mult)
                    self.transpose_tile(ob, oT, t * 128, tps, t)
                for n0 in range(0, D, 512):
                    w = wch[kw % 2]
                    kw += 1
                    b.dma(w, self.wb[("m_out", j)][:, :, n0:n0 + 512])
                    for t in range(nt):
                        p = ps[kp % 4]
                        kp += 1
                        b.mm(p, [(oT[:, kt, t * 128:(t + 1) * 128], w[:, kt, :]) for kt in range(KT)])
                        o, gg = ot[t % 2], og[t % 2]
                        rows = self.xs[r0 + t * 128:r0 + (t + 1) * 128, n0:n0 + 512]
                        b.dma(o, rows, q="act")
                        b.tt("dve", gg, p, G[:, n0:n0 + 512], ALU.mult)
                        b.tt("pool", o, o, gg, ALU.add)
                        b.dma(rows, o, q="pool")


def host_inputs(cfg, inputs, ncores):
    c = cfg
    NB = c["NB"]
    x = np.asarray(inputs["x"], dtype=np.float32)
    ctx = np.asarray(inputs["ctx"], dtype=np.float32)
    cc = np.asarray(inputs["c"], dtype=np.float32)
    c_ctx = np.asarray(inputs["c_ctx"], dtype=np.float32)
    shared = {}
    for name in input_shapes(c):
        if name in ("xs0", "ccT", "ident", "msel", "mmask", "rconst", "rconst2"):
            continue
        a = np.ascontiguousarray(np.asarray(inputs[name], dtype=np.float32))
        shp = input_shapes(c)[name]
        if list(a.shape) != shp:
            a = np.zeros(shp, np.float32)
        shared[name] = a
    shared["ident"] = np.eye(128, dtype=np.float32)
    shared["msel"], shared["mmask"], shared["rconst"], shared["rconst2"] = const_tables()
    maps = []
    for k in range(ncores):
        xs0 = np.concatenate([np.concatenate([ctx[k * NB + j], x[k * NB + j]], axis=0) for j in range(NB)], axis=0)
        rows = np.concatenate([cc[k * NB:(k + 1) * NB], c_ctx[None, :]], axis=0)
        ccT = np.ascontiguousarray(rows.T.reshape(c["KT"], 128, NB + 1).transpose(1, 0, 2))
        m = dict(shared)
        m["xs0"] = np.ascontiguousarray(xs0)
        m["ccT"] = ccT
        maps.append(m)
    return maps


def const_tables():
    msel = np.zeros((64, 64, 64), np.float32)
    for l in range(64):
        msel[l, l, :] = 1.0
    jj, tt = np.meshgrid(np.arange(64), np.arange(64), indexing="ij")
    mmask = np.zeros((64, 2, 64), np.float32)
    mmask[:, 0, :] = np.where(jj <= tt, 0.0, -30000.0)
    mmask[:, 1, :] = np.where(jj >= tt, 0.0, -30000.0)
    rconst = np.zeros((64, 7, 64), np.float32)
    rconst[:, 0, :] = (jj < tt)
    rconst[:, 1, :] = (jj > tt)
    rconst[:, 2, :] = (jj <= tt)
    rconst[:, 3, :] = (jj >= tt)
    rconst[:, 4, :] = (jj > tt)
    rconst[:, 5, :] = (jj < tt)
    rconst[:, 6, :] = (jj == tt)
    rconst2 = np.zeros((64, 2, 128), np.float32)
    for z in range(2):
        rconst2[:, z, 0:64] = rconst[:, 0 + z, :]
        rconst2[:, z, 64:128] = rconst[:, 2 + z, :]
    return msel, mmask, rconst, rconst2


_NC_CACHE = {}


def run_net(cfg, inputs, ncores, **netkw):
    key = (tuple(sorted(cfg.items())), tuple(sorted(netkw.items())))
    if key not in _NC_CACHE:
        net = Net(cfg, **netkw)
        _NC_CACHE[key] = net.build()
    nc = _NC_CACHE[key]
    maps = host_inputs(cfg, inputs, ncores)
    res = run_bass_kernel_spmd(nc, maps, core_ids=list(range(ncores)))
    outs = [r["out"].reshape(cfg["NB"], cfg["T"], cfg["D"]) for r in res.results]
    return np.concatenate(outs, axis=0)


def kernel(**inputs):
    cfg = make_cfg()
    return run_net(cfg, inputs, 8).astype(np.float32)
```
